# Optimizing a Trainium2 kernel written in Bass

```python
import math
import jax, jax.numpy as jnp
from jax import lax
import numpy as np

D_MODEL = 1024
BATCH = 8
SEQ = 8192
DEPTH = 2

CTX_LEN = 256
GRID_W = 64
HEAD_DIM = 64
A_Q_HEADS = 6
A_KV_HEADS = 2
F_GROUPS = 4
F_GROUP_CH = 64
C_Q_HEADS = 6
C_KV_HEADS = 2
WINDOW = 128
Q_BLOCK = 128
ROPE_THETA = 10000.0
AXIS_ROT = HEAD_DIM // 2
D_FF = -(-8 * D_MODEL // (3 * 256)) * 256
NORM_EPS = 1e-6
NEG_INF = -1e30

Q_A_W = A_Q_HEADS * HEAD_DIM
KV_A_W = A_KV_HEADS * HEAD_DIM
F_W = F_GROUPS * F_GROUP_CH
Q_C_W = C_Q_HEADS * HEAD_DIM
KV_C_W = C_KV_HEADS * HEAD_DIM
IN_WIDTH = Q_A_W + 2 * KV_A_W + F_W + Q_C_W + 2 * KV_C_W
MIX_WIDTH = Q_A_W + F_W + Q_C_W

kernel_name = 'hybrid_dit_global_fourier_window_heads'


def rms_norm(x, g):
    xf = x.astype(jnp.float32)
    y = xf * lax.rsqrt(jnp.mean(xf * xf, axis=-1, keepdims=True) + NORM_EPS)
    return (y * g.astype(jnp.float32)).astype(x.dtype)


def adaln(cond, w_ada, b_ada):
    m = jax.nn.silu(cond) @ w_ada + b_ada
    return jnp.split(m, 6, axis=-1)


def axial_rope_tables(n_tokens):
    rows = n_tokens // GRID_W
    row = jnp.repeat(jnp.arange(rows, dtype=jnp.float32), GRID_W)
    col = jnp.tile(jnp.arange(GRID_W, dtype=jnp.float32), rows)
    inv_freq = ROPE_THETA ** (-jnp.arange(0, AXIS_ROT, 2, dtype=jnp.float32) / AXIS_ROT)
    ang = jnp.concatenate([row[:, None] * inv_freq, col[:, None] * inv_freq], axis=-1)
    return jnp.cos(ang), jnp.sin(ang)


def apply_rope(x, cos, sin):
    xf = x.astype(jnp.float32)
    half = HEAD_DIM // 2
    x1, x2 = xf[..., :half], xf[..., half:]
    c = cos[None, :, None, :]
    s = sin[None, :, None, :]
    return jnp.concatenate([x1 * c - x2 * s, x2 * c + x1 * s], axis=-1).astype(x.dtype)


def gqa_logits(q, k):
    b, lq, h, dh = q.shape
    kvh = k.shape[2]
    qg = q.reshape(b, lq, kvh, h // kvh, dh)
    return jnp.einsum('bqhgd,bkhd->bhgqk', qg, k, preferred_element_type=jnp.float32) * (dh ** -0.5)


def gqa_combine(p, v):
    o = jnp.einsum('bhgqk,bkhd->bqhgd', p.astype(v.dtype), v)
    return o.reshape(o.shape[0], o.shape[1], -1)


def sink_logits(sink, like):
    kvh, g = like.shape[1], like.shape[2]
    return jnp.broadcast_to(sink.astype(jnp.float32).reshape(1, kvh, g, 1, 1), like.shape[:-1] + (1,))


def dense_attention(q, k, v, sink=None):
    s = gqa_logits(q, k)
    if sink is None:
        return gqa_combine(jax.nn.softmax(s, axis=-1), v)
    s = jnp.concatenate([s, sink_logits(sink, s)], axis=-1)
    p = jax.nn.softmax(s, axis=-1)[..., :-1]
    return gqa_combine(p, v)


def global_attention(q, k, v, k_ctx, v_ctx):
    b, s, h, dh = q.shape
    nb = s // Q_BLOCK
    k_all = jnp.concatenate([k_ctx, k], axis=1)
    v_all = jnp.concatenate([v_ctx, v], axis=1)
    qb = jnp.moveaxis(q.reshape(b, nb, Q_BLOCK, h, dh), 1, 0)
    ob = lax.map(lambda qi: dense_attention(qi, k_all, v_all), qb)
    return jnp.moveaxis(ob, 0, 1).reshape(b, s, h * dh)


def window_attention(q, k, v, k_ctx, v_ctx, sink):
    b, s, h, dh = q.shape
    nb = s // Q_BLOCK
    band = Q_BLOCK + 2 * WINDOW
    n_ctx = k_ctx.shape[1]
    pad = ((0, 0), (WINDOW, WINDOW), (0, 0), (0, 0))
    kp = jnp.pad(k, pad)
    vp = jnp.pad(v, pad)
    qb = jnp.moveaxis(q.reshape(b, nb, Q_BLOCK, h, dh), 1, 0)
    qi_idx = jnp.arange(Q_BLOCK)[:, None]
    kj = jnp.arange(band)[None, :]
    in_window = jnp.abs(kj - WINDOW - qi_idx) <= WINDOW

    def one_block(args):
        qi, n = args
        start = n * Q_BLOCK
        kb = lax.dynamic_slice_in_dim(kp, start, band, axis=1)
        vb = lax.dynamic_slice_in_dim(vp, start, band, axis=1)
        kpos = start - WINDOW + kj
        valid = in_window & (kpos >= 0) & (kpos < s)
        s_loc = jnp.where(valid, gqa_logits(qi, kb), NEG_INF)
        s_ctx = gqa_logits(qi, k_ctx)
        logits = jnp.concatenate([s_loc, s_ctx, sink_logits(sink, s_loc)], axis=-1)
        p = jax.nn.softmax(logits, axis=-1)
        return gqa_combine(p[..., :band], vb) + gqa_combine(p[..., band:band + n_ctx], v_ctx)

    ob = lax.map(one_block, (qb, jnp.arange(nb)))
    return jnp.moveaxis(ob, 0, 1).reshape(b, s, h * dh)


def fourier_mix(u):
    b, l, _ = u.shape
    ug = u.astype(jnp.float32).reshape(b, l, F_GROUPS, F_GROUP_CH)
    f = jnp.fft.fftn(ug, axes=(1, 3), norm='ortho').real
    return f.reshape(b, l, F_W).astype(u.dtype)


def mixer_inputs(h, w_in, q_norm, k_norm, rope):
    b, l, _ = h.shape
    widths = (Q_A_W, KV_A_W, KV_A_W, F_W, Q_C_W, KV_C_W, KV_C_W)
    points = [int(p) for p in np.cumsum(widths)[:-1]]
    qa, ka, va, fb, qc, kc, vc = jnp.split(h @ w_in, points, axis=-1)
    qa = rms_norm(qa.reshape(b, l, A_Q_HEADS, HEAD_DIM), q_norm)
    ka = rms_norm(ka.reshape(b, l, A_KV_HEADS, HEAD_DIM), k_norm)
    va = va.reshape(b, l, A_KV_HEADS, HEAD_DIM)
    qc = qc.reshape(b, l, C_Q_HEADS, HEAD_DIM)
    kc = kc.reshape(b, l, C_KV_HEADS, HEAD_DIM)
    vc = vc.reshape(b, l, C_KV_HEADS, HEAD_DIM)
    if rope is not None:
        cos, sin = rope
        qa, ka = apply_rope(qa, cos, sin), apply_rope(ka, cos, sin)
        qc, kc = apply_rope(qc, cos, sin), apply_rope(kc, cos, sin)
    return qa, ka, va, fb, qc, kc, vc


def swiglu(h, w_gate, w_up, w_down):
    return (jax.nn.silu(h @ w_gate) * (h @ w_up)) @ w_down


def setup_inputs(seed: int = 0) -> dict:
    key = jax.random.key(seed)
    ks = jax.random.split(key, 18)
    n = jax.random.normal
    f32 = jnp.float32
    return {
        'x': n(ks[0], (BATCH, SEQ, D_MODEL), f32),
        'c': n(ks[1], (BATCH, D_MODEL), f32),
        'ctx': n(ks[2], (BATCH, CTX_LEN, D_MODEL), f32),
        'c_ctx': n(ks[3], (D_MODEL,), f32),
        'w_ada': 0.02 * n(ks[4], (DEPTH, D_MODEL, 6 * D_MODEL), f32),
        'b_ada': 0.01 * n(ks[5], (DEPTH, 6 * D_MODEL), f32),
        'g_mix': 1.0 + 0.05 * n(ks[6], (DEPTH, D_MODEL), f32),
        'g_ffn': 1.0 + 0.05 * n(ks[7], (DEPTH, D_MODEL), f32),
        'w_in': n(ks[8], (DEPTH, D_MODEL, IN_WIDTH), f32) * D_MODEL ** -0.5,
        'q_norm': 1.0 + 0.05 * n(ks[9], (DEPTH, HEAD_DIM), f32),
        'k_norm': 1.0 + 0.05 * n(ks[10], (DEPTH, HEAD_DIM), f32),
        'sink': 0.5 * n(ks[11], (DEPTH, C_Q_HEADS), f32),
        'w_out': n(ks[12], (DEPTH, MIX_WIDTH, D_MODEL), f32) * MIX_WIDTH ** -0.5,
        'w_gate': n(ks[13], (DEPTH, D_MODEL, D_FF), f32) * D_MODEL ** -0.5,
        'w_up': n(ks[14], (DEPTH, D_MODEL, D_FF), f32) * D_MODEL ** -0.5,
        'w_down': n(ks[15], (DEPTH, D_FF, D_MODEL), f32) * D_FF ** -0.5,
        'g_final': 1.0 + 0.05 * n(ks[16], (D_MODEL,), f32),
    }


def reference(x, c, ctx, c_ctx, w_ada, b_ada, g_mix, g_ffn, w_in, q_norm, k_norm, sink, w_out, w_gate, w_up, w_down, g_final):
    rope = axial_rope_tables(x.shape[1])
    for l in range(DEPTH):
        update_ctx = l < DEPTH - 1
        sh1, sc1, gt1, sh2, sc2, gt2 = [m[:, None, :] for m in adaln(c, w_ada[l], b_ada[l])]
        csh1, csc1, cgt1, csh2, csc2, cgt2 = adaln(c_ctx, w_ada[l], b_ada[l])
        h = rms_norm(x, g_mix[l]) * (1 + sc1) + sh1
        hc = rms_norm(ctx, g_mix[l]) * (1 + csc1) + csh1
        qa, ka, va, fb, qc, kc, vc = mixer_inputs(h, w_in[l], q_norm[l], k_norm[l], rope)
        qac, kac, vac, fbc, qcc, kcc, vcc = mixer_inputs(hc, w_in[l], q_norm[l], k_norm[l], None)
        o = jnp.concatenate([
            global_attention(qa, ka, va, kac, vac),
            fourier_mix(fb),
            window_attention(qc, kc, vc, kcc, vcc, sink[l]),
        ], axis=-1)
        x = x + gt1 * (o @ w_out[l])
        x = x + gt2 * swiglu(rms_norm(x, g_ffn[l]) * (1 + sc2) + sh2, w_gate[l], w_up[l], w_down[l])
        if update_ctx:
            oc = jnp.concatenate([
                dense_attention(qac, kac, vac),
                fourier_mix(fbc),
                dense_attention(qcc, kcc, vcc, sink[l]),
            ], axis=-1)
            ctx = ctx + cgt1 * (oc @ w_out[l])
            ctx = ctx + cgt2 * swiglu(rms_norm(ctx, g_ffn[l]) * (1 + csc2) + csh2, w_gate[l], w_up[l], w_down[l])
    return rms_norm(x, g_final)
```

```python
import contextlib
import math
import numpy as np
import concourse.bass as bass
import concourse.mybir as mybir
from concourse.bass_utils import run_bass_kernel_spmd

F32 = mybir.dt.float32
BF16 = mybir.dt.bfloat16
AF = mybir.ActivationFunctionType
ALU = mybir.AluOpType

D = 1024
DFF = 2816
NFC = DFF // 128
CTX = 256
DEPTH = 2
EPS = 1e-6
VW = 512
ENGS = ['tensor', 'vector', 'scalar', 'gpsimd', 'sync']


class Buf:
    __slots__ = ('name', 'last_w', 'readers')

    def __init__(self, name):
        self.name = name
        self.last_w = None
        self.readers = {}


class Tile:
    def __init__(self, t, name):
        self.t = t
        self.b = Buf(name)

    def __getitem__(self, k):
        return self.t[k]


class SemState:
    def __init__(self, nc, st, n_dma_sems=26):
        self.cnt = {e: 0 for e in ENGS}
        self.known = {e: {} for e in ENGS}
        self.n_dma = n_dma_sems
        self.dma_val = [0] * n_dma_sems
        self.n_hw = n_dma_sems - 10
        self.rr_hw = 0
        self.rr_sw = 0
        self.sems = {}
        for e in ENGS:
            self.sems[e] = st.enter_context(nc.semaphore("sem_" + e))
        for k in range(n_dma_sems):
            self.sems[('dma', k)] = st.enter_context(nc.semaphore("sem_d%d" % k))


class Sched:
    def __init__(self, nc, state):
        self.nc = nc
        self.ops = {e: [] for e in ENGS}
        self.state = state

    @property
    def cnt(self):
        return self.state.cnt

    @property
    def known(self):
        return self.state.known

    @property
    def dma_val(self):
        return self.state.dma_val

    @property
    def n_dma(self):
        return self.state.n_dma

    def emit(self, engine, fn, reads=(), writes=(), dma=False, signal=True):
        need = {}

        def add(ev):
            if ev is None:
                return
            k, v = ev
            if need.get(k, 0) < v:
                need[k] = v

        for b in reads:
            add(b.last_w)
        for b in writes:
            add(b.last_w)
            for k, v in b.readers.items():
                add((k, v))
        if engine == 'tensor':
            need.pop('tensor', None)
        kd = None
        if dma:
            stt_ = self.state
            if engine == 'gpsimd':
                kd = stt_.n_hw + stt_.rr_sw
                stt_.rr_sw = (stt_.rr_sw + 1) % (stt_.n_dma - stt_.n_hw)
            else:
                kd = stt_.rr_hw
                stt_.rr_hw = (stt_.rr_hw + 1) % stt_.n_hw
            if self.dma_val[kd] > 0:
                add((('dma', kd), self.dma_val[kd]))
        kn = self.known[engine]
        waits = []
        for k, v in need.items():
            if kn.get(k, 0) >= v:
                continue
            kn[k] = v
            waits.append((k, v))
        if dma:
            self.dma_val[kd] += 16
            ev = (('dma', kd), self.dma_val[kd])
            inc = ev
        elif signal:
            self.cnt[engine] += 1
            ev = (engine, self.cnt[engine])
            inc = ev
        else:
            ev = (engine, self.cnt[engine] + 1)
            inc = None
        self.ops[engine].append((waits, fn, inc))
        for b in writes:
            b.last_w = ev
            b.readers = {}
        for b in reads:
            if b.last_w is not ev:
                if b.readers.get(ev[0], 0) < ev[1]:
                    b.readers[ev[0]] = ev[1]
        return ev

    def wait_all(self, engine):
        need = {}
        for e in ENGS:
            if e != engine and self.cnt[e] > 0:
                need[e] = self.cnt[e]
        for k in range(self.n_dma):
            if self.dma_val[k] > 0:
                need[('dma', k)] = self.dma_val[k]
        kn = self.known[engine]
        waits = [(k, v) for k, v in need.items() if kn.get(k, 0) < v]
        for k, v in waits:
            kn[k] = v
        self.ops[engine].append((waits, None, None))

    def build(self):
        nc = self.nc
        sems = self.state.sems
        with nc.Block() as block:
            def run(engname):
                def body(eng):
                    for waits, fn, inc in self.ops[engname]:
                        if fn is None:
                            for k, v in waits:
                                eng.wait_ge(sems[k], v)
                            continue
                        for k, v in waits[1:]:
                            eng.wait_ge(sems[k], v)
                        ins = fn(eng)
                        if waits:
                            k, v = waits[0]
                            ins._wait_ge(sems[k], v)
                        if inc is not None:
                            k, v = inc
                            ins.then_inc(sems[k], 16 if isinstance(k, tuple) else 1)
                return body

            block.tensor(run('tensor'))
            block.vector(run('vector'))
            block.scalar(run('scalar'))
            block.gpsimd(run('gpsimd'))
            block.sync(run('sync'))


class Phase:
    _uid = [0]
    state = None

    def __init__(self, nc, name):
        self.nc = nc
        self.name = name
        self.st = contextlib.ExitStack()
        self.S = Sched(nc, Phase.state)
        self.banks = []

    def _nm(self, name):
        Phase._uid[0] += 1
        return "%s_%s_%d" % (self.name, name, Phase._uid[0])

    def sb(self, name, shape, dt, n=1):
        out = []
        for i in range(n):
            nm = self._nm(name)
            t = self.st.enter_context(self.nc.sbuf_tensor(nm, list(shape), dt))
            out.append(Tile(t, nm))
        return out if n > 1 else out[0]

    def psum_banks(self, n=8):
        for i in range(n):
            nm = self._nm("bank")
            t = self.st.enter_context(self.nc.psum_tensor(nm, [128, 512], F32))
            self.banks.append(Tile(t, nm))
        return self.banks

    def wrap(self, t, name):
        return Tile(t, self._nm(name))

    def dma(self, out, in_, reads=(), writes=(), eng='sync'):
        self.S.emit(eng, lambda e: e.dma_start(out=out, in_=in_), reads=[r.b for r in reads],
                    writes=[w.b for w in writes], dma=True)

    def mm(self, out, lhsT, rhs, start, stop, reads, bank, signal=True, skip=False):
        if skip:
            fn = lambda e: e.matmul(out, lhsT=lhsT, rhs=rhs, start=start, stop=stop, skip_group_check=True)
        else:
            fn = lambda e: e.matmul(out, lhsT=lhsT, rhs=rhs, start=start, stop=stop)
        self.S.emit('tensor', fn, reads=[r.b for r in reads], writes=[bank.b], signal=signal)

    def op(self, eng, fn, reads=(), writes=()):
        self.S.emit(eng, fn, reads=[r.b for r in reads], writes=[w.b for w in writes])

    def act(self, out, in_, func, reads, writes, scale=None, bias=None, eng='scalar'):
        kw = {}
        if scale is not None:
            kw['scale'] = scale
        if bias is not None:
            kw['bias'] = bias
        self.op(eng, lambda e: e.activation(out=out, in_=in_, func=func, **kw), reads, writes)

    def tt(self, out, in0, in1, op, reads, writes, eng='vector'):
        self.op(eng, lambda e: e.tensor_tensor(out=out, in0=in0, in1=in1, op=op), reads, writes)

    def stt(self, out, in0, scalar, in1, op0, op1, reads, writes):
        self.op('vector', lambda e: e.scalar_tensor_tensor(out=out, in0=in0, scalar=scalar, in1=in1, op0=op0, op1=op1),
                reads, writes)

    def ts(self, out, in0, s1, s2, op0, op1, reads, writes, eng='gpsimd'):
        if op1 is None:
            self.op(eng, lambda e: e.tensor_scalar(out=out, in0=in0, scalar1=s1, scalar2=None, op0=op0), reads, writes)
        else:
            self.op(eng, lambda e: e.tensor_scalar(out=out, in0=in0, scalar1=s1, scalar2=s2, op0=op0, op1=op1), reads, writes)

    def copy(self, out, in_, reads, writes, eng='vector'):
        if eng == 'scalar':
            self.op(eng, lambda e: e.activation(out=out, in_=in_, func=AF.Copy), reads, writes)
        else:
            self.op(eng, lambda e: e.tensor_copy(out=out, in_=in_), reads, writes)

    def memset(self, ap, val, writes, eng='vector'):
        self.op(eng, lambda e: e.memset(ap, val), (), writes)

    def finish(self):
        for e in ENGS:
            self.S.wait_all(e)
        self.S.build()
        self.st.close()


def v3(ap, inner):
    return ap.rearrange("p (a b) -> p a b", b=inner)


def MOD(l, who, i):
    return ((l * 2 + who) * 6 + i) * 8
GS_BASE = DEPTH * 2 * 6 * 8
def GS(l, who, k):
    return GS_BASE + ((l * 2 + who) * 2 + k) * 8
PPW = GS_BASE + DEPTH * 2 * 2 * 8
def SM_BADA(l): return l * 48
def SM_GMIX(l): return DEPTH * 48 + l * 8
def SM_GFFN(l): return DEPTH * 48 + DEPTH * 8 + l * 8
SM_GFIN = DEPTH * 48 + 2 * DEPTH * 8
def SM_QKN(l): return SM_GFIN + 8 + l * 4
SMW = SM_GFIN + 8 + DEPTH * 4


class K:
    pass


def rstd_from_sums(ph, bank_ap, n_div, tmp, out, reads_bank, tmp_t, out_t):
    ph.act(tmp, bank_ap, AF.Ln, reads=[], writes=[reads_bank, tmp_t], scale=1.0 / n_div, bias=EPS)
    ph.act(out, tmp, AF.Exp, reads=[tmp_t], writes=[out_t], scale=-0.5)


def precast(ph, k, l, ffn=True, win=True, dep=()):
    if win and l < DEPTH:
        for r in range(0, D, 256):
            ph.dma(k.winB[l, r:r + 256, :], k.w_in_p[l, r:r + 256, :], reads=dep, eng='gpsimd')
    if ffn and l < DEPTH:
        for r in range(0, D, 256):
            ph.dma(k.woB[l, r:r + 256, :], k.w_out_p[l, r:r + 256, :], reads=dep, eng='gpsimd')
            ph.dma(k.wgB[l, r:r + 256, :], k.w_gate[l, r:r + 256, :], reads=dep, eng='gpsimd')
            ph.dma(k.wuB[l, r:r + 256, :], k.w_up[l, r:r + 256, :], reads=dep, eng='gpsimd')
        for r in range(0, DFF, 256):
            ph.dma(k.wdB[l, r:r + 256, :], k.w_down[l, r:r + 256, :], reads=dep, eng='gpsimd')


def phase_A(k, l):
    nc = k.nc
    ph = Phase(nc, "A%d" % l)
    banks = ph.psum_banks(4)
    pp = ph.wrap(k.pp, "pp")
    sm = ph.wrap(k.sm, "sm")
    cv = ph.sb("cv", [128, 16], F32)
    sc = ph.sb("sc", [128, 16], F32)
    wst = ph.sb("wst", [128, 6 * D], F32, n=2)
    wfT = ph.sb("wfT", [128, 2 * D], F32)
    bd = ph.sb("bd", [128, 256], F32)
    wfo = ph.sb("wfo", [128, 512], BF16, n=2)
    if l == 0:
        ph.dma(sm[:, :], k.small[:, :], writes=[sm])
        precast(ph, k, 0, ffn=False, win=True)
    ph.dma(cv[:, :], k.cvec[:, :], writes=[cv])
    ph.act(sc[:, :], cv[:, :], AF.Silu, reads=[cv], writes=[sc])
    b0 = banks[0]
    ph.memset(b0[:, 0:96], 0.0, writes=[b0])
    import os
    dbg = os.environ.get("KDBG", "")
    for kc in range(0 if 'nomm' in dbg else 8):
        w = wst[kc % 2]
        ph.dma(w[:, :], k.w_ada[l, kc * 128:(kc + 1) * 128, :], writes=[w])
        for cc in range(48):
            ph.mm(b0[:, cc * 2:cc * 2 + 2], w[:, cc * 128:(cc + 1) * 128], sc[:, kc * 2:kc * 2 + 2],
                  start=False, stop=(kc == 7), reads=[w, sc], bank=b0, signal=(cc == 47), skip=True)
    for who in range(2):
        src = v3(b0[:, 0:96], 2)[:, :, who]
        c0 = MOD(l, who, 0)
        ph.tt(pp[:, c0:c0 + 48], src, sm[:, SM_BADA(l):SM_BADA(l) + 48], ALU.add, reads=[sm], writes=[b0, pp])
        for kk, (i_sc, gcol) in enumerate(((1, SM_GMIX(l)), (4, SM_GFFN(l)))):
            cs = MOD(l, who, i_sc)
            cg = GS(l, who, kk)
            ph.stt(pp[:, cg:cg + 8], pp[:, cs:cs + 8], 1.0, sm[:, gcol:gcol + 8], ALU.add, ALU.mult,
                   reads=[sm], writes=[pp])
    ph.dma(v3(wfT[:, :], D), k.w_in_fT[l].rearrange("(c p) d -> p c d", p=128), writes=[wfT])
    ph.dma(bd[:, :], k.cmat[:, 0:256], writes=[bd])
    for dc in range(0 if 'nowf' in dbg else 8):
        bk = banks[1 + dc % 2]
        for t in range(2):
            for fc in range(2):
                q = t * 2 + fc
                ph.mm(bk[:, q * 128:(q + 1) * 128], wfT[:, fc * D + dc * 128: fc * D + (dc + 1) * 128],
                      bd[:, t * 128:(t + 1) * 128], start=True, stop=True, reads=[wfT, bd], bank=bk,
                      signal=(q == 3))
        o = wfo[dc % 2]
        ph.copy(o[:, :], bk[:, :], reads=[], writes=[bk, o])
        ph.dma(k.wfD[l, dc * 128:(dc + 1) * 128, :], o[:, :], reads=[o], eng='gpsimd')
    ph.finish()


CH_QA, CH_QAS, CH_KA, CH_KAS, CH_QC, CH_QCS, CH_KC, CH_KCS = 0, 3, 6, 7, 8, 11, 14, 15
TM0 = 2048
WCOLS = 2048 + 768


def phase_B(k, l):
    nc = k.nc
    S = k.S
    NT = S // 512
    ph = Phase(nc, "B%d" % l)
    banks = ph.psum_banks(8)
    pp = ph.wrap(k.pp, "pp")
    sm = ph.wrap(k.sm, "sm")
    wT = ph.sb("wT", [128, 8 * WCOLS], BF16)
    wv = v3(wT[:, :], WCOLS)
    xts = ph.sb("xt", [128, 8 * 512], F32, n=2)
    sq = ph.sb("sq", [128, 8 * 512], BF16)
    hTs = ph.sb("hT", [128, 8 * 512], BF16, n=2)
    ones = ph.sb("ones", [128, 128], BF16)
    bones = ph.sb("bones", [128, 128], BF16)
    lnt = ph.sb("lnt", [128, 512], F32)
    rstd = ph.sb("rstd", [128, 512], F32)
    rC = ph.sb("rC", [128, 512], F32, n=3)
    rS = ph.sb("rS", [128, 512], F32, n=3)
    sqh = ph.sb("sqh", [128, 512], BF16, n=2)
    lnh = ph.sb("lnh", [128, 512], F32)
    rinv = ph.sb("rinv", [128, 512], F32, n=2)
    t1s = ph.sb("t1", [128, 512], F32, n=2)
    t2s = ph.sb("t2", [128, 512], F32, n=2)
    qo = ph.sb("qo", [128, 512], BF16, n=4)
    vts = ph.sb("vt", [128, VW], BF16, n=4)
    zts = ph.sb("zt", [128, 512], BF16, n=4)

    ph.memset(ones[:, :], 1.0, writes=[ones])
    ph.memset(bones[:, :], 0.0, writes=[bones])
    ph.memset(bones[0:64, 0:64], 1.0, writes=[bones])
    ph.memset(bones[64:128, 64:128], 1.0, writes=[bones])
    for vt in vts:
        ph.memset(vt[:, :], 1.0, writes=[vt], eng='gpsimd')
    for c0 in range(0, 2304, 768):
        ph.dma(wv[:, :, c0:c0 + 768], k.winB[l, :, c0:c0 + 768].rearrange("(c p) f -> p c f", p=128), writes=[wT])
    ph.dma(wv[:, :, 2304:2816], k.wfD[l].rearrange("(c p) f -> p c f", p=128), writes=[wT])

    qk = SM_QKN(l)
    gq, gqs, gk, gks = (sm[:, qk + i:qk + i + 1] for i in range(4))
    pair_i = [0]
    qo_i = [0]
    vt_i = [0]

    tiles = [('ctx', 0)] + [('x', t) for t in range(NT)]

    def info(ti):
        kind, t = tiles[ti]
        isx = kind == 'x'
        N = 512 if isx else CTX
        off = CTX + t * 512 if isx else 0
        who = 0 if isx else 1
        need_q = isx or l == 0
        xt, hT = xts[ti % 2], hTs[ti % 2]
        return kind, t, isx, N, off, who, need_q, xtk[ti % 2], hTk[ti % 2], v3(xt[:, :], 512), v3(hT[:, :], 512)

    sv = v3(sq[:, :], 512)
    xtk = [[Tile(x_.t, "xk%d_%d" % (i, c)) for c in range(8)] for i, x_ in enumerate(xts)]
    hTk = [[Tile(h_.t, "hk%d_%d" % (i, c)) for c in range(8)] for i, h_ in enumerate(hTs)]

    def pre(ti):
        kind, t, isx, N, off, who, need_q, xt, hT, xv, hv = info(ti)
        if isx:
            src = (k.xT if l == 0 else k.xD1)[:, t * 512:(t + 1) * 512]
        else:
            src = (k.ctxT if l == 0 else k.ctxD1)[:, :]
        ph.dma(xv[:, :, 0:N], src.rearrange("(c p) t -> p c t", p=128), writes=xt)
        if isx:
            c_t, s_t = rC[t % 3], rS[t % 3]
            ph.dma(c_t[:, :], k.ropeC[:, t * 512:(t + 1) * 512], writes=[c_t])
            ph.dma(s_t[:, :], k.ropeS[:, t * 512:(t + 1) * 512], writes=[s_t])

    def chain1(ti):
        kind, t, isx, N, off, who, need_q, xt, hT, xv, hv = info(ti)
        ph.act(sv[:, :, 0:N], xv[:, :, 0:N], AF.Square, reads=xt, writes=[sq])

    def chain2(ti):
        kind, t, isx, N, off, who, need_q, xt, hT, xv, hv = info(ti)
        bS = banks[0]
        for kc in range(8):
            ph.mm(bS[:, 0:N], ones[:, :], sv[:, kc, 0:N], start=(kc == 0), stop=(kc == 7), reads=[ones, sq],
                  bank=bS, signal=(kc == 7))

    def chain3(ti):
        kind, t, isx, N, off, who, need_q, xt, hT, xv, hv = info(ti)
        bS = banks[0]
        rstd_from_sums(ph, bS[:, 0:N], D, lnt[:, 0:N], rstd[:, 0:N], bS, lnt, rstd)
        g0 = GS(l, who, 0)
        s0 = MOD(l, who, 0)
        for kc in range(8):
            ph.tt(xv[:, kc, 0:N], xv[:, kc, 0:N], rstd[:, 0:N], ALU.mult, reads=[rstd], writes=[xt[kc]])
            if kc % 3 == 2:
                ph.act(hv[:, kc, 0:N], xv[:, kc, 0:N], AF.Identity, reads=[xt[kc], pp], writes=[hT[kc]],
                       scale=pp[:, g0 + kc:g0 + kc + 1], bias=pp[:, s0 + kc:s0 + kc + 1])
            else:
                ph.ts(hv[:, kc, 0:N], xv[:, kc, 0:N], pp[:, g0 + kc:g0 + kc + 1], pp[:, s0 + kc:s0 + kc + 1],
                      ALU.mult, ALU.add, reads=[xt[kc], pp], writes=[hT[kc]], eng='gpsimd')

    def work(ti, part):
        kind, t, isx, N, off, who, need_q, xt, hT, xv, hv = info(ti)
        if isx:
            c_t, s_t = rC[t % 3], rS[t % 3]

        def proj(ch, bank):
            for kc in range(8):
                ph.mm(bank[:, 0:N], wv[:, kc, ch * 128:(ch + 1) * 128], hv[:, kc, 0:N], start=(kc == 0),
                      stop=(kc == 7), reads=[wT, hT[kc]], bank=bank, signal=(kc == 7))

        def out_tile():
            o = qo[qo_i[0] % 4]
            qo_i[0] += 1
            return o

        jobs = []
        for j in range(3):
            if need_q:
                jobs.append((CH_QA + j, CH_QAS + j, True, gq, gqs, k.qaD[j, :, off:off + N]))
        jobs.append((CH_KA, CH_KAS, True, gk, gks, k.kaD[:, off:off + N]))
        for j in range(3):
            if need_q:
                jobs.append((CH_QC + j, CH_QCS + j, False, None, None, k.qcD[j, :, off:off + N]))
        jobs.append((CH_KC, CH_KCS, False, None, None, k.kcD[:, off:off + N]))
        nsplit = 3 if len(jobs) > 3 else 1
        jobs = jobs[:nsplit] if part == 0 else jobs[nsplit:]
        for (ch, chs, normed, g, gs_, dst) in jobs:
            pi = pair_i[0]
            pair_i[0] += 1
            b1 = banks[1 + 2 * (pi % 2)]
            b2 = banks[2 + 2 * (pi % 2)]
            bN = banks[5]
            proj(ch, b1)
            if isx:
                proj(chs, b2)
            o = out_tile()
            t1 = t1s[pi % 2]
            t2 = t2s[pi % 2]
            if normed:
                sh_ = sqh[pi % 2]
                ri = rinv[pi % 2]
                ph.act(sh_[:, 0:N], b1[:, 0:N], AF.Square, reads=[], writes=[b1, sh_])
                ph.mm(bN[:, 0:N], bones[:, :], sh_[:, 0:N], start=True, stop=True, reads=[bones, sh_], bank=bN)
                rstd_from_sums(ph, bN[:, 0:N], 64, lnh[:, 0:N], ri[:, 0:N], bN, lnh, ri)
                if isx:
                    ph.stt(t1[:, 0:N], b1[:, 0:N], g, c_t[:, 0:N], ALU.mult, ALU.mult, reads=[sm, c_t], writes=[b1, t1])
                    ph.stt(t2[:, 0:N], b2[:, 0:N], gs_, s_t[:, 0:N], ALU.mult, ALU.mult, reads=[sm, s_t], writes=[b2, t2])
                    ph.tt(t1[:, 0:N], t1[:, 0:N], t2[:, 0:N], ALU.add, reads=[t2], writes=[t1], eng='gpsimd')
                    ph.tt(o[:, 0:N], t1[:, 0:N], ri[:, 0:N], ALU.mult, reads=[t1, ri], writes=[o])
                else:
                    ph.stt(o[:, 0:N], b1[:, 0:N], g, ri[:, 0:N], ALU.mult, ALU.mult, reads=[sm, ri], writes=[b1, o])
            else:
                if isx:
                    ph.tt(t1[:, 0:N], b1[:, 0:N], c_t[:, 0:N], ALU.mult, reads=[c_t], writes=[b1, t1])
                    ph.tt(t2[:, 0:N], b2[:, 0:N], s_t[:, 0:N], ALU.mult, reads=[s_t], writes=[b2, t2])
                    ph.tt(o[:, 0:N], t1[:, 0:N], t2[:, 0:N], ALU.add, reads=[t1, t2], writes=[o], eng='gpsimd')
                else:
                    ph.act(o[:, 0:N], b1[:, 0:N], AF.Copy, reads=[], writes=[b1, o])
            ph.dma(dst, o[:, 0:N], reads=[o], eng='sync')
        if part == 0:
            return
        for s_ in range(N // 128):
            b6, b7 = banks[6], banks[7]
            ncol = 768 if need_q else 256
            for kc in range(8):
                lw = hv[:, kc, s_ * 128:(s_ + 1) * 128]
                ph.mm(b6[:, 0:min(ncol, 512)], lw, wv[:, kc, TM0:TM0 + min(ncol, 512)], start=(kc == 0), stop=(kc == 7),
                      reads=[wT, hT[kc]], bank=b6, signal=(kc == 7 and not need_q))
                if need_q:
                    ph.mm(b7[:, 0:256], lw, wv[:, kc, TM0 + 512:TM0 + 768], start=(kc == 0), stop=(kc == 7),
                          reads=[wT, hT[kc]], bank=b7, signal=(kc == 7))
            vt = vts[vt_i[0] % 4]
            zt = zts[vt_i[0] % 4]
            vt_i[0] += 1
            vin = b6[:, 0:256].rearrange("p (a g d) -> p a g d", a=2, g=2)
            vout = vt[:, :].rearrange("p (a w) -> p a w", a=2)
            ph.copy(vout[:, :, 0:64], vin[:, :, 0, :], reads=[], writes=[b6, vt], eng='scalar')
            ph.copy(vout[:, :, 192:256], vin[:, :, 1, :], reads=[], writes=[b6, vt], eng='scalar')
            ph.dma(k.vD[off + s_ * 128: off + (s_ + 1) * 128, :], vt[:, :], reads=[vt], eng='sync')
            if need_q:
                ph.copy(zt[:, 0:256], b6[:, 256:512], reads=[], writes=[b6, zt], eng='scalar')
                ph.copy(zt[:, 256:512], b7[:, 0:256], reads=[], writes=[b7, zt], eng='scalar')
                zd = k.zD if isx else k.zDc
                r0 = (t * 512 if isx else 0) + s_ * 128
                ph.dma(zd[:, r0:r0 + 128, :].rearrange("r p j -> p r j"), v3(zt[:, :], 256), reads=[zt], eng='sync')

    NTL = len(tiles)
    pre(0)
    pre(1)
    chain1(0)
    chain2(0)
    chain3(0)
    for ti in range(NTL):
        if ti + 1 < NTL:
            chain1(ti + 1)
        work(ti, 0)
        if ti + 1 < NTL:
            chain2(ti + 1)
            chain3(ti + 1)
        if ti + 2 < NTL:
            pre(ti + 2)
        work(ti, 1)
    ph.finish()


def attn_epilogue(ph, bO, bR, half, N, oe, rr, onesf, selhi, dst_ap):
    if half == 0:
        ph.copy(oe[0:65, 0:N], bO[0:65, 0:N], reads=[], writes=[bO, oe])
        ph.op('vector', lambda e: e.reciprocal(out=rr[64:65, 0:N], in_=oe[64:65, 0:N]), reads=[oe], writes=[rr])
        ph.mm(bR[0:64, 0:N], onesf[64:65, 0:64], rr[64:65, 0:N], start=True, stop=True, reads=[onesf, rr], bank=bR)
        ph.tt(dst_ap, oe[0:64, 0:N] if dst_ap.ndim == 2 else v3(oe[0:64, 0:N], 128),
              bR[0:64, 0:N] if dst_ap.ndim == 2 else v3(bR[0:64, 0:N], 128), ALU.mult, reads=[oe], writes=[bR, ph._dst])
    else:
        ph.copy(oe[:, 0:N], bO[:, 0:N], reads=[], writes=[bO, oe])
        ph.op('vector', lambda e: e.reciprocal(out=rr[0:1, 0:N], in_=oe[0:1, 0:N]), reads=[oe], writes=[rr])
        ph.mm(bR[:, 0:N], selhi[0:1, 0:128], rr[0:1, 0:N], start=True, stop=True, reads=[selhi, rr], bank=bR)
        ph.tt(dst_ap, oe[64:128, 0:N] if dst_ap.ndim == 2 else v3(oe[64:128, 0:N], 128),
              bR[64:128, 0:N] if dst_ap.ndim == 2 else v3(bR[64:128, 0:N], 128), ALU.mult, reads=[oe], writes=[bR, ph._dst])


def attn_consts(ph):
    onesf = ph.sb("onesf", [128, 128], F32)
    selhi = ph.sb("selhi", [128, 128], F32)
    ph.memset(onesf[:, :], 1.0, writes=[onesf])
    ph.memset(selhi[:, :], 0.0, writes=[selhi])
    ph.memset(selhi[0:1, 64:128], 1.0, writes=[selhi])
    return onesf, selhi


def phase_C(k, l):
    nc = k.nc
    S = k.S
    NT = S // 512
    NKC = 2 + S // 128
    ph = Phase(nc, "C%d" % l)
    pairs = []
    for i in range(4):
        nm = ph._nm("pp")
        pairs.append(ph.st.enter_context(nc.psum_tensor(nm, [128, 1024], F32)))
    NST, LA = 3, 2
    ST = [Tile(pairs[0], "st0"), Tile(pairs[1], "st1"), Tile(pairs[2], "st2")]
    bO = [Tile(pairs[3], "bO0"), Tile(pairs[3], "bO1")]
    kT = ph.sb("kT", [128, CTX + S], BF16)
    vA = ph.sb("vA", [128, NKC * 256], BF16)
    vAv = v3(vA[:, :], 256)
    qts = ph.sb("qt", [128, 3 * 512], BF16, n=2)
    pts = ph.sb("pt", [128, 1024], BF16, n=4)
    oes = ph.sb("oe", [128, 1024], F32, n=2)
    rrs = ph.sb("rr", [128, 1024], F32, n=2)
    ots = ph.sb("ot", [128, 3 * 512], BF16, n=2)
    kTk, vAk = [], []
    for c0 in range(0, NKC, 8):
        c1 = min(NKC, c0 + 8)
        kTk.append(Tile(kT.t, "kTk%d" % c0))
        vAk.append(Tile(vA.t, "vAk%d" % c0))
        ph.dma(kT[:, c0 * 128:c1 * 128], k.kaD[:, c0 * 128:c1 * 128], writes=[kTk[-1]])
        ph.dma(vAv[:, c0:c1, :], k.vD[c0 * 128:c1 * 128, 0:256].rearrange("(c p) f -> p c f", p=128), writes=[vAk[-1]])

    tiles = ([('ctx', 0)] if l == 0 else []) + [('x', t) for t in range(NT)]

    def tile_info(ti):
        kind, t = tiles[ti]
        N = 512 if kind == 'x' else CTX
        off = CTX + t * 512 if kind == 'x' else 0
        return N, off

    def load_q(ti):
        N, off = tile_info(ti)
        qt = qts[ti % 2]
        ph.dma(v3(qt[:, :], 512)[:, :, 0:N], k.qaD[:, :, off:off + N].rearrange("c p t -> p c t"), writes=[qt])

    steps = []
    for ti in range(len(tiles)):
        nkc = 2 if tiles[ti][0] == 'ctx' else NKC
        for j in range(3):
            for c in range(nkc):
                steps.append((ti, j, c, nkc))

    def emit_qk(si):
        ti, j, c, nkc = steps[si]
        N, off = tile_info(ti)
        qt = qts[ti % 2]
        st = ST[si % NST]
        qv = v3(qt[:, :], 512)
        ph.mm(st[:, 0:N], kT[0:64, c * 128:(c + 1) * 128], qv[0:64, j, 0:N], start=True, stop=True,
              reads=[kTk[c // 8], qt], bank=st, signal=False)
        ph.mm(st[:, 512:512 + N], kT[64:128, c * 128:(c + 1) * 128], qv[64:128, j, 0:N], start=True, stop=True,
              reads=[kTk[c // 8], qt], bank=st, signal=True)

    deferred = []
    pctok = Tile(None, 'pctok')
    load_q(0)
    if len(tiles) > 1:
        load_q(1)
    for s0_ in range(min(LA, len(steps))):
        emit_qk(s0_)
    grp = 0
    for si, (ti, j, c, nkc) in enumerate(steps):
        N, off = tile_info(ti)
        if c == 0 and j == 0 and ti >= 1 and ti + 1 < len(tiles):
            load_q(ti + 1)
        if si + LA < len(steps):
            emit_qk(si + LA)
        st = ST[si % NST]
        pt = pts[si % 4]
        ph.act(v3(pt[:, :], 512)[:, :, 0:N], v3(st[:, :], 512)[:, :, 0:N], AF.Exp, reads=[], writes=[st, pt], scale=0.125)
        ph.mm(bO[0][:, 0:N], vAv[:, c, 0:128], pt[:, 0:N], start=(c == 0), stop=(c == nkc - 1),
              reads=[vAk[c // 8], pt], bank=bO[0], signal=False)
        ph.mm(bO[1][:, 512:512 + N], vAv[:, c, 128:256], pt[:, 512:512 + N], start=(c == 0), stop=(c == nkc - 1),
              reads=[vAk[c // 8], pt], bank=bO[1], signal=True)
        if c == nkc - 1:
            ot = ots[ti % 2]
            oe = oes[grp % 2]
            rr = rrs[grp % 2]
            grp += 1
            ov = v3(ot[:, :], 512)

            ph.copy(oe[:, 0:N], bO[0][:, 0:N], reads=[], writes=[bO[0], oe], eng='vector')
            ph.copy(oe[:, 512:512 + N], bO[1][:, 512:512 + N], reads=[], writes=[bO[1], oe])
            ph.op('vector', lambda e, rr=rr, oe=oe, N_=N: e.reciprocal(out=rr[0:64, 0:N_], in_=oe[64:128, 0:N_]),
                  reads=[oe], writes=[rr])
            ph.op('vector', lambda e, rr=rr, oe=oe, N_=N: e.reciprocal(out=rr[64:128, 512:512 + N_], in_=oe[0:64, 512:512 + N_]),
                  reads=[oe], writes=[rr])
            ph.tt(ov[0:64, j, 0:N], oe[0:64, 0:N], rr[0:64, 0:N], ALU.mult, reads=[oe, rr], writes=[ot])
            trig = grp == (4 if l == 0 else 1)
            ph.tt(ov[64:128, j, 0:N], oe[64:128, 512:512 + N], rr[64:128, 512:512 + N], ALU.mult, reads=[oe, rr],
                  writes=[ot, pctok] if trig else [ot])
            if j == 2:
                ph.dma(k.oD[0:3, :, off:off + N].rearrange("c p t -> p c t"), ov[:, :, 0:N], reads=[ot], eng='sync')
            if trig:
                precast(ph, k, l, ffn=True, win=False, dep=[pctok])
                precast(ph, k, l + 1, ffn=False, win=True, dep=[pctok])
    for _, fn in deferred:
        fn()
    ph.finish()


def phase_D(k, l):
    nc = k.nc
    S = k.S
    NT = S // 512
    NQB = S // 128
    NKC = 2 + NQB
    ph = Phase(nc, "D%d" % l)
    pairs = []
    for i in range(4):
        nm = ph._nm("pp")
        pairs.append(ph.st.enter_context(nc.psum_tensor(nm, [128, 1024], F32)))
    NST, LA = 3, 2
    ST = [Tile(pairs[0], "st0"), Tile(pairs[1], "st1"), Tile(pairs[2], "st2")]
    bO = [Tile(pairs[3], "bO0"), Tile(pairs[3], "bO1")]
    kT = ph.sb("kT", [128, CTX + S], BF16)
    vC = ph.sb("vC", [128, NKC * 256], BF16)
    vCv = v3(vC[:, :], 256)
    qts = ph.sb("qt", [128, 3 * 512], BF16, n=2)
    pts = ph.sb("pt", [128, 1024], BF16, n=4)
    NR = 3
    oes = ph.sb("oe", [128, 1024], F32, n=NR)
    rrs = ph.sb("rr", [128, 1024], F32, n=NR)
    ots = ph.sb("ot", [128, 3 * 512], BF16, n=2)
    msk = ph.sb("msk", [128, 256], BF16)
    sk32 = ph.sb("sk32", [1, 6 * 256], F32)
    skb = ph.sb("skb", [1, 6 * 256], BF16)
    esel = ph.sb("esel", [1, 256], BF16)
    kTk, vCk = [], []
    for c0 in range(0, NKC, 8):
        c1 = min(NKC, c0 + 8)
        kTk.append(Tile(kT.t, "kTk%d" % c0))
        vCk.append(Tile(vC.t, "vCk%d" % c0))
        ph.dma(kT[:, c0 * 128:c1 * 128], k.kcD[:, c0 * 128:c1 * 128], writes=[kTk[-1]])
        ph.dma(vCv[:, c0:c1, :], k.vD[c0 * 128:c1 * 128, 256:512].rearrange("(c p) f -> p c f", p=128), writes=[vCk[-1]])
    ph.dma(msk[:, :], k.cbf[:, CB_MASK:CB_MASK + 256], writes=[msk], eng='gpsimd')
    ph.dma(sk32[:, :], k.sinkrow[l, :, :], writes=[sk32])
    ph.act(skb[:, :], sk32[:, :], AF.Exp, reads=[sk32], writes=[skb])
    ph.memset(esel[:, :], 0.0, writes=[esel])
    ph.memset(esel[0:1, 64:128], 1.0, writes=[esel])
    ph.memset(esel[0:1, 128:192], 1.0, writes=[esel])
    skv = v3(skb[:, :], 256)
    mv = v3(msk[:, :], 128)

    tiles = ([('ctx', 0)] if l == 0 else []) + [('x', t) for t in range(NT)]

    def tile_info(ti):
        kind, t = tiles[ti]
        N = 512 if kind == 'x' else CTX
        off = CTX + t * 512 if kind == 'x' else 0
        return kind, t, N, off

    def load_q(ti):
        kind, t, N, off = tile_info(ti)
        qt = qts[ti % 2]
        ph.dma(v3(qt[:, :], 512)[:, :, 0:N], k.qcD[:, :, off:off + N].rearrange("c p t -> p c t"), writes=[qt])

    groups = []
    for ti in range(len(tiles)):
        kind, t, N, off = tile_info(ti)
        qt, ot = qts[ti % 2], ots[ti % 2]
        qv, ov = v3(qt[:, :], 512), v3(ot[:, :], 512)
        if kind == 'ctx':
            for j in range(3):
                g = dict(ti=ti, NN=CTX, n3=1, nq=CTX, chunks=[(0, 0, None), (128, 1, None)],
                         q=[qv[0:64, j, 0:CTX], qv[64:128, j, 0:CTX]],
                         sink=[skv[0:1, j, 0:CTX], skv[0:1, 3 + j, 0:CTX]],
                         dst=[ov[0:64, j, 0:CTX], ov[64:128, j, 0:CTX]], last=(j == 2))
                groups.append(g)
        else:
            for nb in range(4):
                n = t * 4 + nb
                chunks = [(0, 0, None), (128, 1, None)]
                if n - 1 >= 0:
                    chunks.append((CTX + (n - 1) * 128, 2 + n - 1, 0))
                chunks.append((CTX + n * 128, 2 + n, None))
                if n + 1 < NQB:
                    chunks.append((CTX + (n + 1) * 128, 2 + n + 1, 1))
                cs = slice(nb * 128, (nb + 1) * 128)
                g = dict(ti=ti, NN=384, n3=3, nq=128, chunks=chunks,
                         q=[qv[0:64, :, cs], qv[64:128, :, cs]],
                         sink=[skv[0:1, 0:3, 0:128], skv[0:1, 3:6, 0:128]],
                         dst=[ov[0:64, :, cs], ov[64:128, :, cs]], last=(nb == 3))
                groups.append(g)
    steps = []
    for gi, g in enumerate(groups):
        for ci in range(len(g['chunks'])):
            steps.append((gi, ci))

    def emit_qk(si):
        gi, ci = steps[si]
        g = groups[gi]
        NN = g['NN']
        kcol = g['chunks'][ci][0]
        st = ST[si % NST]
        qt = qts[g['ti'] % 2]
        kt_ = kTk[(kcol // 128) // 8]
        ph.mm(st[:, 0:NN], kT[0:64, kcol:kcol + 128], g['q'][0], start=True, stop=True, reads=[kt_, qt], bank=st, signal=False)
        ph.mm(st[:, 512:512 + NN], kT[64:128, kcol:kcol + 128], g['q'][1], start=True, stop=True, reads=[kt_, qt], bank=st,
              signal=True)

    deferred = []
    load_q(0)
    if len(tiles) > 1:
        load_q(1)
    for s0_ in range(min(LA, len(steps))):
        emit_qk(s0_)
    loaded = {0, 1}
    for si, (gi, ci) in enumerate(steps):
        g = groups[gi]
        NN, n3, nq, ti = g['NN'], g['n3'], g['nq'], g['ti']
        kind, t, N, off = tile_info(ti)
        nch = len(g['chunks'])
        kcol, vci, mi = g['chunks'][ci]
        if ci == 0 and ti + 1 < len(tiles) and (ti + 1) not in loaded and (gi == 0 or groups[gi - 1]['ti'] != ti):
            loaded.add(ti + 1)
            load_q(ti + 1)
        if si + LA < len(steps):
            emit_qk(si + LA)
        st = ST[si % NST]
        pt = pts[si % 4]
        ph.act(v3(pt[:, :], 512)[:, :, 0:NN], v3(st[:, :], 512)[:, :, 0:NN], AF.Exp, reads=[], writes=[st, pt], scale=0.125)
        if mi is not None:
            pv4 = pt[:, :].rearrange("p (h a q) -> p h a q", h=2, a=4)[:, :, 0:3, :]
            mb = mv[:, mi, :].unsqueeze(1).unsqueeze(1).to_broadcast([128, 2, 3, 128])
            ph.tt(pv4, pv4, mb, ALU.mult, reads=[msk], writes=[pt], eng='gpsimd')
        ph.mm(bO[0][:, 0:NN], vCv[:, vci, 0:128], pt[:, 0:NN], start=(ci == 0), stop=False,
              reads=[vCk[vci // 8], pt], bank=bO[0], signal=False)
        ph.mm(bO[1][:, 512:512 + NN], vCv[:, vci, 128:256], pt[:, 512:512 + NN], start=(ci == 0), stop=False,
              reads=[vCk[vci // 8], pt], bank=bO[1], signal=(ci < nch - 1))
        if ci == nch - 1:
            ph.mm(bO[0][:, 0:NN], esel[0:1, 0:128], g['sink'][0], start=False, stop=True, reads=[esel, skb], bank=bO[0],
                  signal=False)
            ph.mm(bO[1][:, 512:512 + NN], esel[0:1, 128:256], g['sink'][1], start=False, stop=True, reads=[esel, skb],
                  bank=bO[1], signal=True)
            oe, rr = oes[gi % NR], rrs[gi % NR]
            ot = ots[ti % 2]
            ph.copy(oe[:, 0:NN], bO[0][:, 0:NN], reads=[], writes=[bO[0], oe], eng='scalar')
            ph.copy(oe[:, 512:512 + NN], bO[1][:, 512:512 + NN], reads=[], writes=[bO[1], oe], eng='scalar')
            ph.op('vector', lambda e, rr=rr, oe=oe, N_=NN: e.reciprocal(out=rr[0:64, 0:N_], in_=oe[64:128, 0:N_]),
                  reads=[oe], writes=[rr])
            ph.act(rr[0:64, 512:512 + NN], oe[0:64, 512:512 + NN], AF.Ln, reads=[oe], writes=[rr])
            ph.act(rr[64:128, 512:512 + NN], rr[0:64, 512:512 + NN], AF.Exp, reads=[], writes=[rr], scale=-1.0)

            def vw(ap, g=g, nq=nq):
                return ap if g['n3'] == 1 else v3(ap, nq)
            ph.tt(g['dst'][0], vw(oe[0:64, 0:NN]), vw(rr[0:64, 0:NN]), ALU.mult, reads=[oe, rr], writes=[ot])
            ph.tt(g['dst'][1], vw(oe[64:128, 512:512 + NN]), vw(rr[64:128, 512:512 + NN]), ALU.mult, reads=[oe, rr], writes=[ot])
            if g['last']:
                ov = v3(ot[:, :], 512)
                ph.dma(k.oD[5:8, :, off:off + N].rearrange("c p t -> p c t"), ov[:, :, 0:N], reads=[ot], eng='sync')
    for _, fn in deferred:
        fn()
    ph.finish()


CB_M1, CB_M2, CB_C128, CB_S128, CB_C256, CB_S256, CB_MASK = 0, 128, 256, 384, 512, 1024, 1536
CBW = 1536 + 256


def phase_E(k, l):
    nc = k.nc
    S = k.S
    L1 = S // 128
    P2 = 2 * L1
    ph = Phase(nc, "E%d" % l)
    banks = ph.psum_banks(8)
    cb = ph.sb("cb", [128, CBW], BF16)
    tw = ph.sb("tw", [128, 256], F32)
    ph.dma(cb[:, :], k.cbf[:, :], writes=[cb], eng='gpsimd')
    ph.dma(tw[:, :], k.cmat[:, 256:512], writes=[tw])
    SL = 32
    vs = ph.sb("v", [128, SL * 256], BF16, n=2)
    Zs = ph.sb("Zs", [128, SL * 256], BF16, n=2)
    t1s = ph.sb("t1", [128, 512], F32, n=2)
    t2s = ph.sb("t2", [128, 512], F32, n=2)
    Zt = ph.sb("Zt", [128, 2 * L1 * 256], BF16)
    ofs = ph.sb("of", [128, S], BF16, n=2)
    zdv = k.zD.rearrange("r (a b) j -> r a (b j)", b=128)
    ZDv = k.ZD.rearrange("r a b j -> (r a) (b j)")
    g = 0
    zdb = [Tile(None, "ZDslab%d" % i) for i in range(128 // SL)]
    for sl in range(128 // SL):
        v = vs[sl % 2]
        Zo = Zs[sl % 2]
        for r in range(2):
            ph.dma(v[r * L1:(r + 1) * L1, :], zdv[r, :, sl * SL * 256:(sl + 1) * SL * 256], writes=[v])
        for cg in range(SL // 2):
            cols = slice(cg * 512, (cg + 1) * 512)
            l2a = sl * SL + cg * 2
            bY, bW = banks[(g % 2) * 2], banks[(g % 2) * 2 + 1]
            t1, t2 = t1s[g % 2], t2s[g % 2]
            g += 1
            ph.mm(bY[0:P2, :], cb[0:P2, CB_M1:CB_M1 + P2], v[0:P2, cols], start=True, stop=True, reads=[cb, v], bank=bY)
            ph.mm(bW[0:P2, :], cb[0:P2, CB_M2:CB_M2 + P2], v[0:P2, cols], start=True, stop=True, reads=[cb, v], bank=bW)
            ph.tt(v3(t1[0:P2, :], 256), v3(bY[0:P2, :], 256), tw[0:P2, l2a:l2a + 2].unsqueeze(2).to_broadcast([P2, 2, 256]),
                  ALU.mult, reads=[tw], writes=[bY, t1])
            ph.tt(v3(t2[0:P2, :], 256), v3(bW[0:P2, :], 256),
                  tw[0:P2, 128 + l2a:128 + l2a + 2].unsqueeze(2).to_broadcast([P2, 2, 256]),
                  ALU.mult, reads=[tw], writes=[bW, t2])
            ph.tt(Zo[0:P2, cols], t1[0:P2, :], t2[0:P2, :], ALU.add, reads=[t1, t2], writes=[Zo], eng='gpsimd')
        ph.dma(ZDv[:, sl * SL * 256:(sl + 1) * SL * 256], Zo[0:P2, :], reads=[Zo], writes=[zdb[sl]], eng='gpsimd')
    Zv = Zt[:, :].rearrange("p (r a j) -> p r a j", r=2, a=L1)
    KG = min(16, L1)
    ztk = {}
    for a0 in range(0, L1, KG):
        for r in range(2):
            ztk[(r, a0)] = Tile(Zt.t, "zt%d_%d" % (r, a0))
            ph.dma(Zv[:, r, a0:a0 + KG, :], k.ZD[r, a0:a0 + KG, :, :].rearrange("a b j -> b a j"), reads=zdb,
                   writes=[ztk[(r, a0)]])
    scale = 1.0 / math.sqrt(S * 64.0)
    gi = 0
    for a0 in range(0, L1, 4):
        for jc in range(2):
            bk = banks[4 + gi % 4]
            for q in range(4):
                a = a0 + q
                ph.mm(bk[:, q * 128:(q + 1) * 128], Zv[:, 0, a, jc * 128:(jc + 1) * 128], cb[:, CB_C128:CB_C128 + 128],
                      start=True, stop=False, reads=[ztk[(0, (a // KG) * KG)], cb], bank=bk, signal=False)
                ph.mm(bk[:, q * 128:(q + 1) * 128], Zv[:, 1, a, jc * 128:(jc + 1) * 128], cb[:, CB_S128:CB_S128 + 128],
                      start=False, stop=True, reads=[ztk[(1, (a // KG) * KG)], cb], bank=bk, signal=(q == 3))
            of = ofs[jc]
            outv = of[:, :].rearrange("p (b a) -> p b a", a=L1)[:, :, a0:a0 + 4]
            inv = bk[:, :].rearrange("p (q b) -> p b q", q=4)
            if gi % 2 == 0:
                ph.act(outv, inv, AF.Copy, reads=[], writes=[bk, of], scale=scale)
            else:
                ph.op('vector', lambda e, outv=outv, inv=inv: e.tensor_scalar(out=outv, in0=inv, scalar1=scale, scalar2=None,
                                                                             op0=ALU.mult), reads=[], writes=[bk, of])
            gi += 1
    for jc in range(2):
        ph.dma(k.oD[3 + jc, :, CTX:CTX + S], ofs[jc][:, :], reads=[ofs[jc]], eng='gpsimd')
    if l == 0:
        zc = ph.sb("zc", [128, 2 * 2 * 256], BF16)
        zcv = zc[:, :].rearrange("p (c r j) -> p c r j", c=2, r=2)
        ofc = ph.sb("ofc", [128, 2 * 256], BF16)
        for c in range(2):
            ph.dma(zcv[:, c, :, :], k.zDc[:, c * 128:(c + 1) * 128, :].rearrange("r p j -> p r j"), writes=[zc])
        sc_c = 1.0 / math.sqrt(CTX * 64.0)
        for jc in range(2):
            bk = banks[jc]
            n = 0
            for c in range(2):
                for r in range(2):
                    base = (CB_C256 if r == 0 else CB_S256) + c * 256
                    ph.mm(bk[:, 0:256], zcv[:, c, r, jc * 128:(jc + 1) * 128], cb[:, base:base + 256],
                          start=(n == 0), stop=(n == 3), reads=[zc, cb], bank=bk, signal=(n == 3))
                    n += 1
            ph.act(ofc[:, jc * 256:(jc + 1) * 256], bk[:, 0:256], AF.Copy, reads=[], writes=[bk, ofc], scale=sc_c)
            ph.dma(k.oD[3 + jc, :, 0:CTX], ofc[:, jc * 256:(jc + 1) * 256], reads=[ofc], eng='gpsimd')
    ph.finish()


def phase_G(k, l):
    nc = k.nc
    S = k.S
    N = 256
    last = (l == DEPTH - 1)
    ph = Phase(nc, "G%d" % l)
    banks = ph.psum_banks(8)
    pp = ph.wrap(k.pp, "pp")
    sm = ph.wrap(k.sm, "sm")
    wo = ph.sb("wo", [128, 8 * D], BF16)
    wg = ph.sb("wg", [128, 8 * DFF], BF16)
    wu = ph.sb("wu", [128, 8 * DFF], BF16)
    wd = ph.sb("wd", [128, NFC * D], BF16)
    wov, wgv, wuv, wdv = v3(wo[:, :], D), v3(wg[:, :], DFF), v3(wu[:, :], DFF), v3(wd[:, :], D)
    xts = ph.sb("xt", [128, 8 * N], F32, n=2)
    ots = ph.sb("ot", [128, 8 * N], BF16, n=2)
    xn = ph.sb("xn", [128, 8 * N], F32)
    hh = ph.sb("hh", [128, 8 * N], BF16)
    sq = hh
    aa = ph.sb("aa", [128, NFC * N], BF16)
    sgs = ph.sb("sg", [128, N], F32, n=2)
    ones = ph.sb("ones", [128, 128], BF16)
    lnt = ph.sb("lnt", [128, N], F32)
    rstd = ph.sb("rstd", [128, N], F32)
    ph.memset(ones[:, :], 1.0, writes=[ones])
    ph.dma(wov, k.woB[l].rearrange("(c p) f -> p c f", p=128), writes=[wo])
    early_load = [True]
    NWG = 4
    WGC = DFF // NWG
    wgs = [Tile(wg.t, "wg%d" % i) for i in range(NWG)]
    wus = [Tile(wu.t, "wu%d" % i) for i in range(NWG)]

    def emit_big_weights():
        for gI in range(NWG):
            cs = slice(gI * WGC, (gI + 1) * WGC)
            ph.dma(wgv[:, :, cs], k.wgB[l, :, cs].rearrange("(c p) f -> p c f", p=128), writes=[wgs[gI]])
            ph.dma(wuv[:, :, cs], k.wuB[l, :, cs].rearrange("(c p) f -> p c f", p=128), writes=[wus[gI]])
        for c0 in range(0, NFC, 11):
            ph.dma(wdv[:, c0:c0 + 11, :], k.wdB[l, c0 * 128:(c0 + 11) * 128, :].rearrange("(c p) f -> p c f", p=128),
                   writes=[wd])

    tiles = [('x', t) for t in range(S // N)] + ([('ctx', 0)] if not last else [])
    lnf = ph.sb("lnf", [128, N], F32)
    rstdf = ph.sb("rstdf", [128, N], F32)
    hv = v3(hh[:, :], N)
    nv = v3(xn[:, :], N)
    av = v3(aa[:, :], N)
    sv = v3(sq[:, :], N)
    bi = [0]
    xnk = [Tile(xn.t, "xnk%d" % c) for c in range(8)]
    hhk = [Tile(hh.t, "hhk%d" % c) for c in range(8)]

    def info(ti):
        kind, t = tiles[ti]
        isx = kind == 'x'
        who = 0 if isx else 1
        xt, ot = xts[ti % 2], ots[ti % 2]
        return kind, t, isx, who, xt, ot, v3(xt[:, :], N), v3(ot[:, :], N)

    def load(ti):
        kind, t, isx, who, xt, ot, xv, ov = info(ti)
        off = CTX + t * N if isx else 0
        if isx:
            src = (k.xT if l == 0 else k.xD1)[:, t * N:(t + 1) * N]
        else:
            src = k.ctxT[:, :]
        ph.dma(xv, src.rearrange("(c p) t -> p c t", p=128), writes=[xt])
        ph.dma(ov, k.oD[:, :, off:off + N].rearrange("c p t -> p c t"), writes=[ot])

    def st_A(ti):
        kind, t, isx, who, xt, ot, xv, ov = info(ti)
        gt1 = MOD(l, who, 2)
        for dc in range(8):
            bk = banks[1 + bi[0] % 2]
            bi[0] += 1
            for mc in range(8):
                ph.mm(bk[:, 0:N], wov[:, mc, dc * 128:(dc + 1) * 128], ov[:, mc, :], start=(mc == 0), stop=(mc == 7),
                      reads=[wo, ot], bank=bk, signal=(mc == 7))
            ph.stt(xv[:, dc, :], bk[:, 0:N], pp[:, gt1 + dc:gt1 + dc + 1], xv[:, dc, :], ALU.mult, ALU.add,
                   reads=[pp], writes=[bk, xt])

    def st_Bn(ti):
        kind, t, isx, who, xt, ot, xv, ov = info(ti)
        ph.act(sv, xv, AF.Square, reads=[xt], writes=hhk)

    def st_Bs(ti):
        bS = banks[0]
        for kc in range(8):
            ph.mm(bS[:, 0:N], ones[:, :], sv[:, kc, :], start=(kc == 0), stop=(kc == 7), reads=[ones, hhk[kc]], bank=bS,
                  signal=(kc == 7))

    def st_Bc(ti):
        kind, t, isx, who, xt, ot, xv, ov = info(ti)
        sh2, gs2 = MOD(l, who, 3), GS(l, who, 1)
        bS = banks[0]
        rstd_from_sums(ph, bS[:, 0:N], D, lnt[:, :], rstd[:, :], bS, lnt, rstd)
        for kc in range(8):
            ph.tt(nv[:, kc, :], xv[:, kc, :], rstd[:, :], ALU.mult, reads=[xt, rstd], writes=[xnk[kc]])
            if kc % 3 == 2:
                ph.act(hv[:, kc, :], nv[:, kc, :], AF.Identity, reads=[xnk[kc], pp], writes=[hhk[kc]],
                       scale=pp[:, gs2 + kc:gs2 + kc + 1], bias=pp[:, sh2 + kc:sh2 + kc + 1])
            else:
                ph.ts(hv[:, kc, :], nv[:, kc, :], pp[:, gs2 + kc:gs2 + kc + 1], pp[:, sh2 + kc:sh2 + kc + 1], ALU.mult,
                      ALU.add, reads=[xnk[kc], pp], writes=[hhk[kc]], eng='gpsimd')

    def st_C(ti):
        for fc in range(NFC):
            bG, bU = banks[3 + (fc % 2) * 2], banks[4 + (fc % 2) * 2]
            sg = sgs[fc % 2]
            for kc in range(8):
                ph.mm(bG[:, 0:N], wgv[:, kc, fc * 128:(fc + 1) * 128], hv[:, kc, :], start=(kc == 0), stop=(kc == 7),
                      reads=[wgs[(fc * 128) // WGC], wgs[(fc * 128 + 127) // WGC], hhk[kc]], bank=bG, signal=(kc == 7))
            for kc in range(8):
                ph.mm(bU[:, 0:N], wuv[:, kc, fc * 128:(fc + 1) * 128], hv[:, kc, :], start=(kc == 0), stop=(kc == 7),
                      reads=[wus[(fc * 128) // WGC], wus[(fc * 128 + 127) // WGC], hhk[kc]], bank=bU, signal=(kc == 7))
            ph.act(sg[:, :], bG[:, 0:N], AF.Silu, reads=[], writes=[bG, sg])
            ph.tt(av[:, fc, :], sg[:, :], bU[:, 0:N], ALU.mult, reads=[sg], writes=[bU, aa])

    def st_D(ti, dcs):
        kind, t, isx, who, xt, ot, xv, ov = info(ti)
        gt2 = MOD(l, who, 5)
        for dc in dcs:
            bk = banks[1 + bi[0] % 2]
            bi[0] += 1
            for fc in range(NFC):
                ph.mm(bk[:, 0:N], wdv[:, fc, dc * 128:(dc + 1) * 128], av[:, fc, :], start=(fc == 0), stop=(fc == NFC - 1),
                      reads=[wd, aa], bank=bk, signal=(fc == NFC - 1))
            ph.stt(xv[:, dc, :], bk[:, 0:N], pp[:, gt2 + dc:gt2 + dc + 1], xv[:, dc, :], ALU.mult, ALU.add,
                   reads=[pp], writes=[bk, xt])

    def st_out(ti):
        kind, t, isx, who, xt, ot, xv, ov = info(ti)
        if last:
            sv2 = v3(aa[:, 0:8 * N], N)
            ph.act(sv2, xv, AF.Square, reads=[xt], writes=[aa])
            bS = banks[7]
            for kc in range(8):
                ph.mm(bS[:, 0:N], ones[:, :], sv2[:, kc, :], start=(kc == 0), stop=(kc == 7), reads=[ones, aa], bank=bS,
                      signal=(kc == 7))
            rstd_from_sums(ph, bS[:, 0:N], D, lnf[:, :], rstdf[:, :], bS, lnf, rstdf)
            for kc in range(8):
                ph.stt(nv[:, kc, :], xv[:, kc, :], sm[:, SM_GFIN + kc:SM_GFIN + kc + 1], rstdf[:, :], ALU.mult, ALU.mult,
                       reads=[xt, sm, rstdf], writes=[xnk[kc]])
            ph.dma(k.yT[:, t * N:(t + 1) * N].rearrange("(c p) t -> p c t", p=128), nv, reads=xnk, eng='sync')
        else:
            dst = k.xD1[:, t * N:(t + 1) * N] if isx else k.ctxD1[:, :]
            ph.dma(dst.rearrange("(c p) t -> p c t", p=128), xv, reads=[xt], eng='sync')

    NTL = len(tiles)
    load(0)
    if NTL > 1:
        load(1)
    emit_big_weights()
    st_A(0)
    st_Bn(0)
    st_Bs(0)
    st_Bc(0)
    for ti in range(NTL):
        st_C(ti)
        if ti + 1 < NTL:
            st_A(ti + 1)
            st_Bn(ti + 1)
            st_D(ti, range(0, 2))
            st_Bs(ti + 1)
            st_Bc(ti + 1)
            st_D(ti, range(2, 8))
        else:
            st_D(ti, range(0, 8))
        st_out(ti)
        if ti + 2 < NTL:
            load(ti + 2)
    ph.finish()


def build_program(S):
    nc = bass.Bass("TRN2", target_bir_lowering=False)
    k = K()
    k.nc = nc
    k.S = S
    L1 = S // 128

    def din(name, shape, dt=F32):
        return nc.dram_tensor(name, list(shape), dt, kind="ExternalInput").ap()

    def dscr(name, shape, dt):
        return nc.dram_tensor(name, list(shape), dt).ap()

    k.xT = din("xT", [D, S])
    k.ctxT = din("ctxT", [D, CTX])
    k.cvec = din("cvec", [128, 16])
    k.small = din("small", [128, SMW])
    k.w_ada = din("w_ada", [DEPTH, D, 6 * D])
    k.w_in_p = din("w_in_p", [DEPTH, D, 2304])
    k.w_in_fT = din("w_in_fT", [DEPTH, 256, D])
    k.w_out_p = din("w_out_p", [DEPTH, D, D])
    k.w_gate = din("w_gate", [DEPTH, D, DFF])
    k.w_up = din("w_up", [DEPTH, D, DFF])
    k.w_down = din("w_down", [DEPTH, DFF, D])
    k.sinkrow = din("sinkrow", [DEPTH, 1, 6 * 256])
    k.ropeC = din("ropeC", [128, S])
    k.ropeS = din("ropeS", [128, S])
    k.cmat = din("cmat", [128, 512])
    k.cbf = din("cbf", [128, CBW])
    k.yT = nc.dram_tensor("yT", [D, S], F32, kind="ExternalOutput").ap()

    k.wfD = dscr("wfD", [DEPTH, D, 512], BF16)
    k.qaD = dscr("qaD", [3, 128, CTX + S], BF16)
    k.qcD = dscr("qcD", [3, 128, CTX + S], BF16)
    k.kaD = dscr("kaD", [128, CTX + S], BF16)
    k.kcD = dscr("kcD", [128, CTX + S], BF16)
    k.vD = dscr("vD", [CTX + S, VW], BF16)
    k.zD = dscr("zD", [2, S, 256], BF16)
    k.zDc = dscr("zDc", [2, CTX, 256], BF16)
    k.ZD = dscr("ZD", [2, L1, 128, 256], BF16)
    k.oD = dscr("oD", [8, 128, CTX + S], BF16)
    k.woB = dscr("woB", [DEPTH, D, D], BF16)
    k.wgB = dscr("wgB", [DEPTH, D, DFF], BF16)
    k.wuB = dscr("wuB", [DEPTH, D, DFF], BF16)
    k.wdB = dscr("wdB", [DEPTH, DFF, D], BF16)
    k.winB = dscr("winB", [DEPTH, D, 2304], BF16)
    k.xD1 = dscr("xD1", [D, S], F32)
    k.ctxD1 = dscr("ctxD1", [D, CTX], F32)

    with contextlib.ExitStack() as st:
        Phase.state = SemState(nc, st)
        k.pp = st.enter_context(nc.sbuf_tensor("pp_persist", [128, PPW], F32))
        k.sm = st.enter_context(nc.sbuf_tensor("sm_persist", [128, SMW], F32))
        import os
        sel = os.environ.get("KPHASES", "")
        for l in range(DEPTH):
            for nm, f in (("A", phase_A), ("B", phase_B), ("C", phase_C), ("D", phase_D), ("E", phase_E), ("G", phase_G)):
                if sel and ("%s%d" % (nm, l)) not in sel.split(","):
                    continue
                f(k, l)
    return nc


def host_constants(S):
    L1 = S // 128
    f32 = np.float32
    tok = np.arange(S)
    row = (tok // 64).astype(f32)
    col = (tok % 64).astype(f32)
    inv_freq = (np.float32(10000.0) ** (-np.arange(0, 32, 2, dtype=f32) / np.float32(32.0))).astype(f32)
    ang = np.concatenate([row[:, None] * inv_freq, col[:, None] * inv_freq], axis=-1).astype(f32)
    cos = np.cos(ang).astype(f32).T
    sin = np.sin(ang).astype(f32).T
    ropeC = np.concatenate([cos, cos, cos, cos], axis=0)
    ropeS = np.concatenate([-sin, sin, -sin, sin], axis=0)
    c = np.arange(64)
    a64 = 2 * np.pi * np.outer(c, c) / 64.0
    C64, S64 = np.cos(a64), np.sin(a64)
    Z = np.zeros((64, 64))
    cmat = np.zeros((128, 512), f32)
    cmat[:, 0:128] = np.block([[C64, Z], [Z, C64]])
    cmat[:, 128:256] = np.block([[-S64, Z], [Z, -S64]])
    k1 = np.arange(L1)
    l2 = np.arange(128)
    at = 2 * np.pi * np.outer(k1, l2) / S
    cmat[0:2 * L1, 256:384] = np.concatenate([np.cos(at), np.cos(at)], axis=0)
    cmat[0:2 * L1, 384:512] = np.concatenate([np.sin(at), np.sin(at)], axis=0)
    cbf = np.zeros((128, CBW), f32)
    a1 = 2 * np.pi * np.outer(k1, k1) / L1
    Cc, Sc = np.cos(a1), np.sin(a1)
    M1 = np.block([[Cc, Sc], [-Sc, Cc]])
    M2 = np.block([[-Sc, Cc], [-Cc, -Sc]])
    cbf[0:2 * L1, CB_M1:CB_M1 + 2 * L1] = M1.T
    cbf[0:2 * L1, CB_M2:CB_M2 + 2 * L1] = M2.T
    a128 = 2 * np.pi * np.outer(l2, l2) / 128.0
    cbf[:, CB_C128:CB_C128 + 128] = np.cos(a128)
    cbf[:, CB_S128:CB_S128 + 128] = np.sin(a128)
    n256 = np.arange(256)
    a256 = 2 * np.pi * np.outer(n256, n256) / 256.0
    C256, S256 = np.cos(a256), np.sin(a256)
    for cch in range(2):
        cbf[:, CB_C256 + cch * 256:CB_C256 + (cch + 1) * 256] = C256[cch * 128:(cch + 1) * 128, :]
        cbf[:, CB_S256 + cch * 256:CB_S256 + (cch + 1) * 256] = S256[cch * 128:(cch + 1) * 128, :]
    a = np.arange(128)[:, None]
    i = np.arange(128)[None, :]
    cbf[:, CB_MASK:CB_MASK + 128] = (a >= i)
    cbf[:, CB_MASK + 128:CB_MASK + 256] = (a <= i)
    return ropeC.astype(f32), ropeS.astype(f32), cmat, cbf.astype(f32)


def host_layout(inp, S):
    f32 = np.float32
    g = lambda n: np.asarray(inp[n], dtype=f32)
    w_in = g('w_in')

    def pair(base, j):
        return np.concatenate([np.arange(base + 64 * j, base + 64 * j + 64), np.arange(base + 64 * (3 + j), base + 64 * (3 + j) + 64)])

    def swap(idx):
        idx = idx.reshape(-1, 64)
        return np.concatenate([idx[:, 32:], idx[:, :32]], axis=1).reshape(-1)

    cols = []
    qa = [pair(0, j) for j in range(3)]
    ka = np.arange(384, 512)
    qc = [pair(896, j) for j in range(3)]
    kc = np.arange(1280, 1408)
    cols += qa + [swap(q) for q in qa] + [ka, swap(ka)] + qc + [swap(q) for q in qc] + [kc, swap(kc)]
    cols += [np.arange(512, 640), np.arange(1408, 1536)]
    cols = np.concatenate(cols)
    assert cols.shape[0] == 2304
    w_in_p = np.ascontiguousarray(w_in[:, :, cols])
    w_in_fT = np.ascontiguousarray(np.transpose(w_in[:, :, 640:896], (0, 2, 1)))
    rows = np.concatenate([pair(0, j) for j in range(3)] + [np.arange(384, 640)] + [pair(640, j) for j in range(3)])
    w_out_p = np.ascontiguousarray(g('w_out')[:, rows, :])

    def pl(v):
        return np.ascontiguousarray(v.reshape(8, 128).T)

    small = np.zeros((128, SMW), f32)
    b_ada, g_mix, g_ffn = g('b_ada'), g('g_mix'), g('g_ffn')
    qn, kn = g('q_norm'), g('k_norm')
    p = np.arange(128) % 64
    ps = (p + 32) % 64
    for l in range(DEPTH):
        small[:, SM_BADA(l):SM_BADA(l) + 48] = b_ada[l].reshape(48, 128).T
        small[:, SM_GMIX(l):SM_GMIX(l) + 8] = pl(g_mix[l])
        small[:, SM_GFFN(l):SM_GFFN(l) + 8] = pl(g_ffn[l])
        small[:, SM_QKN(l) + 0] = qn[l][p]
        small[:, SM_QKN(l) + 1] = qn[l][ps]
        small[:, SM_QKN(l) + 2] = kn[l][p]
        small[:, SM_QKN(l) + 3] = kn[l][ps]
    small[:, SM_GFIN:SM_GFIN + 8] = pl(g('g_final'))
    sinkrow = np.ascontiguousarray(np.repeat(g('sink')[:, None, :, None], 256, axis=3).reshape(DEPTH, 1, 6 * 256))
    ropeC, ropeS, cmat, cbf = host_constants(S)
    shared = dict(small=small, w_ada=g('w_ada'), w_in_p=w_in_p, w_in_fT=w_in_fT, w_out_p=w_out_p,
                  w_gate=g('w_gate'), w_up=g('w_up'), w_down=g('w_down'), sinkrow=sinkrow,
                  ropeC=ropeC, ropeS=ropeS, cmat=cmat, cbf=cbf)
    x, c, ctx, c_ctx = g('x'), g('c'), g('ctx'), g('c_ctx')
    B = x.shape[0]
    maps = []
    for b in range(B):
        cvec = np.zeros((128, 16), f32)
        cvec[:, 0::2] = pl(c[b])
        cvec[:, 1::2] = pl(c_ctx)
        m = dict(shared)
        m['xT'] = np.ascontiguousarray(x[b].T)
        m['ctxT'] = np.ascontiguousarray(ctx[b].T)
        m['cvec'] = cvec
        maps.append(m)
    return maps


_CACHE = {}


def run(inp, S):
    maps = host_layout(inp, S)
    if S not in _CACHE:
        _CACHE[S] = build_program(S)
    nc = _CACHE[S]
    res = run_bass_kernel_spmd(nc, maps, core_ids=list(range(len(maps))))
    out = np.stack([np.ascontiguousarray(r["yT"].T) for r in res.results], axis=0)
    return out.astype(np.float32)


def kernel(**inputs):
    return run(inputs, 8192)
```

```python
import contextlib
import math
import numpy as np
import concourse.bass as bass
import concourse.mybir as mybir
from concourse.bass_utils import run_bass_kernel_spmd

F32 = mybir.dt.float32
BF16 = mybir.dt.bfloat16
AF = mybir.ActivationFunctionType
ALU = mybir.AluOpType

D = 1024
DFF = 2816
NFC = DFF // 128
CTX = 256
DEPTH = 2
EPS = 1e-6
VW = 512
ENGS = ['tensor', 'vector', 'scalar', 'gpsimd', 'sync']


class Buf:
    __slots__ = ('name', 'last_w', 'readers')

    def __init__(self, name):
        self.name = name
        self.last_w = None
        self.readers = {}


class Tile:
    def __init__(self, t, name):
        self.t = t
        self.b = Buf(name)

    def __getitem__(self, k):
        return self.t[k]


class SemState:
    def __init__(self, nc, st, n_dma_sems=26):
        self.cnt = {e: 0 for e in ENGS}
        self.known = {e: {} for e in ENGS}
        self.n_dma = n_dma_sems
        self.dma_val = [0] * n_dma_sems
        self.n_hw = n_dma_sems - 10
        self.rr_hw = 0
        self.rr_sw = 0
        self.sems = {}
        for e in ENGS:
            self.sems[e] = st.enter_context(nc.semaphore("sem_" + e))
        for k in range(n_dma_sems):
            self.sems[('dma', k)] = st.enter_context(nc.semaphore("sem_d%d" % k))


class Sched:
    def __init__(self, nc, state):
        self.nc = nc
        self.ops = {e: [] for e in ENGS}
        self.state = state

    @property
    def cnt(self):
        return self.state.cnt

    @property
    def known(self):
        return self.state.known

    @property
    def dma_val(self):
        return self.state.dma_val

    @property
    def n_dma(self):
        return self.state.n_dma

    def emit(self, engine, fn, reads=(), writes=(), dma=False, signal=True):
        need = {}

        def add(ev):
            if ev is None:
                return
            k, v = ev
            if need.get(k, 0) < v:
                need[k] = v

        for b in reads:
            add(b.last_w)
        for b in writes:
            add(b.last_w)
            for k, v in b.readers.items():
                add((k, v))
        if engine == 'tensor':
            need.pop('tensor', None)
        kd = None
        if dma:
            stt_ = self.state
            if engine == 'gpsimd':
                kd = stt_.n_hw + stt_.rr_sw
                stt_.rr_sw = (stt_.rr_sw + 1) % (stt_.n_dma - stt_.n_hw)
            else:
                kd = stt_.rr_hw
                stt_.rr_hw = (stt_.rr_hw + 1) % stt_.n_hw
            if self.dma_val[kd] > 0:
                add((('dma', kd), self.dma_val[kd]))
        kn = self.known[engine]
        waits = []
        for k, v in need.items():
            if kn.get(k, 0) >= v:
                continue
            kn[k] = v
            waits.append((k, v))
        if dma:
            self.dma_val[kd] += 16
            ev = (('dma', kd), self.dma_val[kd])
            inc = ev
        elif signal:
            self.cnt[engine] += 1
            ev = (engine, self.cnt[engine])
            inc = ev
        else:
            ev = (engine, self.cnt[engine] + 1)
            inc = None
        self.ops[engine].append((waits, fn, inc))
        for b in writes:
            b.last_w = ev
            b.readers = {}
        for b in reads:
            if b.last_w is not ev:
                if b.readers.get(ev[0], 0) < ev[1]:
                    b.readers[ev[0]] = ev[1]
        return ev

    def wait_all(self, engine):
        need = {}
        for e in ENGS:
            if e != engine and self.cnt[e] > 0:
                need[e] = self.cnt[e]
        for k in range(self.n_dma):
            if self.dma_val[k] > 0:
                need[('dma', k)] = self.dma_val[k]
        kn = self.known[engine]
        waits = [(k, v) for k, v in need.items() if kn.get(k, 0) < v]
        for k, v in waits:
            kn[k] = v
        self.ops[engine].append((waits, None, None))

    def build(self):
        nc = self.nc
        sems = self.state.sems
        with nc.Block() as block:
            def run(engname):
                def body(eng):
                    for waits, fn, inc in self.ops[engname]:
                        if fn is None:
                            for k, v in waits:
                                eng.wait_ge(sems[k], v)
                            continue
                        for k, v in waits[1:]:
                            eng.wait_ge(sems[k], v)
                        ins = fn(eng)
                        if waits:
                            k, v = waits[0]
                            ins._wait_ge(sems[k], v)
                        if inc is not None:
                            k, v = inc
                            ins.then_inc(sems[k], 16 if isinstance(k, tuple) else 1)
                return body

            block.tensor(run('tensor'))
            block.vector(run('vector'))
            block.scalar(run('scalar'))
            block.gpsimd(run('gpsimd'))
            block.sync(run('sync'))


class Phase:
    _uid = [0]
    state = None

    def __init__(self, nc, name):
        self.nc = nc
        self.name = name
        self.st = contextlib.ExitStack()
        self.S = Sched(nc, Phase.state)
        self.banks = []

    def _nm(self, name):
        Phase._uid[0] += 1
        return "%s_%s_%d" % (self.name, name, Phase._uid[0])

    def sb(self, name, shape, dt, n=1):
        out = []
        for i in range(n):
            nm = self._nm(name)
            t = self.st.enter_context(self.nc.sbuf_tensor(nm, list(shape), dt))
            out.append(Tile(t, nm))
        return out if n > 1 else out[0]

    def psum_banks(self, n=8):
        for i in range(n):
            nm = self._nm("bank")
            t = self.st.enter_context(self.nc.psum_tensor(nm, [128, 512], F32))
            self.banks.append(Tile(t, nm))
        return self.banks

    def wrap(self, t, name):
        return Tile(t, self._nm(name))

    def dma(self, out, in_, reads=(), writes=(), eng='sync'):
        self.S.emit(eng, lambda e: e.dma_start(out=out, in_=in_), reads=[r.b for r in reads],
                    writes=[w.b for w in writes], dma=True)

    def mm(self, out, lhsT, rhs, start, stop, reads, bank, signal=True, skip=False):
        if skip:
            fn = lambda e: e.matmul(out, lhsT=lhsT, rhs=rhs, start=start, stop=stop, skip_group_check=True)
        else:
            fn = lambda e: e.matmul(out, lhsT=lhsT, rhs=rhs, start=start, stop=stop)
        self.S.emit('tensor', fn, reads=[r.b for r in reads], writes=[bank.b], signal=signal)

    def op(self, eng, fn, reads=(), writes=()):
        self.S.emit(eng, fn, reads=[r.b for r in reads], writes=[w.b for w in writes])

    def act(self, out, in_, func, reads, writes, scale=None, bias=None, eng='scalar'):
        kw = {}
        if scale is not None:
            kw['scale'] = scale
        if bias is not None:
            kw['bias'] = bias
        self.op(eng, lambda e: e.activation(out=out, in_=in_, func=func, **kw), reads, writes)

    def tt(self, out, in0, in1, op, reads, writes, eng='vector'):
        self.op(eng, lambda e: e.tensor_tensor(out=out, in0=in0, in1=in1, op=op), reads, writes)

    def stt(self, out, in0, scalar, in1, op0, op1, reads, writes):
        self.op('vector', lambda e: e.scalar_tensor_tensor(out=out, in0=in0, scalar=scalar, in1=in1, op0=op0, op1=op1),
                reads, writes)

    def ts(self, out, in0, s1, s2, op0, op1, reads, writes, eng='gpsimd'):
        if op1 is None:
            self.op(eng, lambda e: e.tensor_scalar(out=out, in0=in0, scalar1=s1, scalar2=None, op0=op0), reads, writes)
        else:
            self.op(eng, lambda e: e.tensor_scalar(out=out, in0=in0, scalar1=s1, scalar2=s2, op0=op0, op1=op1), reads, writes)

    def copy(self, out, in_, reads, writes, eng='vector'):
        if eng == 'scalar':
            self.op(eng, lambda e: e.activation(out=out, in_=in_, func=AF.Copy), reads, writes)
        else:
            self.op(eng, lambda e: e.tensor_copy(out=out, in_=in_), reads, writes)

    def memset(self, ap, val, writes, eng='vector'):
        self.op(eng, lambda e: e.memset(ap, val), (), writes)

    def finish(self):
        for e in ENGS:
            self.S.wait_all(e)
        self.S.build()
        self.st.close()


def v3(ap, inner):
    return ap.rearrange("p (a b) -> p a b", b=inner)


def MOD(l, who, i):
    return ((l * 2 + who) * 6 + i) * 8
GS_BASE = DEPTH * 2 * 6 * 8
def GS(l, who, k):
    return GS_BASE + ((l * 2 + who) * 2 + k) * 8
PPW = GS_BASE + DEPTH * 2 * 2 * 8
def SM_BADA(l): return l * 48
def SM_GMIX(l): return DEPTH * 48 + l * 8
def SM_GFFN(l): return DEPTH * 48 + DEPTH * 8 + l * 8
SM_GFIN = DEPTH * 48 + 2 * DEPTH * 8
def SM_QKN(l): return SM_GFIN + 8 + l * 4
SMW = SM_GFIN + 8 + DEPTH * 4


class K:
    pass


def rstd_from_sums(ph, bank_ap, n_div, tmp, out, reads_bank, tmp_t, out_t):
    ph.act(tmp, bank_ap, AF.Ln, reads=[], writes=[reads_bank, tmp_t], scale=1.0 / n_div, bias=EPS)
    ph.act(out, tmp, AF.Exp, reads=[tmp_t], writes=[out_t], scale=-0.5)


def precast(ph, k, l, ffn=True, win=True, dep=()):
    if win and l < DEPTH:
        for r in range(0, D, 256):
            ph.dma(k.winB[l, r:r + 256, :], k.w_in_p[l, r:r + 256, :], reads=dep, eng='gpsimd')
    if ffn and l < DEPTH:
        for r in range(0, D, 256):
            ph.dma(k.woB[l, r:r + 256, :], k.w_out_p[l, r:r + 256, :], reads=dep, eng='gpsimd')
            ph.dma(k.wgB[l, r:r + 256, :], k.w_gate[l, r:r + 256, :], reads=dep, eng='gpsimd')
            ph.dma(k.wuB[l, r:r + 256, :], k.w_up[l, r:r + 256, :], reads=dep, eng='gpsimd')
        for r in range(0, DFF, 256):
            ph.dma(k.wdB[l, r:r + 256, :], k.w_down[l, r:r + 256, :], reads=dep, eng='gpsimd')


def phase_A(k, l):
    nc = k.nc
    ph = Phase(nc, "A%d" % l)
    banks = ph.psum_banks(4)
    pp = ph.wrap(k.pp, "pp")
    sm = ph.wrap(k.sm, "sm")
    cv = ph.sb("cv", [128, 16], F32)
    sc = ph.sb("sc", [128, 16], F32)
    wst = ph.sb("wst", [128, 6 * D], F32, n=2)
    wfT = ph.sb("wfT", [128, 2 * D], F32)
    bd = ph.sb("bd", [128, 256], F32)
    wfo = ph.sb("wfo", [128, 512], BF16, n=2)
    if l == 0:
        ph.dma(sm[:, :], k.small[:, :], writes=[sm])
        precast(ph, k, 0, ffn=False, win=True)
    ph.dma(cv[:, :], k.cvec[:, :], writes=[cv])
    ph.act(sc[:, :], cv[:, :], AF.Silu, reads=[cv], writes=[sc])
    b0 = banks[0]
    ph.memset(b0[:, 0:96], 0.0, writes=[b0])
    import os
    dbg = os.environ.get("KDBG", "")
    for kc in range(0 if 'nomm' in dbg else 8):
        w = wst[kc % 2]
        ph.dma(w[:, :], k.w_ada[l, kc * 128:(kc + 1) * 128, :], writes=[w])
        for cc in range(48):
            ph.mm(b0[:, cc * 2:cc * 2 + 2], w[:, cc * 128:(cc + 1) * 128], sc[:, kc * 2:kc * 2 + 2],
                  start=False, stop=(kc == 7), reads=[w, sc], bank=b0, signal=(cc == 47), skip=True)
    for who in range(2):
        src = v3(b0[:, 0:96], 2)[:, :, who]
        c0 = MOD(l, who, 0)
        ph.tt(pp[:, c0:c0 + 48], src, sm[:, SM_BADA(l):SM_BADA(l) + 48], ALU.add, reads=[sm], writes=[b0, pp])
        for kk, (i_sc, gcol) in enumerate(((1, SM_GMIX(l)), (4, SM_GFFN(l)))):
            cs = MOD(l, who, i_sc)
            cg = GS(l, who, kk)
            ph.stt(pp[:, cg:cg + 8], pp[:, cs:cs + 8], 1.0, sm[:, gcol:gcol + 8], ALU.add, ALU.mult,
                   reads=[sm], writes=[pp])
    ph.dma(v3(wfT[:, :], D), k.w_in_fT[l].rearrange("(c p) d -> p c d", p=128), writes=[wfT])
    ph.dma(bd[:, :], k.cmat[:, 0:256], writes=[bd])
    for dc in range(0 if 'nowf' in dbg else 8):
        bk = banks[1 + dc % 2]
        for t in range(2):
            for fc in range(2):
                q = t * 2 + fc
                ph.mm(bk[:, q * 128:(q + 1) * 128], wfT[:, fc * D + dc * 128: fc * D + (dc + 1) * 128],
                      bd[:, t * 128:(t + 1) * 128], start=True, stop=True, reads=[wfT, bd], bank=bk,
                      signal=(q == 3))
        o = wfo[dc % 2]
        ph.copy(o[:, :], bk[:, :], reads=[], writes=[bk, o])
        ph.dma(k.wfD[l, dc * 128:(dc + 1) * 128, :], o[:, :], reads=[o], eng='gpsimd')
    ph.finish()


CH_QA, CH_QAS, CH_KA, CH_KAS, CH_QC, CH_QCS, CH_KC, CH_KCS = 0, 3, 6, 7, 8, 11, 14, 15
TM0 = 2048
WCOLS = 2048 + 768


def phase_B(k, l):
    nc = k.nc
    S = k.S
    NT = S // 512
    ph = Phase(nc, "B%d" % l)
    banks = ph.psum_banks(8)
    pp = ph.wrap(k.pp, "pp")
    sm = ph.wrap(k.sm, "sm")
    wT = ph.sb("wT", [128, 8 * WCOLS], BF16)
    wv = v3(wT[:, :], WCOLS)
    xts = ph.sb("xt", [128, 8 * 512], F32, n=2)
    sq = ph.sb("sq", [128, 8 * 512], BF16)
    hTs = ph.sb("hT", [128, 8 * 512], BF16, n=2)
    ones = ph.sb("ones", [128, 128], BF16)
    bones = ph.sb("bones", [128, 128], BF16)
    lnt = ph.sb("lnt", [128, 512], F32)
    rstd = ph.sb("rstd", [128, 512], F32)
    rC = ph.sb("rC", [128, 512], F32, n=3)
    rS = ph.sb("rS", [128, 512], F32, n=3)
    sqh = ph.sb("sqh", [128, 512], BF16, n=2)
    lnh = ph.sb("lnh", [128, 512], F32)
    rinv = ph.sb("rinv", [128, 512], F32, n=2)
    t1s = ph.sb("t1", [128, 512], F32, n=2)
    t2s = ph.sb("t2", [128, 512], F32, n=2)
    qo = ph.sb("qo", [128, 512], BF16, n=4)
    vts = ph.sb("vt", [128, VW], BF16, n=4)
    zts = ph.sb("zt", [128, 512], BF16, n=4)

    ph.memset(ones[:, :], 1.0, writes=[ones])
    ph.memset(bones[:, :], 0.0, writes=[bones])
    ph.memset(bones[0:64, 0:64], 1.0, writes=[bones])
    ph.memset(bones[64:128, 64:128], 1.0, writes=[bones])
    for vt in vts:
        ph.memset(vt[:, :], 1.0, writes=[vt], eng='gpsimd')
    for c0 in range(0, 2304, 768):
        ph.dma(wv[:, :, c0:c0 + 768], k.winB[l, :, c0:c0 + 768].rearrange("(c p) f -> p c f", p=128), writes=[wT])
    ph.dma(wv[:, :, 2304:2816], k.wfD[l].rearrange("(c p) f -> p c f", p=128), writes=[wT])

    qk = SM_QKN(l)
    gq, gqs, gk, gks = (sm[:, qk + i:qk + i + 1] for i in range(4))
    pair_i = [0]
    qo_i = [0]
    vt_i = [0]

    tiles = [('ctx', 0)] + [('x', t) for t in range(NT)]

    def info(ti):
        kind, t = tiles[ti]
        isx = kind == 'x'
        N = 512 if isx else CTX
        off = CTX + t * 512 if isx else 0
        who = 0 if isx else 1
        need_q = isx or l == 0
        xt, hT = xts[ti % 2], hTs[ti % 2]
        return kind, t, isx, N, off, who, need_q, xtk[ti % 2], hTk[ti % 2], v3(xt[:, :], 512), v3(hT[:, :], 512)

    sv = v3(sq[:, :], 512)
    pending = []
    xtk = [[Tile(x_.t, "xk%d_%d" % (i, c)) for c in range(8)] for i, x_ in enumerate(xts)]
    hTk = [[Tile(h_.t, "hk%d_%d" % (i, c)) for c in range(8)] for i, h_ in enumerate(hTs)]

    def pre(ti):
        kind, t, isx, N, off, who, need_q, xt, hT, xv, hv = info(ti)
        if isx:
            src = (k.xT if l == 0 else k.xD1)[:, t * 512:(t + 1) * 512]
        else:
            src = (k.ctxT if l == 0 else k.ctxD1)[:, :]
        ph.dma(xv[:, :, 0:N], src.rearrange("(c p) t -> p c t", p=128), writes=xt)
        if isx:
            c_t, s_t = rC[t % 3], rS[t % 3]
            ph.dma(c_t[:, :], k.ropeC[:, t * 512:(t + 1) * 512], writes=[c_t])
            ph.dma(s_t[:, :], k.ropeS[:, t * 512:(t + 1) * 512], writes=[s_t])

    def chain1(ti):
        kind, t, isx, N, off, who, need_q, xt, hT, xv, hv = info(ti)
        ph.act(sv[:, :, 0:N], xv[:, :, 0:N], AF.Square, reads=xt, writes=[sq])

    def chain2(ti):
        kind, t, isx, N, off, who, need_q, xt, hT, xv, hv = info(ti)
        bS = banks[0]
        for kc in range(8):
            ph.mm(bS[:, 0:N], ones[:, :], sv[:, kc, 0:N], start=(kc == 0), stop=(kc == 7), reads=[ones, sq],
                  bank=bS, signal=(kc == 7))

    def chain3(ti):
        kind, t, isx, N, off, who, need_q, xt, hT, xv, hv = info(ti)
        bS = banks[0]
        rstd_from_sums(ph, bS[:, 0:N], D, lnt[:, 0:N], rstd[:, 0:N], bS, lnt, rstd)
        g0 = GS(l, who, 0)
        s0 = MOD(l, who, 0)
        for kc in range(8):
            ph.tt(xv[:, kc, 0:N], xv[:, kc, 0:N], rstd[:, 0:N], ALU.mult, reads=[rstd], writes=[xt[kc]])
            if kc % 3 == 2:
                ph.act(hv[:, kc, 0:N], xv[:, kc, 0:N], AF.Identity, reads=[xt[kc], pp], writes=[hT[kc]],
                       scale=pp[:, g0 + kc:g0 + kc + 1], bias=pp[:, s0 + kc:s0 + kc + 1])
            else:
                ph.ts(hv[:, kc, 0:N], xv[:, kc, 0:N], pp[:, g0 + kc:g0 + kc + 1], pp[:, s0 + kc:s0 + kc + 1],
                      ALU.mult, ALU.add, reads=[xt[kc], pp], writes=[hT[kc]], eng='gpsimd')

    def work(ti, part):
        kind, t, isx, N, off, who, need_q, xt, hT, xv, hv = info(ti)
        if isx:
            c_t, s_t = rC[t % 3], rS[t % 3]

        def proj(ch, bank):
            for kc in range(8):
                ph.mm(bank[:, 0:N], wv[:, kc, ch * 128:(ch + 1) * 128], hv[:, kc, 0:N], start=(kc == 0),
                      stop=(kc == 7), reads=[wT, hT[kc]], bank=bank, signal=(kc == 7))

        def out_tile():
            o = qo[qo_i[0] % 4]
            qo_i[0] += 1
            return o

        jobs = []
        for j in range(3):
            if need_q:
                jobs.append((CH_QA + j, CH_QAS + j, True, gq, gqs, k.qaD[j, :, off:off + N]))
        jobs.append((CH_KA, CH_KAS, True, gk, gks, k.kaD[:, off:off + N]))
        for j in range(3):
            if need_q:
                jobs.append((CH_QC + j, CH_QCS + j, False, None, None, k.qcD[j, :, off:off + N]))
        jobs.append((CH_KC, CH_KCS, False, None, None, k.kcD[:, off:off + N]))
        nsplit = 3 if len(jobs) > 3 else 1
        jobs = jobs[:nsplit] if part == 0 else jobs[nsplit:]
        for (ch, chs, normed, g, gs_, dst) in jobs:
            pi = pair_i[0]
            pair_i[0] += 1
            b1 = banks[1 + 2 * (pi % 2)]
            b2 = banks[2 + 2 * (pi % 2)]
            bN = banks[5]
            proj(ch, b1)
            if isx:
                proj(chs, b2)
            o = out_tile()
            t1 = t1s[pi % 2]
            t2 = t2s[pi % 2]
            sh_ = sqh[pi % 2]
            ri = rinv[pi % 2]
            if normed:
                ph.act(sh_[:, 0:N], b1[:, 0:N], AF.Square, reads=[], writes=[b1, sh_])

            def back(normed=normed, g=g, gs_=gs_, dst=dst, b1=b1, b2=b2, bN=bN, o=o, t1=t1, t2=t2, sh_=sh_, ri=ri,
                     isx=isx, N=N):
                if normed:
                    ph.mm(bN[:, 0:N], bones[:, :], sh_[:, 0:N], start=True, stop=True, reads=[bones, sh_], bank=bN)
                    rstd_from_sums(ph, bN[:, 0:N], 64, lnh[:, 0:N], ri[:, 0:N], bN, lnh, ri)
                    if isx:
                        ph.stt(t1[:, 0:N], b1[:, 0:N], g, c_t[:, 0:N], ALU.mult, ALU.mult, reads=[sm, c_t], writes=[b1, t1])
                        ph.stt(t2[:, 0:N], b2[:, 0:N], gs_, s_t[:, 0:N], ALU.mult, ALU.mult, reads=[sm, s_t], writes=[b2, t2])
                        ph.tt(t1[:, 0:N], t1[:, 0:N], t2[:, 0:N], ALU.add, reads=[t2], writes=[t1], eng='gpsimd')
                        ph.tt(o[:, 0:N], t1[:, 0:N], ri[:, 0:N], ALU.mult, reads=[t1, ri], writes=[o])
                    else:
                        ph.stt(o[:, 0:N], b1[:, 0:N], g, ri[:, 0:N], ALU.mult, ALU.mult, reads=[sm, ri], writes=[b1, o])
                else:
                    if isx:
                        ph.tt(t1[:, 0:N], b1[:, 0:N], c_t[:, 0:N], ALU.mult, reads=[c_t], writes=[b1, t1])
                        ph.tt(t2[:, 0:N], b2[:, 0:N], s_t[:, 0:N], ALU.mult, reads=[s_t], writes=[b2, t2])
                        ph.tt(o[:, 0:N], t1[:, 0:N], t2[:, 0:N], ALU.add, reads=[t1, t2], writes=[o], eng='gpsimd')
                    else:
                        ph.act(o[:, 0:N], b1[:, 0:N], AF.Copy, reads=[], writes=[b1, o])
                ph.dma(dst, o[:, 0:N], reads=[o], eng='sync')

            if pending:
                pending.pop(0)()
            pending.append(back)
        if part == 0:
            return
        while pending:
            pending.pop(0)()
        for s_ in range(N // 128):
            b6, b7 = banks[6], banks[7]
            ncol = 768 if need_q else 256
            for kc in range(8):
                lw = hv[:, kc, s_ * 128:(s_ + 1) * 128]
                ph.mm(b6[:, 0:min(ncol, 512)], lw, wv[:, kc, TM0:TM0 + min(ncol, 512)], start=(kc == 0), stop=(kc == 7),
                      reads=[wT, hT[kc]], bank=b6, signal=(kc == 7 and not need_q))
                if need_q:
                    ph.mm(b7[:, 0:256], lw, wv[:, kc, TM0 + 512:TM0 + 768], start=(kc == 0), stop=(kc == 7),
                          reads=[wT, hT[kc]], bank=b7, signal=(kc == 7))
            vt = vts[vt_i[0] % 4]
            zt = zts[vt_i[0] % 4]
            vt_i[0] += 1
            vin = b6[:, 0:256].rearrange("p (a g d) -> p a g d", a=2, g=2)
            vout = vt[:, :].rearrange("p (a w) -> p a w", a=2)
            ph.copy(vout[:, :, 0:64], vin[:, :, 0, :], reads=[], writes=[b6, vt], eng='scalar')
            ph.copy(vout[:, :, 192:256], vin[:, :, 1, :], reads=[], writes=[b6, vt], eng='scalar')
            ph.dma(k.vD[off + s_ * 128: off + (s_ + 1) * 128, :], vt[:, :], reads=[vt], eng='sync')
            if need_q:
                ph.copy(zt[:, 0:256], b6[:, 256:512], reads=[], writes=[b6, zt], eng='scalar')
                ph.copy(zt[:, 256:512], b7[:, 0:256], reads=[], writes=[b7, zt], eng='scalar')
                zd = k.zD if isx else k.zDc
                r0 = (t * 512 if isx else 0) + s_ * 128
                ph.dma(zd[:, r0:r0 + 128, :].rearrange("r p j -> p r j"), v3(zt[:, :], 256), reads=[zt], eng='sync')

    NTL = len(tiles)
    pre(0)
    pre(1)
    chain1(0)
    chain2(0)
    chain3(0)
    for ti in range(NTL):
        if ti + 1 < NTL:
            chain1(ti + 1)
        work(ti, 0)
        if ti + 1 < NTL:
            chain2(ti + 1)
            chain3(ti + 1)
        if ti + 2 < NTL:
            pre(ti + 2)
        work(ti, 1)
    ph.finish()


def attn_epilogue(ph, bO, bR, half, N, oe, rr, onesf, selhi, dst_ap):
    if half == 0:
        ph.copy(oe[0:65, 0:N], bO[0:65, 0:N], reads=[], writes=[bO, oe])
        ph.op('vector', lambda e: e.reciprocal(out=rr[64:65, 0:N], in_=oe[64:65, 0:N]), reads=[oe], writes=[rr])
        ph.mm(bR[0:64, 0:N], onesf[64:65, 0:64], rr[64:65, 0:N], start=True, stop=True, reads=[onesf, rr], bank=bR)
        ph.tt(dst_ap, oe[0:64, 0:N] if dst_ap.ndim == 2 else v3(oe[0:64, 0:N], 128),
              bR[0:64, 0:N] if dst_ap.ndim == 2 else v3(bR[0:64, 0:N], 128), ALU.mult, reads=[oe], writes=[bR, ph._dst])
    else:
        ph.copy(oe[:, 0:N], bO[:, 0:N], reads=[], writes=[bO, oe])
        ph.op('vector', lambda e: e.reciprocal(out=rr[0:1, 0:N], in_=oe[0:1, 0:N]), reads=[oe], writes=[rr])
        ph.mm(bR[:, 0:N], selhi[0:1, 0:128], rr[0:1, 0:N], start=True, stop=True, reads=[selhi, rr], bank=bR)
        ph.tt(dst_ap, oe[64:128, 0:N] if dst_ap.ndim == 2 else v3(oe[64:128, 0:N], 128),
              bR[64:128, 0:N] if dst_ap.ndim == 2 else v3(bR[64:128, 0:N], 128), ALU.mult, reads=[oe], writes=[bR, ph._dst])


def attn_consts(ph):
    onesf = ph.sb("onesf", [128, 128], F32)
    selhi = ph.sb("selhi", [128, 128], F32)
    ph.memset(onesf[:, :], 1.0, writes=[onesf])
    ph.memset(selhi[:, :], 0.0, writes=[selhi])
    ph.memset(selhi[0:1, 64:128], 1.0, writes=[selhi])
    return onesf, selhi


def phase_C(k, l):
    nc = k.nc
    S = k.S
    NT = S // 512
    NKC = 2 + S // 128
    ph = Phase(nc, "C%d" % l)
    pairs = []
    for i in range(4):
        nm = ph._nm("pp")
        pairs.append(ph.st.enter_context(nc.psum_tensor(nm, [128, 1024], F32)))
    NST, LA = 3, 2
    ST = [Tile(pairs[0], "st0"), Tile(pairs[1], "st1"), Tile(pairs[2], "st2")]
    bO = [Tile(pairs[3], "bO0"), Tile(pairs[3], "bO1")]
    kT = ph.sb("kT", [128, CTX + S], BF16)
    vA = ph.sb("vA", [128, NKC * 256], BF16)
    vAv = v3(vA[:, :], 256)
    qts = ph.sb("qt", [128, 3 * 512], BF16, n=2)
    pts = ph.sb("pt", [128, 1024], BF16, n=4)
    oes = ph.sb("oe", [128, 1024], F32, n=2)
    rrs = ph.sb("rr", [128, 1024], F32, n=2)
    ots = ph.sb("ot", [128, 3 * 512], BF16, n=2)
    kTk, vAk = [], []
    for c0 in range(0, NKC, 8):
        c1 = min(NKC, c0 + 8)
        kTk.append(Tile(kT.t, "kTk%d" % c0))
        vAk.append(Tile(vA.t, "vAk%d" % c0))
        ph.dma(kT[:, c0 * 128:c1 * 128], k.kaD[:, c0 * 128:c1 * 128], writes=[kTk[-1]])
        ph.dma(vAv[:, c0:c1, :], k.vD[c0 * 128:c1 * 128, 0:256].rearrange("(c p) f -> p c f", p=128), writes=[vAk[-1]])

    tiles = ([('ctx', 0)] if l == 0 else []) + [('x', t) for t in range(NT)]

    def tile_info(ti):
        kind, t = tiles[ti]
        N = 512 if kind == 'x' else CTX
        off = CTX + t * 512 if kind == 'x' else 0
        return N, off

    def load_q(ti):
        N, off = tile_info(ti)
        qt = qts[ti % 2]
        ph.dma(v3(qt[:, :], 512)[:, :, 0:N], k.qaD[:, :, off:off + N].rearrange("c p t -> p c t"), writes=[qt])

    steps = []
    for ti in range(len(tiles)):
        nkc = 2 if tiles[ti][0] == 'ctx' else NKC
        for j in range(3):
            for c in range(nkc):
                steps.append((ti, j, c, nkc))

    def emit_qk(si):
        ti, j, c, nkc = steps[si]
        N, off = tile_info(ti)
        qt = qts[ti % 2]
        st = ST[si % NST]
        qv = v3(qt[:, :], 512)
        ph.mm(st[:, 0:N], kT[0:64, c * 128:(c + 1) * 128], qv[0:64, j, 0:N], start=True, stop=True,
              reads=[kTk[c // 8], qt], bank=st, signal=False)
        ph.mm(st[:, 512:512 + N], kT[64:128, c * 128:(c + 1) * 128], qv[64:128, j, 0:N], start=True, stop=True,
              reads=[kTk[c // 8], qt], bank=st, signal=True)

    deferred = []
    pctok = Tile(None, 'pctok')
    load_q(0)
    if len(tiles) > 1:
        load_q(1)
    for s0_ in range(min(LA, len(steps))):
        emit_qk(s0_)
    grp = 0
    for si, (ti, j, c, nkc) in enumerate(steps):
        N, off = tile_info(ti)
        if c == 0 and j == 0 and ti >= 1 and ti + 1 < len(tiles):
            load_q(ti + 1)
        if si + LA < len(steps):
            emit_qk(si + LA)
        st = ST[si % NST]
        pt = pts[si % 4]
        ph.act(v3(pt[:, :], 512)[:, :, 0:N], v3(st[:, :], 512)[:, :, 0:N], AF.Exp, reads=[], writes=[st, pt], scale=0.125)
        ph.mm(bO[0][:, 0:N], vAv[:, c, 0:128], pt[:, 0:N], start=(c == 0), stop=(c == nkc - 1),
              reads=[vAk[c // 8], pt], bank=bO[0], signal=False)
        ph.mm(bO[1][:, 512:512 + N], vAv[:, c, 128:256], pt[:, 512:512 + N], start=(c == 0), stop=(c == nkc - 1),
              reads=[vAk[c // 8], pt], bank=bO[1], signal=True)
        if c == nkc - 1:
            ot = ots[ti % 2]
            oe = oes[grp % 2]
            rr = rrs[grp % 2]
            grp += 1
            ov = v3(ot[:, :], 512)

            ph.copy(oe[:, 0:N], bO[0][:, 0:N], reads=[], writes=[bO[0], oe], eng='vector')
            ph.copy(oe[:, 512:512 + N], bO[1][:, 512:512 + N], reads=[], writes=[bO[1], oe])
            ph.op('vector', lambda e, rr=rr, oe=oe, N_=N: e.reciprocal(out=rr[0:64, 0:N_], in_=oe[64:128, 0:N_]),
                  reads=[oe], writes=[rr])
            ph.op('vector', lambda e, rr=rr, oe=oe, N_=N: e.reciprocal(out=rr[64:128, 512:512 + N_], in_=oe[0:64, 512:512 + N_]),
                  reads=[oe], writes=[rr])
            ph.tt(ov[0:64, j, 0:N], oe[0:64, 0:N], rr[0:64, 0:N], ALU.mult, reads=[oe, rr], writes=[ot])
            trig = grp == (4 if l == 0 else 1)
            ph.tt(ov[64:128, j, 0:N], oe[64:128, 512:512 + N], rr[64:128, 512:512 + N], ALU.mult, reads=[oe, rr],
                  writes=[ot, pctok] if trig else [ot])
            if j == 2:
                ph.dma(k.oD[0:3, :, off:off + N].rearrange("c p t -> p c t"), ov[:, :, 0:N], reads=[ot], eng='sync')
            if trig:
                precast(ph, k, l, ffn=True, win=False, dep=[pctok])
                precast(ph, k, l + 1, ffn=False, win=True, dep=[pctok])
    for _, fn in deferred:
        fn()
    ph.finish()


def phase_D(k, l):
    nc = k.nc
    S = k.S
    NT = S // 512
    NQB = S // 128
    NKC = 2 + NQB
    ph = Phase(nc, "D%d" % l)
    pairs = []
    for i in range(4):
        nm = ph._nm("pp")
        pairs.append(ph.st.enter_context(nc.psum_tensor(nm, [128, 1024], F32)))
    NST, LA = 3, 2
    ST = [Tile(pairs[0], "st0"), Tile(pairs[1], "st1"), Tile(pairs[2], "st2")]
    bO = [Tile(pairs[3], "bO0"), Tile(pairs[3], "bO1")]
    kT = ph.sb("kT", [128, CTX + S], BF16)
    vC = ph.sb("vC", [128, NKC * 256], BF16)
    vCv = v3(vC[:, :], 256)
    qts = ph.sb("qt", [128, 3 * 512], BF16, n=2)
    pts = ph.sb("pt", [128, 1024], BF16, n=4)
    NR = 3
    oes = ph.sb("oe", [128, 1024], F32, n=NR)
    rrs = ph.sb("rr", [128, 1024], F32, n=NR)
    ots = ph.sb("ot", [128, 3 * 512], BF16, n=2)
    msk = ph.sb("msk", [128, 256], BF16)
    sk32 = ph.sb("sk32", [1, 6 * 256], F32)
    skb = ph.sb("skb", [1, 6 * 256], BF16)
    esel = ph.sb("esel", [1, 256], BF16)
    kTk, vCk = [], []
    for c0 in range(0, NKC, 8):
        c1 = min(NKC, c0 + 8)
        kTk.append(Tile(kT.t, "kTk%d" % c0))
        vCk.append(Tile(vC.t, "vCk%d" % c0))
        ph.dma(kT[:, c0 * 128:c1 * 128], k.kcD[:, c0 * 128:c1 * 128], writes=[kTk[-1]])
        ph.dma(vCv[:, c0:c1, :], k.vD[c0 * 128:c1 * 128, 256:512].rearrange("(c p) f -> p c f", p=128), writes=[vCk[-1]])
    ph.dma(msk[:, :], k.cbf[:, CB_MASK:CB_MASK + 256], writes=[msk], eng='gpsimd')
    ph.dma(sk32[:, :], k.sinkrow[l, :, :], writes=[sk32])
    ph.act(skb[:, :], sk32[:, :], AF.Exp, reads=[sk32], writes=[skb])
    ph.memset(esel[:, :], 0.0, writes=[esel])
    ph.memset(esel[0:1, 64:128], 1.0, writes=[esel])
    ph.memset(esel[0:1, 128:192], 1.0, writes=[esel])
    skv = v3(skb[:, :], 256)
    mv = v3(msk[:, :], 128)

    tiles = ([('ctx', 0)] if l == 0 else []) + [('x', t) for t in range(NT)]

    def tile_info(ti):
        kind, t = tiles[ti]
        N = 512 if kind == 'x' else CTX
        off = CTX + t * 512 if kind == 'x' else 0
        return kind, t, N, off

    def load_q(ti):
        kind, t, N, off = tile_info(ti)
        qt = qts[ti % 2]
        ph.dma(v3(qt[:, :], 512)[:, :, 0:N], k.qcD[:, :, off:off + N].rearrange("c p t -> p c t"), writes=[qt])

    groups = []
    for ti in range(len(tiles)):
        kind, t, N, off = tile_info(ti)
        qt, ot = qts[ti % 2], ots[ti % 2]
        qv, ov = v3(qt[:, :], 512), v3(ot[:, :], 512)
        if kind == 'ctx':
            for j in range(3):
                g = dict(ti=ti, NN=CTX, n3=1, nq=CTX, chunks=[(0, 0, None), (128, 1, None)],
                         q=[qv[0:64, j, 0:CTX], qv[64:128, j, 0:CTX]],
                         sink=[skv[0:1, j, 0:CTX], skv[0:1, 3 + j, 0:CTX]],
                         dst=[ov[0:64, j, 0:CTX], ov[64:128, j, 0:CTX]], last=(j == 2))
                groups.append(g)
        else:
            for nb in range(4):
                n = t * 4 + nb
                chunks = [(0, 0, None), (128, 1, None)]
                if n - 1 >= 0:
                    chunks.append((CTX + (n - 1) * 128, 2 + n - 1, 0))
                chunks.append((CTX + n * 128, 2 + n, None))
                if n + 1 < NQB:
                    chunks.append((CTX + (n + 1) * 128, 2 + n + 1, 1))
                cs = slice(nb * 128, (nb + 1) * 128)
                g = dict(ti=ti, NN=384, n3=3, nq=128, chunks=chunks,
                         q=[qv[0:64, :, cs], qv[64:128, :, cs]],
                         sink=[skv[0:1, 0:3, 0:128], skv[0:1, 3:6, 0:128]],
                         dst=[ov[0:64, :, cs], ov[64:128, :, cs]], last=(nb == 3))
                groups.append(g)
    steps = []
    for gi, g in enumerate(groups):
        for ci in range(len(g['chunks'])):
            steps.append((gi, ci))

    def emit_qk(si):
        gi, ci = steps[si]
        g = groups[gi]
        NN = g['NN']
        kcol = g['chunks'][ci][0]
        st = ST[si % NST]
        qt = qts[g['ti'] % 2]
        kt_ = kTk[(kcol // 128) // 8]
        ph.mm(st[:, 0:NN], kT[0:64, kcol:kcol + 128], g['q'][0], start=True, stop=True, reads=[kt_, qt], bank=st, signal=False)
        ph.mm(st[:, 512:512 + NN], kT[64:128, kcol:kcol + 128], g['q'][1], start=True, stop=True, reads=[kt_, qt], bank=st,
              signal=True)

    deferred = []
    load_q(0)
    if len(tiles) > 1:
        load_q(1)
    for s0_ in range(min(LA, len(steps))):
        emit_qk(s0_)
    loaded = {0, 1}
    for si, (gi, ci) in enumerate(steps):
        g = groups[gi]
        NN, n3, nq, ti = g['NN'], g['n3'], g['nq'], g['ti']
        kind, t, N, off = tile_info(ti)
        nch = len(g['chunks'])
        kcol, vci, mi = g['chunks'][ci]
        if ci == 0 and ti + 1 < len(tiles) and (ti + 1) not in loaded and (gi == 0 or groups[gi - 1]['ti'] != ti):
            loaded.add(ti + 1)
            load_q(ti + 1)
        if si + LA < len(steps):
            emit_qk(si + LA)
        st = ST[si % NST]
        pt = pts[si % 4]
        ph.act(v3(pt[:, :], 512)[:, :, 0:NN], v3(st[:, :], 512)[:, :, 0:NN], AF.Exp, reads=[], writes=[st, pt], scale=0.125)
        if mi is not None:
            pv4 = pt[:, :].rearrange("p (h a q) -> p h a q", h=2, a=4)[:, :, 0:3, :]
            mb = mv[:, mi, :].unsqueeze(1).unsqueeze(1).to_broadcast([128, 2, 3, 128])
            ph.tt(pv4, pv4, mb, ALU.mult, reads=[msk], writes=[pt], eng='gpsimd')
        ph.mm(bO[0][:, 0:NN], vCv[:, vci, 0:128], pt[:, 0:NN], start=(ci == 0), stop=False,
              reads=[vCk[vci // 8], pt], bank=bO[0], signal=False)
        ph.mm(bO[1][:, 512:512 + NN], vCv[:, vci, 128:256], pt[:, 512:512 + NN], start=(ci == 0), stop=False,
              reads=[vCk[vci // 8], pt], bank=bO[1], signal=(ci < nch - 1))
        if ci == nch - 1:
            ph.mm(bO[0][:, 0:NN], esel[0:1, 0:128], g['sink'][0], start=False, stop=True, reads=[esel, skb], bank=bO[0],
                  signal=False)
            ph.mm(bO[1][:, 512:512 + NN], esel[0:1, 128:256], g['sink'][1], start=False, stop=True, reads=[esel, skb],
                  bank=bO[1], signal=True)
            oe, rr = oes[gi % NR], rrs[gi % NR]
            ot = ots[ti % 2]
            ph.copy(oe[:, 0:NN], bO[0][:, 0:NN], reads=[], writes=[bO[0], oe], eng='scalar')
            ph.copy(oe[:, 512:512 + NN], bO[1][:, 512:512 + NN], reads=[], writes=[bO[1], oe], eng='scalar')
            ph.op('vector', lambda e, rr=rr, oe=oe, N_=NN: e.reciprocal(out=rr[0:64, 0:N_], in_=oe[64:128, 0:N_]),
                  reads=[oe], writes=[rr])
            ph.op('vector', lambda e, rr=rr, oe=oe, N_=NN: e.reciprocal(out=rr[64:128, 512:512 + N_], in_=oe[0:64, 512:512 + N_]),
                  reads=[oe], writes=[rr])

            def vw(ap, g=g, nq=nq):
                return ap if g['n3'] == 1 else v3(ap, nq)
            ph.tt(g['dst'][0], vw(oe[0:64, 0:NN]), vw(rr[0:64, 0:NN]), ALU.mult, reads=[oe, rr], writes=[ot])
            ph.tt(g['dst'][1], vw(oe[64:128, 512:512 + NN]), vw(rr[64:128, 512:512 + NN]), ALU.mult, reads=[oe, rr], writes=[ot])
            if g['last']:
                ov = v3(ot[:, :], 512)
                ph.dma(k.oD[5:8, :, off:off + N].rearrange("c p t -> p c t"), ov[:, :, 0:N], reads=[ot], eng='sync')
    for _, fn in deferred:
        fn()
    ph.finish()


CB_M1, CB_M2, CB_C128, CB_S128, CB_C256, CB_S256, CB_MASK = 0, 128, 256, 384, 512, 1024, 1536
CBW = 1536 + 256


def phase_E(k, l):
    nc = k.nc
    S = k.S
    L1 = S // 128
    P2 = 2 * L1
    ph = Phase(nc, "E%d" % l)
    banks = ph.psum_banks(8)
    cb = ph.sb("cb", [128, CBW], BF16)
    tw = ph.sb("tw", [128, 256], F32)
    ph.dma(cb[:, :], k.cbf[:, :], writes=[cb], eng='gpsimd')
    ph.dma(tw[:, :], k.cmat[:, 256:512], writes=[tw])
    SL = 32
    vs = ph.sb("v", [128, SL * 256], BF16, n=2)
    Zs = ph.sb("Zs", [128, SL * 256], BF16, n=2)
    t1s = ph.sb("t1", [128, 512], F32, n=2)
    t2s = ph.sb("t2", [128, 512], F32, n=2)
    Zt = ph.sb("Zt", [128, 2 * L1 * 256], BF16)
    ofs = ph.sb("of", [128, S], BF16, n=2)
    zdv = k.zD.rearrange("r (a b) j -> r a (b j)", b=128)
    ZDv = k.ZD.rearrange("r a b j -> (r a) (b j)")
    g = 0
    zdb = [Tile(None, "ZDslab%d" % i) for i in range(128 // SL)]
    for sl in range(128 // SL):
        v = vs[sl % 2]
        Zo = Zs[sl % 2]
        for r in range(2):
            ph.dma(v[r * L1:(r + 1) * L1, :], zdv[r, :, sl * SL * 256:(sl + 1) * SL * 256], writes=[v])
        for cg in range(SL // 2):
            cols = slice(cg * 512, (cg + 1) * 512)
            l2a = sl * SL + cg * 2
            bY, bW = banks[(g % 2) * 2], banks[(g % 2) * 2 + 1]
            t1, t2 = t1s[g % 2], t2s[g % 2]
            g += 1
            ph.mm(bY[0:P2, :], cb[0:P2, CB_M1:CB_M1 + P2], v[0:P2, cols], start=True, stop=True, reads=[cb, v], bank=bY)
            ph.mm(bW[0:P2, :], cb[0:P2, CB_M2:CB_M2 + P2], v[0:P2, cols], start=True, stop=True, reads=[cb, v], bank=bW)
            ph.tt(v3(t1[0:P2, :], 256), v3(bY[0:P2, :], 256), tw[0:P2, l2a:l2a + 2].unsqueeze(2).to_broadcast([P2, 2, 256]),
                  ALU.mult, reads=[tw], writes=[bY, t1])
            ph.tt(v3(t2[0:P2, :], 256), v3(bW[0:P2, :], 256),
                  tw[0:P2, 128 + l2a:128 + l2a + 2].unsqueeze(2).to_broadcast([P2, 2, 256]),
                  ALU.mult, reads=[tw], writes=[bW, t2])
            ph.tt(Zo[0:P2, cols], t1[0:P2, :], t2[0:P2, :], ALU.add, reads=[t1, t2], writes=[Zo], eng='gpsimd')
        ph.dma(ZDv[:, sl * SL * 256:(sl + 1) * SL * 256], Zo[0:P2, :], reads=[Zo], writes=[zdb[sl]], eng='gpsimd')
    Zv = Zt[:, :].rearrange("p (r a j) -> p r a j", r=2, a=L1)
    KG = min(16, L1)
    ztk = {}
    for a0 in range(0, L1, KG):
        for r in range(2):
            ztk[(r, a0)] = Tile(Zt.t, "zt%d_%d" % (r, a0))
            ph.dma(Zv[:, r, a0:a0 + KG, :], k.ZD[r, a0:a0 + KG, :, :].rearrange("a b j -> b a j"), reads=zdb,
                   writes=[ztk[(r, a0)]])
    scale = 1.0 / math.sqrt(S * 64.0)
    gi = 0
    for a0 in range(0, L1, 4):
        for jc in range(2):
            bk = banks[4 + gi % 4]
            for q in range(4):
                a = a0 + q
                ph.mm(bk[:, q * 128:(q + 1) * 128], Zv[:, 0, a, jc * 128:(jc + 1) * 128], cb[:, CB_C128:CB_C128 + 128],
                      start=True, stop=False, reads=[ztk[(0, (a // KG) * KG)], cb], bank=bk, signal=False)
                ph.mm(bk[:, q * 128:(q + 1) * 128], Zv[:, 1, a, jc * 128:(jc + 1) * 128], cb[:, CB_S128:CB_S128 + 128],
                      start=False, stop=True, reads=[ztk[(1, (a // KG) * KG)], cb], bank=bk, signal=(q == 3))
            of = ofs[jc]
            outv = of[:, :].rearrange("p (b a) -> p b a", a=L1)[:, :, a0:a0 + 4]
            inv = bk[:, :].rearrange("p (q b) -> p b q", q=4)
            if gi % 2 == 0:
                ph.act(outv, inv, AF.Copy, reads=[], writes=[bk, of], scale=scale)
            else:
                ph.op('vector', lambda e, outv=outv, inv=inv: e.tensor_scalar(out=outv, in0=inv, scalar1=scale, scalar2=None,
                                                                             op0=ALU.mult), reads=[], writes=[bk, of])
            gi += 1
    for jc in range(2):
        ph.dma(k.oD[3 + jc, :, CTX:CTX + S], ofs[jc][:, :], reads=[ofs[jc]], eng='gpsimd')
    if l == 0:
        zc = ph.sb("zc", [128, 2 * 2 * 256], BF16)
        zcv = zc[:, :].rearrange("p (c r j) -> p c r j", c=2, r=2)
        ofc = ph.sb("ofc", [128, 2 * 256], BF16)
        for c in range(2):
            ph.dma(zcv[:, c, :, :], k.zDc[:, c * 128:(c + 1) * 128, :].rearrange("r p j -> p r j"), writes=[zc])
        sc_c = 1.0 / math.sqrt(CTX * 64.0)
        for jc in range(2):
            bk = banks[jc]
            n = 0
            for c in range(2):
                for r in range(2):
                    base = (CB_C256 if r == 0 else CB_S256) + c * 256
                    ph.mm(bk[:, 0:256], zcv[:, c, r, jc * 128:(jc + 1) * 128], cb[:, base:base + 256],
                          start=(n == 0), stop=(n == 3), reads=[zc, cb], bank=bk, signal=(n == 3))
                    n += 1
            ph.act(ofc[:, jc * 256:(jc + 1) * 256], bk[:, 0:256], AF.Copy, reads=[], writes=[bk, ofc], scale=sc_c)
            ph.dma(k.oD[3 + jc, :, 0:CTX], ofc[:, jc * 256:(jc + 1) * 256], reads=[ofc], eng='gpsimd')
    ph.finish()


def phase_G(k, l):
    nc = k.nc
    S = k.S
    N = 256
    last = (l == DEPTH - 1)
    ph = Phase(nc, "G%d" % l)
    banks = ph.psum_banks(8)
    pp = ph.wrap(k.pp, "pp")
    sm = ph.wrap(k.sm, "sm")
    wo = ph.sb("wo", [128, 8 * D], BF16)
    wg = ph.sb("wg", [128, 8 * DFF], BF16)
    wu = ph.sb("wu", [128, 8 * DFF], BF16)
    wd = ph.sb("wd", [128, NFC * D], BF16)
    wov, wgv, wuv, wdv = v3(wo[:, :], D), v3(wg[:, :], DFF), v3(wu[:, :], DFF), v3(wd[:, :], D)
    xts = ph.sb("xt", [128, 8 * N], F32, n=2)
    ots = ph.sb("ot", [128, 8 * N], BF16, n=2)
    xn = ph.sb("xn", [128, 8 * N], F32)
    hh = ph.sb("hh", [128, 8 * N], BF16)
    sq = hh
    aa = ph.sb("aa", [128, NFC * N], BF16)
    sgs = ph.sb("sg", [128, N], F32, n=2)
    ones = ph.sb("ones", [128, 128], BF16)
    lnt = ph.sb("lnt", [128, N], F32)
    rstd = ph.sb("rstd", [128, N], F32)
    ph.memset(ones[:, :], 1.0, writes=[ones])
    ph.dma(wov, k.woB[l].rearrange("(c p) f -> p c f", p=128), writes=[wo])
    early_load = [True]
    NWG = 4
    WGC = DFF // NWG
    wgs = [Tile(wg.t, "wg%d" % i) for i in range(NWG)]
    wus = [Tile(wu.t, "wu%d" % i) for i in range(NWG)]

    def emit_big_weights():
        for gI in range(NWG):
            cs = slice(gI * WGC, (gI + 1) * WGC)
            ph.dma(wgv[:, :, cs], k.wgB[l, :, cs].rearrange("(c p) f -> p c f", p=128), writes=[wgs[gI]])
            ph.dma(wuv[:, :, cs], k.wuB[l, :, cs].rearrange("(c p) f -> p c f", p=128), writes=[wus[gI]])
        for c0 in range(0, NFC, 11):
            ph.dma(wdv[:, c0:c0 + 11, :], k.wdB[l, c0 * 128:(c0 + 11) * 128, :].rearrange("(c p) f -> p c f", p=128),
                   writes=[wd])

    tiles = [('x', t) for t in range(S // N)] + ([('ctx', 0)] if not last else [])
    lnf = ph.sb("lnf", [128, N], F32)
    rstdf = ph.sb("rstdf", [128, N], F32)
    hv = v3(hh[:, :], N)
    nv = v3(xn[:, :], N)
    av = v3(aa[:, :], N)
    sv = v3(sq[:, :], N)
    bi = [0]
    xnk = [Tile(xn.t, "xnk%d" % c) for c in range(8)]
    hhk = [Tile(hh.t, "hhk%d" % c) for c in range(8)]

    def info(ti):
        kind, t = tiles[ti]
        isx = kind == 'x'
        who = 0 if isx else 1
        xt, ot = xts[ti % 2], ots[ti % 2]
        return kind, t, isx, who, xt, ot, v3(xt[:, :], N), v3(ot[:, :], N)

    def load(ti):
        kind, t, isx, who, xt, ot, xv, ov = info(ti)
        off = CTX + t * N if isx else 0
        if isx:
            src = (k.xT if l == 0 else k.xD1)[:, t * N:(t + 1) * N]
        else:
            src = k.ctxT[:, :]
        ph.dma(xv, src.rearrange("(c p) t -> p c t", p=128), writes=[xt])
        ph.dma(ov, k.oD[:, :, off:off + N].rearrange("c p t -> p c t"), writes=[ot])

    def st_A(ti):
        kind, t, isx, who, xt, ot, xv, ov = info(ti)
        gt1 = MOD(l, who, 2)
        for dc in range(8):
            bk = banks[1 + bi[0] % 2]
            bi[0] += 1
            for mc in range(8):
                ph.mm(bk[:, 0:N], wov[:, mc, dc * 128:(dc + 1) * 128], ov[:, mc, :], start=(mc == 0), stop=(mc == 7),
                      reads=[wo, ot], bank=bk, signal=(mc == 7))
            ph.stt(xv[:, dc, :], bk[:, 0:N], pp[:, gt1 + dc:gt1 + dc + 1], xv[:, dc, :], ALU.mult, ALU.add,
                   reads=[pp], writes=[bk, xt])

    def st_Bn(ti):
        kind, t, isx, who, xt, ot, xv, ov = info(ti)
        ph.act(sv, xv, AF.Square, reads=[xt], writes=hhk)

    def st_Bs(ti):
        bS = banks[0]
        for kc in range(8):
            ph.mm(bS[:, 0:N], ones[:, :], sv[:, kc, :], start=(kc == 0), stop=(kc == 7), reads=[ones, hhk[kc]], bank=bS,
                  signal=(kc == 7))

    def st_Bc(ti):
        kind, t, isx, who, xt, ot, xv, ov = info(ti)
        sh2, gs2 = MOD(l, who, 3), GS(l, who, 1)
        bS = banks[0]
        rstd_from_sums(ph, bS[:, 0:N], D, lnt[:, :], rstd[:, :], bS, lnt, rstd)
        for kc in range(8):
            ph.tt(nv[:, kc, :], xv[:, kc, :], rstd[:, :], ALU.mult, reads=[xt, rstd], writes=[xnk[kc]])
            if kc % 3 == 2:
                ph.act(hv[:, kc, :], nv[:, kc, :], AF.Identity, reads=[xnk[kc], pp], writes=[hhk[kc]],
                       scale=pp[:, gs2 + kc:gs2 + kc + 1], bias=pp[:, sh2 + kc:sh2 + kc + 1])
            else:
                ph.ts(hv[:, kc, :], nv[:, kc, :], pp[:, gs2 + kc:gs2 + kc + 1], pp[:, sh2 + kc:sh2 + kc + 1], ALU.mult,
                      ALU.add, reads=[xnk[kc], pp], writes=[hhk[kc]], eng='gpsimd')

    def st_C(ti):
        for fc in range(NFC):
            bG, bU = banks[3 + (fc % 2) * 2], banks[4 + (fc % 2) * 2]
            sg = sgs[fc % 2]
            for kc in range(8):
                ph.mm(bG[:, 0:N], wgv[:, kc, fc * 128:(fc + 1) * 128], hv[:, kc, :], start=(kc == 0), stop=(kc == 7),
                      reads=[wgs[(fc * 128) // WGC], wgs[(fc * 128 + 127) // WGC], hhk[kc]], bank=bG, signal=(kc == 7))
            for kc in range(8):
                ph.mm(bU[:, 0:N], wuv[:, kc, fc * 128:(fc + 1) * 128], hv[:, kc, :], start=(kc == 0), stop=(kc == 7),
                      reads=[wus[(fc * 128) // WGC], wus[(fc * 128 + 127) // WGC], hhk[kc]], bank=bU, signal=(kc == 7))
            ph.act(sg[:, :], bG[:, 0:N], AF.Silu, reads=[], writes=[bG, sg])
            ph.tt(av[:, fc, :], sg[:, :], bU[:, 0:N], ALU.mult, reads=[sg], writes=[bU, aa])

    def st_D(ti, dcs):
        kind, t, isx, who, xt, ot, xv, ov = info(ti)
        gt2 = MOD(l, who, 5)
        for dc in dcs:
            bk = banks[1 + bi[0] % 2]
            bi[0] += 1
            for fc in range(NFC):
                ph.mm(bk[:, 0:N], wdv[:, fc, dc * 128:(dc + 1) * 128], av[:, fc, :], start=(fc == 0), stop=(fc == NFC - 1),
                      reads=[wd, aa], bank=bk, signal=(fc == NFC - 1))
            ph.stt(xv[:, dc, :], bk[:, 0:N], pp[:, gt2 + dc:gt2 + dc + 1], xv[:, dc, :], ALU.mult, ALU.add,
                   reads=[pp], writes=[bk, xt])

    def st_out(ti):
        kind, t, isx, who, xt, ot, xv, ov = info(ti)
        if last:
            sv2 = v3(aa[:, 0:8 * N], N)
            ph.act(sv2, xv, AF.Square, reads=[xt], writes=[aa])
            bS = banks[7]
            for kc in range(8):
                ph.mm(bS[:, 0:N], ones[:, :], sv2[:, kc, :], start=(kc == 0), stop=(kc == 7), reads=[ones, aa], bank=bS,
                      signal=(kc == 7))
            rstd_from_sums(ph, bS[:, 0:N], D, lnf[:, :], rstdf[:, :], bS, lnf, rstdf)
            for kc in range(8):
                ph.stt(nv[:, kc, :], xv[:, kc, :], sm[:, SM_GFIN + kc:SM_GFIN + kc + 1], rstdf[:, :], ALU.mult, ALU.mult,
                       reads=[xt, sm, rstdf], writes=[xnk[kc]])
            ph.dma(k.yT[:, t * N:(t + 1) * N].rearrange("(c p) t -> p c t", p=128), nv, reads=xnk, eng='sync')
        else:
            dst = k.xD1[:, t * N:(t + 1) * N] if isx else k.ctxD1[:, :]
            ph.dma(dst.rearrange("(c p) t -> p c t", p=128), xv, reads=[xt], eng='sync')

    NTL = len(tiles)
    load(0)
    if NTL > 1:
        load(1)
    emit_big_weights()
    st_A(0)
    st_Bn(0)
    st_Bs(0)
    st_Bc(0)
    for ti in range(NTL):
        st_C(ti)
        if ti + 1 < NTL:
            st_A(ti + 1)
            st_Bn(ti + 1)
            st_D(ti, range(0, 2))
            st_Bs(ti + 1)
            st_Bc(ti + 1)
            st_D(ti, range(2, 8))
        else:
            st_D(ti, range(0, 8))
        st_out(ti)
        if ti + 2 < NTL:
            load(ti + 2)
    ph.finish()


def build_program(S):
    nc = bass.Bass("TRN2", target_bir_lowering=False)
    k = K()
    k.nc = nc
    k.S = S
    L1 = S // 128

    def din(name, shape, dt=F32):
        return nc.dram_tensor(name, list(shape), dt, kind="ExternalInput").ap()

    def dscr(name, shape, dt):
        return nc.dram_tensor(name, list(shape), dt).ap()

    k.xT = din("xT", [D, S])
    k.ctxT = din("ctxT", [D, CTX])
    k.cvec = din("cvec", [128, 16])
    k.small = din("small", [128, SMW])
    k.w_ada = din("w_ada", [DEPTH, D, 6 * D])
    k.w_in_p = din("w_in_p", [DEPTH, D, 2304])
    k.w_in_fT = din("w_in_fT", [DEPTH, 256, D])
    k.w_out_p = din("w_out_p", [DEPTH, D, D])
    k.w_gate = din("w_gate", [DEPTH, D, DFF])
    k.w_up = din("w_up", [DEPTH, D, DFF])
    k.w_down = din("w_down", [DEPTH, DFF, D])
    k.sinkrow = din("sinkrow", [DEPTH, 1, 6 * 256])
    k.ropeC = din("ropeC", [128, S])
    k.ropeS = din("ropeS", [128, S])
    k.cmat = din("cmat", [128, 512])
    k.cbf = din("cbf", [128, CBW])
    k.yT = nc.dram_tensor("yT", [D, S], F32, kind="ExternalOutput").ap()

    k.wfD = dscr("wfD", [DEPTH, D, 512], BF16)
    k.qaD = dscr("qaD", [3, 128, CTX + S], BF16)
    k.qcD = dscr("qcD", [3, 128, CTX + S], BF16)
    k.kaD = dscr("kaD", [128, CTX + S], BF16)
    k.kcD = dscr("kcD", [128, CTX + S], BF16)
    k.vD = dscr("vD", [CTX + S, VW], BF16)
    k.zD = dscr("zD", [2, S, 256], BF16)
    k.zDc = dscr("zDc", [2, CTX, 256], BF16)
    k.ZD = dscr("ZD", [2, L1, 128, 256], BF16)
    k.oD = dscr("oD", [8, 128, CTX + S], BF16)
    k.woB = dscr("woB", [DEPTH, D, D], BF16)
    k.wgB = dscr("wgB", [DEPTH, D, DFF], BF16)
    k.wuB = dscr("wuB", [DEPTH, D, DFF], BF16)
    k.wdB = dscr("wdB", [DEPTH, DFF, D], BF16)
    k.winB = dscr("winB", [DEPTH, D, 2304], BF16)
    k.xD1 = dscr("xD1", [D, S], F32)
    k.ctxD1 = dscr("ctxD1", [D, CTX], F32)

    with contextlib.ExitStack() as st:
        Phase.state = SemState(nc, st)
        k.pp = st.enter_context(nc.sbuf_tensor("pp_persist", [128, PPW], F32))
        k.sm = st.enter_context(nc.sbuf_tensor("sm_persist", [128, SMW], F32))
        import os
        sel = os.environ.get("KPHASES", "")
        for l in range(DEPTH):
            for nm, f in (("A", phase_A), ("B", phase_B), ("C", phase_C), ("D", phase_D), ("E", phase_E), ("G", phase_G)):
                if sel and ("%s%d" % (nm, l)) not in sel.split(","):
                    continue
                f(k, l)
    return nc


def host_constants(S):
    L1 = S // 128
    f32 = np.float32
    tok = np.arange(S)
    row = (tok // 64).astype(f32)
    col = (tok % 64).astype(f32)
    inv_freq = (np.float32(10000.0) ** (-np.arange(0, 32, 2, dtype=f32) / np.float32(32.0))).astype(f32)
    ang = np.concatenate([row[:, None] * inv_freq, col[:, None] * inv_freq], axis=-1).astype(f32)
    cos = np.cos(ang).astype(f32).T
    sin = np.sin(ang).astype(f32).T
    ropeC = np.concatenate([cos, cos, cos, cos], axis=0)
    ropeS = np.concatenate([-sin, sin, -sin, sin], axis=0)
    c = np.arange(64)
    a64 = 2 * np.pi * np.outer(c, c) / 64.0
    C64, S64 = np.cos(a64), np.sin(a64)
    Z = np.zeros((64, 64))
    cmat = np.zeros((128, 512), f32)
    cmat[:, 0:128] = np.block([[C64, Z], [Z, C64]])
    cmat[:, 128:256] = np.block([[-S64, Z], [Z, -S64]])
    k1 = np.arange(L1)
    l2 = np.arange(128)
    at = 2 * np.pi * np.outer(k1, l2) / S
    cmat[0:2 * L1, 256:384] = np.concatenate([np.cos(at), np.cos(at)], axis=0)
    cmat[0:2 * L1, 384:512] = np.concatenate([np.sin(at), np.sin(at)], axis=0)
    cbf = np.zeros((128, CBW), f32)
    a1 = 2 * np.pi * np.outer(k1, k1) / L1
    Cc, Sc = np.cos(a1), np.sin(a1)
    M1 = np.block([[Cc, Sc], [-Sc, Cc]])
    M2 = np.block([[-Sc, Cc], [-Cc, -Sc]])
    cbf[0:2 * L1, CB_M1:CB_M1 + 2 * L1] = M1.T
    cbf[0:2 * L1, CB_M2:CB_M2 + 2 * L1] = M2.T
    a128 = 2 * np.pi * np.outer(l2, l2) / 128.0
    cbf[:, CB_C128:CB_C128 + 128] = np.cos(a128)
    cbf[:, CB_S128:CB_S128 + 128] = np.sin(a128)
    n256 = np.arange(256)
    a256 = 2 * np.pi * np.outer(n256, n256) / 256.0
    C256, S256 = np.cos(a256), np.sin(a256)
    for cch in range(2):
        cbf[:, CB_C256 + cch * 256:CB_C256 + (cch + 1) * 256] = C256[cch * 128:(cch + 1) * 128, :]
        cbf[:, CB_S256 + cch * 256:CB_S256 + (cch + 1) * 256] = S256[cch * 128:(cch + 1) * 128, :]
    a = np.arange(128)[:, None]
    i = np.arange(128)[None, :]
    cbf[:, CB_MASK:CB_MASK + 128] = (a >= i)
    cbf[:, CB_MASK + 128:CB_MASK + 256] = (a <= i)
    return ropeC.astype(f32), ropeS.astype(f32), cmat, cbf.astype(f32)


def host_layout(inp, S):
    f32 = np.float32
    g = lambda n: np.asarray(inp[n], dtype=f32)
    w_in = g('w_in')

    def pair(base, j):
        return np.concatenate([np.arange(base + 64 * j, base + 64 * j + 64), np.arange(base + 64 * (3 + j), base + 64 * (3 + j) + 64)])

    def swap(idx):
        idx = idx.reshape(-1, 64)
        return np.concatenate([idx[:, 32:], idx[:, :32]], axis=1).reshape(-1)

    cols = []
    qa = [pair(0, j) for j in range(3)]
    ka = np.arange(384, 512)
    qc = [pair(896, j) for j in range(3)]
    kc = np.arange(1280, 1408)
    cols += qa + [swap(q) for q in qa] + [ka, swap(ka)] + qc + [swap(q) for q in qc] + [kc, swap(kc)]
    cols += [np.arange(512, 640), np.arange(1408, 1536)]
    cols = np.concatenate(cols)
    assert cols.shape[0] == 2304
    w_in_p = np.ascontiguousarray(w_in[:, :, cols])
    w_in_fT = np.ascontiguousarray(np.transpose(w_in[:, :, 640:896], (0, 2, 1)))
    rows = np.concatenate([pair(0, j) for j in range(3)] + [np.arange(384, 640)] + [pair(640, j) for j in range(3)])
    w_out_p = np.ascontiguousarray(g('w_out')[:, rows, :])

    def pl(v):
        return np.ascontiguousarray(v.reshape(8, 128).T)

    small = np.zeros((128, SMW), f32)
    b_ada, g_mix, g_ffn = g('b_ada'), g('g_mix'), g('g_ffn')
    qn, kn = g('q_norm'), g('k_norm')
    p = np.arange(128) % 64
    ps = (p + 32) % 64
    for l in range(DEPTH):
        small[:, SM_BADA(l):SM_BADA(l) + 48] = b_ada[l].reshape(48, 128).T
        small[:, SM_GMIX(l):SM_GMIX(l) + 8] = pl(g_mix[l])
        small[:, SM_GFFN(l):SM_GFFN(l) + 8] = pl(g_ffn[l])
        small[:, SM_QKN(l) + 0] = qn[l][p]
        small[:, SM_QKN(l) + 1] = qn[l][ps]
        small[:, SM_QKN(l) + 2] = kn[l][p]
        small[:, SM_QKN(l) + 3] = kn[l][ps]
    small[:, SM_GFIN:SM_GFIN + 8] = pl(g('g_final'))
    sinkrow = np.ascontiguousarray(np.repeat(g('sink')[:, None, :, None], 256, axis=3).reshape(DEPTH, 1, 6 * 256))
    ropeC, ropeS, cmat, cbf = host_constants(S)
    shared = dict(small=small, w_ada=g('w_ada'), w_in_p=w_in_p, w_in_fT=w_in_fT, w_out_p=w_out_p,
                  w_gate=g('w_gate'), w_up=g('w_up'), w_down=g('w_down'), sinkrow=sinkrow,
                  ropeC=ropeC, ropeS=ropeS, cmat=cmat, cbf=cbf)
    x, c, ctx, c_ctx = g('x'), g('c'), g('ctx'), g('c_ctx')
    B = x.shape[0]
    maps = []
    for b in range(B):
        cvec = np.zeros((128, 16), f32)
        cvec[:, 0::2] = pl(c[b])
        cvec[:, 1::2] = pl(c_ctx)
        m = dict(shared)
        m['xT'] = np.ascontiguousarray(x[b].T)
        m['ctxT'] = np.ascontiguousarray(ctx[b].T)
        m['cvec'] = cvec
        maps.append(m)
    return maps


_CACHE = {}


def run(inp, S):
    maps = host_layout(inp, S)
    if S not in _CACHE:
        _CACHE[S] = build_program(S)
    nc = _CACHE[S]
    res = run_bass_kernel_spmd(nc, maps, core_ids=list(range(len(maps))))
    out = np.stack([np.ascontiguousarray(r["yT"].T) for r in res.results], axis=0)
    return out.astype(np.float32)


def kernel(**inputs):
    return run(inputs, 8192)
```

```python
import contextlib
import math
import numpy as np
import concourse.bass as bass
import concourse.mybir as mybir
from concourse.bass_utils import run_bass_kernel_spmd

F32 = mybir.dt.float32
BF16 = mybir.dt.bfloat16
AF = mybir.ActivationFunctionType
ALU = mybir.AluOpType

D = 1024
DFF = 2816
NFC = DFF // 128
CTX = 256
DEPTH = 2
EPS = 1e-6
VW = 512
ENGS = ['tensor', 'vector', 'scalar', 'gpsimd', 'sync']


class Buf:
    __slots__ = ('name', 'last_w', 'readers')

    def __init__(self, name):
        self.name = name
        self.last_w = None
        self.readers = {}


class Tile:
    def __init__(self, t, name):
        self.t = t
        self.b = Buf(name)

    def __getitem__(self, k):
        return self.t[k]


class SemState:
    def __init__(self, nc, st, n_dma_sems=26):
        self.cnt = {e: 0 for e in ENGS}
        self.known = {e: {} for e in ENGS}
        self.n_dma = n_dma_sems
        self.dma_val = [0] * n_dma_sems
        self.n_hw = n_dma_sems - 10
        self.rr_hw = 0
        self.rr_sw = 0
        self.sems = {}
        for e in ENGS:
            self.sems[e] = st.enter_context(nc.semaphore("sem_" + e))
        for k in range(n_dma_sems):
            self.sems[('dma', k)] = st.enter_context(nc.semaphore("sem_d%d" % k))


class Sched:
    def __init__(self, nc, state):
        self.nc = nc
        self.ops = {e: [] for e in ENGS}
        self.state = state

    @property
    def cnt(self):
        return self.state.cnt

    @property
    def known(self):
        return self.state.known

    @property
    def dma_val(self):
        return self.state.dma_val

    @property
    def n_dma(self):
        return self.state.n_dma

    def emit(self, engine, fn, reads=(), writes=(), dma=False, signal=True):
        need = {}

        def add(ev):
            if ev is None:
                return
            k, v = ev
            if need.get(k, 0) < v:
                need[k] = v

        for b in reads:
            add(b.last_w)
        for b in writes:
            add(b.last_w)
            for k, v in b.readers.items():
                add((k, v))
        if engine == 'tensor':
            need.pop('tensor', None)
        kd = None
        if dma:
            stt_ = self.state
            if engine == 'gpsimd':
                kd = stt_.n_hw + stt_.rr_sw
                stt_.rr_sw = (stt_.rr_sw + 1) % (stt_.n_dma - stt_.n_hw)
            else:
                kd = stt_.rr_hw
                stt_.rr_hw = (stt_.rr_hw + 1) % stt_.n_hw
            if self.dma_val[kd] > 0:
                add((('dma', kd), self.dma_val[kd]))
        kn = self.known[engine]
        waits = []
        for k, v in need.items():
            if kn.get(k, 0) >= v:
                continue
            kn[k] = v
            waits.append((k, v))
        if dma:
            self.dma_val[kd] += 16
            ev = (('dma', kd), self.dma_val[kd])
            inc = ev
        elif signal:
            self.cnt[engine] += 1
            ev = (engine, self.cnt[engine])
            inc = ev
        else:
            ev = (engine, self.cnt[engine] + 1)
            inc = None
        self.ops[engine].append((waits, fn, inc))
        for b in writes:
            b.last_w = ev
            b.readers = {}
        for b in reads:
            if b.last_w is not ev:
                if b.readers.get(ev[0], 0) < ev[1]:
                    b.readers[ev[0]] = ev[1]
        return ev

    def wait_all(self, engine):
        need = {}
        for e in ENGS:
            if e != engine and self.cnt[e] > 0:
                need[e] = self.cnt[e]
        for k in range(self.n_dma):
            if self.dma_val[k] > 0:
                need[('dma', k)] = self.dma_val[k]
        kn = self.known[engine]
        waits = [(k, v) for k, v in need.items() if kn.get(k, 0) < v]
        for k, v in waits:
            kn[k] = v
        self.ops[engine].append((waits, None, None))

    def build(self):
        nc = self.nc
        sems = self.state.sems
        with nc.Block() as block:
            def run(engname):
                def body(eng):
                    for waits, fn, inc in self.ops[engname]:
                        if fn is None:
                            for k, v in waits:
                                eng.wait_ge(sems[k], v)
                            continue
                        for k, v in waits[1:]:
                            eng.wait_ge(sems[k], v)
                        ins = fn(eng)
                        if waits:
                            k, v = waits[0]
                            ins._wait_ge(sems[k], v)
                        if inc is not None:
                            k, v = inc
                            ins.then_inc(sems[k], 16 if isinstance(k, tuple) else 1)
                return body

            block.tensor(run('tensor'))
            block.vector(run('vector'))
            block.scalar(run('scalar'))
            block.gpsimd(run('gpsimd'))
            block.sync(run('sync'))


class Phase:
    _uid = [0]
    state = None

    def __init__(self, nc, name):
        self.nc = nc
        self.name = name
        self.st = contextlib.ExitStack()
        self.S = Sched(nc, Phase.state)
        self.banks = []

    def _nm(self, name):
        Phase._uid[0] += 1
        return "%s_%s_%d" % (self.name, name, Phase._uid[0])

    def sb(self, name, shape, dt, n=1):
        out = []
        for i in range(n):
            nm = self._nm(name)
            t = self.st.enter_context(self.nc.sbuf_tensor(nm, list(shape), dt))
            out.append(Tile(t, nm))
        return out if n > 1 else out[0]

    def psum_banks(self, n=8):
        for i in range(n):
            nm = self._nm("bank")
            t = self.st.enter_context(self.nc.psum_tensor(nm, [128, 512], F32))
            self.banks.append(Tile(t, nm))
        return self.banks

    def wrap(self, t, name):
        return Tile(t, self._nm(name))

    def dma(self, out, in_, reads=(), writes=(), eng='sync'):
        self.S.emit(eng, lambda e: e.dma_start(out=out, in_=in_), reads=[r.b for r in reads],
                    writes=[w.b for w in writes], dma=True)

    def mm(self, out, lhsT, rhs, start, stop, reads, bank, signal=True, skip=False):
        if skip:
            fn = lambda e: e.matmul(out, lhsT=lhsT, rhs=rhs, start=start, stop=stop, skip_group_check=True)
        else:
            fn = lambda e: e.matmul(out, lhsT=lhsT, rhs=rhs, start=start, stop=stop)
        self.S.emit('tensor', fn, reads=[r.b for r in reads], writes=[bank.b], signal=signal)

    def op(self, eng, fn, reads=(), writes=()):
        self.S.emit(eng, fn, reads=[r.b for r in reads], writes=[w.b for w in writes])

    def act(self, out, in_, func, reads, writes, scale=None, bias=None, eng='scalar'):
        kw = {}
        if scale is not None:
            kw['scale'] = scale
        if bias is not None:
            kw['bias'] = bias
        self.op(eng, lambda e: e.activation(out=out, in_=in_, func=func, **kw), reads, writes)

    def tt(self, out, in0, in1, op, reads, writes, eng='vector'):
        self.op(eng, lambda e: e.tensor_tensor(out=out, in0=in0, in1=in1, op=op), reads, writes)

    def stt(self, out, in0, scalar, in1, op0, op1, reads, writes):
        self.op('vector', lambda e: e.scalar_tensor_tensor(out=out, in0=in0, scalar=scalar, in1=in1, op0=op0, op1=op1),
                reads, writes)

    def ts(self, out, in0, s1, s2, op0, op1, reads, writes, eng='gpsimd'):
        if op1 is None:
            self.op(eng, lambda e: e.tensor_scalar(out=out, in0=in0, scalar1=s1, scalar2=None, op0=op0), reads, writes)
        else:
            self.op(eng, lambda e: e.tensor_scalar(out=out, in0=in0, scalar1=s1, scalar2=s2, op0=op0, op1=op1), reads, writes)

    def copy(self, out, in_, reads, writes, eng='vector'):
        if eng == 'scalar':
            self.op(eng, lambda e: e.activation(out=out, in_=in_, func=AF.Copy), reads, writes)
        else:
            self.op(eng, lambda e: e.tensor_copy(out=out, in_=in_), reads, writes)

    def memset(self, ap, val, writes, eng='vector'):
        self.op(eng, lambda e: e.memset(ap, val), (), writes)

    def finish(self):
        for e in ENGS:
            self.S.wait_all(e)
        self.S.build()
        self.st.close()


def v3(ap, inner):
    return ap.rearrange("p (a b) -> p a b", b=inner)


def MOD(l, who, i):
    return ((l * 2 + who) * 6 + i) * 8
GS_BASE = DEPTH * 2 * 6 * 8
def GS(l, who, k):
    return GS_BASE + ((l * 2 + who) * 2 + k) * 8
PPW = GS_BASE + DEPTH * 2 * 2 * 8
def SM_BADA(l): return l * 48
def SM_GMIX(l): return DEPTH * 48 + l * 8
def SM_GFFN(l): return DEPTH * 48 + DEPTH * 8 + l * 8
SM_GFIN = DEPTH * 48 + 2 * DEPTH * 8
def SM_QKN(l): return SM_GFIN + 8 + l * 4
SMW = SM_GFIN + 8 + DEPTH * 4


class K:
    pass


def rstd_from_sums(ph, bank_ap, n_div, tmp, out, reads_bank, tmp_t, out_t):
    ph.act(tmp, bank_ap, AF.Ln, reads=[], writes=[reads_bank, tmp_t], scale=1.0 / n_div, bias=EPS)
    ph.act(out, tmp, AF.Exp, reads=[tmp_t], writes=[out_t], scale=-0.5)


def precast(ph, k, l, ffn=True, win=True, dep=()):
    if win and l < DEPTH:
        for r in range(0, D, 256):
            ph.dma(k.winB[l, r:r + 256, :], k.w_in_p[l, r:r + 256, :], reads=dep, eng='gpsimd')
    if ffn and l < DEPTH:
        for r in range(0, D, 256):
            ph.dma(k.woB[l, r:r + 256, :], k.w_out_p[l, r:r + 256, :], reads=dep, eng='gpsimd')
            ph.dma(k.wgB[l, r:r + 256, :], k.w_gate[l, r:r + 256, :], reads=dep, eng='gpsimd')
            ph.dma(k.wuB[l, r:r + 256, :], k.w_up[l, r:r + 256, :], reads=dep, eng='gpsimd')
        for r in range(0, DFF, 256):
            ph.dma(k.wdB[l, r:r + 256, :], k.w_down[l, r:r + 256, :], reads=dep, eng='gpsimd')


def phase_A(k, l):
    nc = k.nc
    ph = Phase(nc, "A%d" % l)
    banks = ph.psum_banks(4)
    pp = ph.wrap(k.pp, "pp")
    sm = ph.wrap(k.sm, "sm")
    cv = ph.sb("cv", [128, 16], F32)
    sc = ph.sb("sc", [128, 16], F32)
    wst = ph.sb("wst", [128, 6 * D], F32, n=2)
    wfT = ph.sb("wfT", [128, 2 * D], F32)
    bd = ph.sb("bd", [128, 256], F32)
    wfo = ph.sb("wfo", [128, 512], BF16, n=2)
    if l == 0:
        ph.dma(sm[:, :], k.small[:, :], writes=[sm])
        precast(ph, k, 0, ffn=False, win=True)
    ph.dma(cv[:, :], k.cvec[:, :], writes=[cv])
    ph.act(sc[:, :], cv[:, :], AF.Silu, reads=[cv], writes=[sc])
    b0 = banks[0]
    ph.memset(b0[:, 0:96], 0.0, writes=[b0])
    import os
    dbg = os.environ.get("KDBG", "")
    for kc in range(0 if 'nomm' in dbg else 8):
        w = wst[kc % 2]
        ph.dma(w[:, :], k.w_ada[l, kc * 128:(kc + 1) * 128, :], writes=[w])
        for cc in range(48):
            ph.mm(b0[:, cc * 2:cc * 2 + 2], w[:, cc * 128:(cc + 1) * 128], sc[:, kc * 2:kc * 2 + 2],
                  start=False, stop=(kc == 7), reads=[w, sc], bank=b0, signal=(cc == 47), skip=True)
    for who in range(2):
        src = v3(b0[:, 0:96], 2)[:, :, who]
        c0 = MOD(l, who, 0)
        ph.tt(pp[:, c0:c0 + 48], src, sm[:, SM_BADA(l):SM_BADA(l) + 48], ALU.add, reads=[sm], writes=[b0, pp])
        for kk, (i_sc, gcol) in enumerate(((1, SM_GMIX(l)), (4, SM_GFFN(l)))):
            cs = MOD(l, who, i_sc)
            cg = GS(l, who, kk)
            ph.stt(pp[:, cg:cg + 8], pp[:, cs:cs + 8], 1.0, sm[:, gcol:gcol + 8], ALU.add, ALU.mult,
                   reads=[sm], writes=[pp])
    ph.dma(v3(wfT[:, :], D), k.w_in_fT[l].rearrange("(c p) d -> p c d", p=128), writes=[wfT])
    ph.dma(bd[:, :], k.cmat[:, 0:256], writes=[bd])
    for dc in range(0 if 'nowf' in dbg else 8):
        bk = banks[1 + dc % 2]
        for t in range(2):
            for fc in range(2):
                q = t * 2 + fc
                ph.mm(bk[:, q * 128:(q + 1) * 128], wfT[:, fc * D + dc * 128: fc * D + (dc + 1) * 128],
                      bd[:, t * 128:(t + 1) * 128], start=True, stop=True, reads=[wfT, bd], bank=bk,
                      signal=(q == 3))
        o = wfo[dc % 2]
        ph.copy(o[:, :], bk[:, :], reads=[], writes=[bk, o])
        ph.dma(k.wfD[l, dc * 128:(dc + 1) * 128, :], o[:, :], reads=[o], eng='gpsimd')
    ph.finish()


CH_QA, CH_QAS, CH_KA, CH_KAS, CH_QC, CH_QCS, CH_KC, CH_KCS = 0, 3, 6, 7, 8, 11, 14, 15
TM0 = 2048
WCOLS = 2048 + 768


def phase_B(k, l):
    nc = k.nc
    S = k.S
    NT = S // 512
    ph = Phase(nc, "B%d" % l)
    banks = ph.psum_banks(8)
    pp = ph.wrap(k.pp, "pp")
    sm = ph.wrap(k.sm, "sm")
    wT = ph.sb("wT", [128, 8 * WCOLS], BF16)
    wv = v3(wT[:, :], WCOLS)
    xts = ph.sb("xt", [128, 8 * 512], F32, n=2)
    sq = ph.sb("sq", [128, 8 * 512], BF16)
    hTs = ph.sb("hT", [128, 8 * 512], BF16, n=2)
    ones = ph.sb("ones", [128, 128], BF16)
    bones = ph.sb("bones", [128, 128], BF16)
    lnt = ph.sb("lnt", [128, 512], F32)
    rstd = ph.sb("rstd", [128, 512], F32)
    rC = ph.sb("rC", [128, 512], F32, n=3)
    rS = ph.sb("rS", [128, 512], F32, n=3)
    sqh = ph.sb("sqh", [128, 512], BF16, n=2)
    lnh = ph.sb("lnh", [128, 512], F32)
    rinv = ph.sb("rinv", [128, 512], F32, n=2)
    t1s = ph.sb("t1", [128, 512], F32, n=2)
    t2s = ph.sb("t2", [128, 512], F32, n=2)
    qo = ph.sb("qo", [128, 512], BF16, n=4)
    vts = ph.sb("vt", [128, VW], BF16, n=4)
    zts = ph.sb("zt", [128, 512], BF16, n=4)

    ph.memset(ones[:, :], 1.0, writes=[ones])
    ph.memset(bones[:, :], 0.0, writes=[bones])
    ph.memset(bones[0:64, 0:64], 1.0, writes=[bones])
    ph.memset(bones[64:128, 64:128], 1.0, writes=[bones])
    for vt in vts:
        ph.memset(vt[:, :], 1.0, writes=[vt], eng='gpsimd')
    for c0 in range(0, 2304, 768):
        ph.dma(wv[:, :, c0:c0 + 768], k.winB[l, :, c0:c0 + 768].rearrange("(c p) f -> p c f", p=128), writes=[wT])
    ph.dma(wv[:, :, 2304:2816], k.wfD[l].rearrange("(c p) f -> p c f", p=128), writes=[wT])

    qk = SM_QKN(l)
    gq, gqs, gk, gks = (sm[:, qk + i:qk + i + 1] for i in range(4))
    pair_i = [0]
    qo_i = [0]
    vt_i = [0]

    tiles = [('ctx', 0)] + [('x', t) for t in range(NT)]

    def info(ti):
        kind, t = tiles[ti]
        isx = kind == 'x'
        N = 512 if isx else CTX
        off = CTX + t * 512 if isx else 0
        who = 0 if isx else 1
        need_q = isx or l == 0
        xt, hT = xts[ti % 2], hTs[ti % 2]
        return kind, t, isx, N, off, who, need_q, xtk[ti % 2], hTk[ti % 2], v3(xt[:, :], 512), v3(hT[:, :], 512)

    sv = v3(sq[:, :], 512)
    pending = []
    xtk = [[Tile(x_.t, "xk%d_%d" % (i, c)) for c in range(8)] for i, x_ in enumerate(xts)]
    hTk = [[Tile(h_.t, "hk%d_%d" % (i, c)) for c in range(8)] for i, h_ in enumerate(hTs)]

    def pre(ti):
        kind, t, isx, N, off, who, need_q, xt, hT, xv, hv = info(ti)
        if isx:
            src = (k.xT if l == 0 else k.xD1)[:, t * 512:(t + 1) * 512]
        else:
            src = (k.ctxT if l == 0 else k.ctxD1)[:, :]
        ph.dma(xv[:, :, 0:N], src.rearrange("(c p) t -> p c t", p=128), writes=xt)
        if isx:
            c_t, s_t = rC[t % 3], rS[t % 3]
            ph.dma(c_t[:, :], k.ropeC[:, t * 512:(t + 1) * 512], writes=[c_t])
            ph.dma(s_t[:, :], k.ropeS[:, t * 512:(t + 1) * 512], writes=[s_t])

    def chain1(ti):
        kind, t, isx, N, off, who, need_q, xt, hT, xv, hv = info(ti)
        ph.act(sv[:, :, 0:N], xv[:, :, 0:N], AF.Square, reads=xt, writes=[sq])

    def chain2(ti):
        kind, t, isx, N, off, who, need_q, xt, hT, xv, hv = info(ti)
        bS = banks[0]
        for kc in range(8):
            ph.mm(bS[:, 0:N], ones[:, :], sv[:, kc, 0:N], start=(kc == 0), stop=(kc == 7), reads=[ones, sq],
                  bank=bS, signal=(kc == 7))

    def chain3(ti):
        kind, t, isx, N, off, who, need_q, xt, hT, xv, hv = info(ti)
        bS = banks[0]
        rstd_from_sums(ph, bS[:, 0:N], D, lnt[:, 0:N], rstd[:, 0:N], bS, lnt, rstd)
        g0 = GS(l, who, 0)
        s0 = MOD(l, who, 0)
        for kc in range(8):
            ph.tt(xv[:, kc, 0:N], xv[:, kc, 0:N], rstd[:, 0:N], ALU.mult, reads=[rstd], writes=[xt[kc]])
            if kc % 3 == 2:
                ph.act(hv[:, kc, 0:N], xv[:, kc, 0:N], AF.Identity, reads=[xt[kc], pp], writes=[hT[kc]],
                       scale=pp[:, g0 + kc:g0 + kc + 1], bias=pp[:, s0 + kc:s0 + kc + 1])
            else:
                ph.ts(hv[:, kc, 0:N], xv[:, kc, 0:N], pp[:, g0 + kc:g0 + kc + 1], pp[:, s0 + kc:s0 + kc + 1],
                      ALU.mult, ALU.add, reads=[xt[kc], pp], writes=[hT[kc]], eng='gpsimd')

    def work(ti, part):
        kind, t, isx, N, off, who, need_q, xt, hT, xv, hv = info(ti)
        if isx:
            c_t, s_t = rC[t % 3], rS[t % 3]

        def proj(ch, bank):
            for kc in range(8):
                ph.mm(bank[:, 0:N], wv[:, kc, ch * 128:(ch + 1) * 128], hv[:, kc, 0:N], start=(kc == 0),
                      stop=(kc == 7), reads=[wT, hT[kc]], bank=bank, signal=(kc == 7))

        def out_tile():
            o = qo[qo_i[0] % 4]
            qo_i[0] += 1
            return o

        jobs = []
        for j in range(3):
            if need_q:
                jobs.append((CH_QA + j, CH_QAS + j, True, gq, gqs, k.qaD[j, :, off:off + N]))
        jobs.append((CH_KA, CH_KAS, True, gk, gks, k.kaD[:, off:off + N]))
        for j in range(3):
            if need_q:
                jobs.append((CH_QC + j, CH_QCS + j, False, None, None, k.qcD[j, :, off:off + N]))
        jobs.append((CH_KC, CH_KCS, False, None, None, k.kcD[:, off:off + N]))
        nsplit = 3 if len(jobs) > 3 else 1
        jobs = jobs[:nsplit] if part == 0 else jobs[nsplit:]
        for (ch, chs, normed, g, gs_, dst) in jobs:
            pi = pair_i[0]
            pair_i[0] += 1
            b1 = banks[1 + 2 * (pi % 2)]
            b2 = banks[2 + 2 * (pi % 2)]
            bN = banks[5]
            proj(ch, b1)
            if isx:
                proj(chs, b2)
            o = out_tile()
            t1 = t1s[pi % 2]
            t2 = t2s[pi % 2]
            sh_ = sqh[pi % 2]
            ri = rinv[pi % 2]
            if normed:
                ph.act(sh_[:, 0:N], b1[:, 0:N], AF.Square, reads=[], writes=[b1, sh_])

            def back(normed=normed, g=g, gs_=gs_, dst=dst, b1=b1, b2=b2, bN=bN, o=o, t1=t1, t2=t2, sh_=sh_, ri=ri,
                     isx=isx, N=N):
                if normed:
                    ph.mm(bN[:, 0:N], bones[:, :], sh_[:, 0:N], start=True, stop=True, reads=[bones, sh_], bank=bN)
                    rstd_from_sums(ph, bN[:, 0:N], 64, lnh[:, 0:N], ri[:, 0:N], bN, lnh, ri)
                    if isx:
                        ph.stt(t1[:, 0:N], b1[:, 0:N], g, c_t[:, 0:N], ALU.mult, ALU.mult, reads=[sm, c_t], writes=[b1, t1])
                        ph.stt(t2[:, 0:N], b2[:, 0:N], gs_, s_t[:, 0:N], ALU.mult, ALU.mult, reads=[sm, s_t], writes=[b2, t2])
                        ph.tt(t1[:, 0:N], t1[:, 0:N], t2[:, 0:N], ALU.add, reads=[t2], writes=[t1], eng='gpsimd')
                        ph.tt(o[:, 0:N], t1[:, 0:N], ri[:, 0:N], ALU.mult, reads=[t1, ri], writes=[o])
                    else:
                        ph.stt(o[:, 0:N], b1[:, 0:N], g, ri[:, 0:N], ALU.mult, ALU.mult, reads=[sm, ri], writes=[b1, o])
                else:
                    if isx:
                        ph.tt(t1[:, 0:N], b1[:, 0:N], c_t[:, 0:N], ALU.mult, reads=[c_t], writes=[b1, t1])
                        ph.tt(t2[:, 0:N], b2[:, 0:N], s_t[:, 0:N], ALU.mult, reads=[s_t], writes=[b2, t2])
                        ph.tt(o[:, 0:N], t1[:, 0:N], t2[:, 0:N], ALU.add, reads=[t1, t2], writes=[o], eng='gpsimd')
                    else:
                        ph.act(o[:, 0:N], b1[:, 0:N], AF.Copy, reads=[], writes=[b1, o])
                ph.dma(dst, o[:, 0:N], reads=[o], eng='sync')

            if pending:
                pending.pop(0)()
            pending.append(back)
        if part == 0:
            return
        while pending:
            pending.pop(0)()
        for s_ in range(N // 128):
            b6, b7 = banks[6], banks[7]
            ncol = 768 if need_q else 256
            for kc in range(8):
                lw = hv[:, kc, s_ * 128:(s_ + 1) * 128]
                ph.mm(b6[:, 0:min(ncol, 512)], lw, wv[:, kc, TM0:TM0 + min(ncol, 512)], start=(kc == 0), stop=(kc == 7),
                      reads=[wT, hT[kc]], bank=b6, signal=(kc == 7 and not need_q))
                if need_q:
                    ph.mm(b7[:, 0:256], lw, wv[:, kc, TM0 + 512:TM0 + 768], start=(kc == 0), stop=(kc == 7),
                          reads=[wT, hT[kc]], bank=b7, signal=(kc == 7))
            vt = vts[vt_i[0] % 4]
            zt = zts[vt_i[0] % 4]
            vt_i[0] += 1
            vin = b6[:, 0:256].rearrange("p (a g d) -> p a g d", a=2, g=2)
            vout = vt[:, :].rearrange("p (a w) -> p a w", a=2)
            ph.copy(vout[:, :, 0:64], vin[:, :, 0, :], reads=[], writes=[b6, vt], eng='scalar')
            ph.copy(vout[:, :, 192:256], vin[:, :, 1, :], reads=[], writes=[b6, vt], eng='scalar')
            ph.dma(k.vD[off + s_ * 128: off + (s_ + 1) * 128, :], vt[:, :], reads=[vt], eng='sync')
            if need_q:
                ph.copy(zt[:, 0:256], b6[:, 256:512], reads=[], writes=[b6, zt], eng='scalar')
                ph.copy(zt[:, 256:512], b7[:, 0:256], reads=[], writes=[b7, zt], eng='scalar')
                zd = k.zD if isx else k.zDc
                r0 = (t * 512 if isx else 0) + s_ * 128
                ph.dma(zd[:, r0:r0 + 128, :].rearrange("r p j -> p r j"), v3(zt[:, :], 256), reads=[zt], eng='sync')

    NTL = len(tiles)
    pre(0)
    pre(1)
    chain1(0)
    chain2(0)
    chain3(0)
    for ti in range(NTL):
        if ti + 1 < NTL:
            chain1(ti + 1)
        work(ti, 0)
        if ti + 1 < NTL:
            chain2(ti + 1)
            chain3(ti + 1)
        if ti + 2 < NTL:
            pre(ti + 2)
        work(ti, 1)
    ph.finish()


def attn_epilogue(ph, bO, bR, half, N, oe, rr, onesf, selhi, dst_ap):
    if half == 0:
        ph.copy(oe[0:65, 0:N], bO[0:65, 0:N], reads=[], writes=[bO, oe])
        ph.op('vector', lambda e: e.reciprocal(out=rr[64:65, 0:N], in_=oe[64:65, 0:N]), reads=[oe], writes=[rr])
        ph.mm(bR[0:64, 0:N], onesf[64:65, 0:64], rr[64:65, 0:N], start=True, stop=True, reads=[onesf, rr], bank=bR)
        ph.tt(dst_ap, oe[0:64, 0:N] if dst_ap.ndim == 2 else v3(oe[0:64, 0:N], 128),
              bR[0:64, 0:N] if dst_ap.ndim == 2 else v3(bR[0:64, 0:N], 128), ALU.mult, reads=[oe], writes=[bR, ph._dst])
    else:
        ph.copy(oe[:, 0:N], bO[:, 0:N], reads=[], writes=[bO, oe])
        ph.op('vector', lambda e: e.reciprocal(out=rr[0:1, 0:N], in_=oe[0:1, 0:N]), reads=[oe], writes=[rr])
        ph.mm(bR[:, 0:N], selhi[0:1, 0:128], rr[0:1, 0:N], start=True, stop=True, reads=[selhi, rr], bank=bR)
        ph.tt(dst_ap, oe[64:128, 0:N] if dst_ap.ndim == 2 else v3(oe[64:128, 0:N], 128),
              bR[64:128, 0:N] if dst_ap.ndim == 2 else v3(bR[64:128, 0:N], 128), ALU.mult, reads=[oe], writes=[bR, ph._dst])


def attn_consts(ph):
    onesf = ph.sb("onesf", [128, 128], F32)
    selhi = ph.sb("selhi", [128, 128], F32)
    ph.memset(onesf[:, :], 1.0, writes=[onesf])
    ph.memset(selhi[:, :], 0.0, writes=[selhi])
    ph.memset(selhi[0:1, 64:128], 1.0, writes=[selhi])
    return onesf, selhi


def phase_C(k, l):
    nc = k.nc
    S = k.S
    NT = S // 512
    NKC = 2 + S // 128
    ph = Phase(nc, "C%d" % l)
    pairs = []
    for i in range(4):
        nm = ph._nm("pp")
        pairs.append(ph.st.enter_context(nc.psum_tensor(nm, [128, 1024], F32)))
    NST, LA = 3, 2
    ST = [Tile(pairs[0], "st0"), Tile(pairs[1], "st1"), Tile(pairs[2], "st2")]
    bO = [Tile(pairs[3], "bO0"), Tile(pairs[3], "bO1")]
    kT = ph.sb("kT", [128, CTX + S], BF16)
    vA = ph.sb("vA", [128, NKC * 256], BF16)
    vAv = v3(vA[:, :], 256)
    qts = ph.sb("qt", [128, 3 * 512], BF16, n=2)
    pts = ph.sb("pt", [128, 1024], BF16, n=4)
    oes = ph.sb("oe", [128, 1024], F32, n=2)
    rrs = ph.sb("rr", [128, 1024], F32, n=2)
    ots = ph.sb("ot", [128, 3 * 512], BF16, n=2)
    kTk, vAk = [], []
    for c0 in range(0, NKC, 8):
        c1 = min(NKC, c0 + 8)
        kTk.append(Tile(kT.t, "kTk%d" % c0))
        vAk.append(Tile(vA.t, "vAk%d" % c0))
        ph.dma(kT[:, c0 * 128:c1 * 128], k.kaD[:, c0 * 128:c1 * 128], writes=[kTk[-1]])
        ph.dma(vAv[:, c0:c1, :], k.vD[c0 * 128:c1 * 128, 0:256].rearrange("(c p) f -> p c f", p=128), writes=[vAk[-1]])

    tiles = ([('ctx', 0)] if l == 0 else []) + [('x', t) for t in range(NT)]

    def tile_info(ti):
        kind, t = tiles[ti]
        N = 512 if kind == 'x' else CTX
        off = CTX + t * 512 if kind == 'x' else 0
        return N, off

    def load_q(ti):
        N, off = tile_info(ti)
        qt = qts[ti % 2]
        ph.dma(v3(qt[:, :], 512)[:, :, 0:N], k.qaD[:, :, off:off + N].rearrange("c p t -> p c t"), writes=[qt])

    steps = []
    for ti in range(len(tiles)):
        nkc = 2 if tiles[ti][0] == 'ctx' else NKC
        for j in range(3):
            for c in range(nkc):
                steps.append((ti, j, c, nkc))

    def emit_qk(si):
        ti, j, c, nkc = steps[si]
        N, off = tile_info(ti)
        qt = qts[ti % 2]
        st = ST[si % NST]
        qv = v3(qt[:, :], 512)
        ph.mm(st[:, 0:N], kT[0:64, c * 128:(c + 1) * 128], qv[0:64, j, 0:N], start=True, stop=True,
              reads=[kTk[c // 8], qt], bank=st, signal=False)
        ph.mm(st[:, 512:512 + N], kT[64:128, c * 128:(c + 1) * 128], qv[64:128, j, 0:N], start=True, stop=True,
              reads=[kTk[c // 8], qt], bank=st, signal=True)

    deferred = []
    pctok = Tile(None, 'pctok')
    load_q(0)
    if len(tiles) > 1:
        load_q(1)
    for s0_ in range(min(LA, len(steps))):
        emit_qk(s0_)
    grp = 0
    for si, (ti, j, c, nkc) in enumerate(steps):
        N, off = tile_info(ti)
        if c == 0 and j == 0 and ti >= 1 and ti + 1 < len(tiles):
            load_q(ti + 1)
        if si + LA < len(steps):
            emit_qk(si + LA)
        st = ST[si % NST]
        pt = pts[si % 4]
        ph.act(v3(pt[:, :], 512)[:, :, 0:N], v3(st[:, :], 512)[:, :, 0:N], AF.Exp, reads=[], writes=[st, pt], scale=0.125)
        ph.mm(bO[0][:, 0:N], vAv[:, c, 0:128], pt[:, 0:N], start=(c == 0), stop=(c == nkc - 1),
              reads=[vAk[c // 8], pt], bank=bO[0], signal=False)
        ph.mm(bO[1][:, 512:512 + N], vAv[:, c, 128:256], pt[:, 512:512 + N], start=(c == 0), stop=(c == nkc - 1),
              reads=[vAk[c // 8], pt], bank=bO[1], signal=True)
        if c == nkc - 1:
            ot = ots[ti % 2]
            oe = oes[grp % 2]
            rr = rrs[grp % 2]
            grp += 1
            ov = v3(ot[:, :], 512)

            ph.copy(oe[:, 0:N], bO[0][:, 0:N], reads=[], writes=[bO[0], oe], eng='vector')
            ph.copy(oe[:, 512:512 + N], bO[1][:, 512:512 + N], reads=[], writes=[bO[1], oe])
            ph.op('vector', lambda e, rr=rr, oe=oe, N_=N: e.reciprocal(out=rr[0:64, 0:N_], in_=oe[64:128, 0:N_]),
                  reads=[oe], writes=[rr])
            ph.op('vector', lambda e, rr=rr, oe=oe, N_=N: e.reciprocal(out=rr[64:128, 512:512 + N_], in_=oe[0:64, 512:512 + N_]),
                  reads=[oe], writes=[rr])
            ph.tt(ov[0:64, j, 0:N], oe[0:64, 0:N], rr[0:64, 0:N], ALU.mult, reads=[oe, rr], writes=[ot])
            trig = grp == (4 if l == 0 else 1)
            ph.tt(ov[64:128, j, 0:N], oe[64:128, 512:512 + N], rr[64:128, 512:512 + N], ALU.mult, reads=[oe, rr],
                  writes=[ot, pctok] if trig else [ot])
            if j == 2:
                ph.dma(k.oD[0:3, :, off:off + N].rearrange("c p t -> p c t"), ov[:, :, 0:N], reads=[ot], eng='sync')
            if trig:
                precast(ph, k, l, ffn=True, win=False, dep=[pctok])
                precast(ph, k, l + 1, ffn=False, win=True, dep=[pctok])
    for _, fn in deferred:
        fn()
    ph.finish()


def phase_D(k, l):
    nc = k.nc
    S = k.S
    NT = S // 512
    NQB = S // 128
    NKC = 2 + NQB
    ph = Phase(nc, "D%d" % l)
    pairs = []
    for i in range(4):
        nm = ph._nm("pp")
        pairs.append(ph.st.enter_context(nc.psum_tensor(nm, [128, 1024], F32)))
    NST, LA = 3, 2
    ST = [Tile(pairs[0], "st0"), Tile(pairs[1], "st1"), Tile(pairs[2], "st2")]
    bO = [Tile(pairs[3], "bO0"), Tile(pairs[3], "bO1")]
    kT = ph.sb("kT", [128, CTX + S], BF16)
    vC = ph.sb("vC", [128, NKC * 256], BF16)
    vCv = v3(vC[:, :], 256)
    qts = ph.sb("qt", [128, 3 * 512], BF16, n=2)
    pts = ph.sb("pt", [128, 1024], BF16, n=4)
    NR = 3
    oes = ph.sb("oe", [128, 1024], F32, n=NR)
    rrs = ph.sb("rr", [128, 1024], F32, n=NR)
    ots = ph.sb("ot", [128, 3 * 512], BF16, n=2)
    msk = ph.sb("msk", [128, 256], BF16)
    sk32 = ph.sb("sk32", [1, 6 * 256], F32)
    skb = ph.sb("skb", [1, 6 * 256], BF16)
    esel = ph.sb("esel", [1, 256], BF16)
    kTk, vCk = [], []
    for c0 in range(0, NKC, 8):
        c1 = min(NKC, c0 + 8)
        kTk.append(Tile(kT.t, "kTk%d" % c0))
        vCk.append(Tile(vC.t, "vCk%d" % c0))
        ph.dma(kT[:, c0 * 128:c1 * 128], k.kcD[:, c0 * 128:c1 * 128], writes=[kTk[-1]])
        ph.dma(vCv[:, c0:c1, :], k.vD[c0 * 128:c1 * 128, 256:512].rearrange("(c p) f -> p c f", p=128), writes=[vCk[-1]])
    ph.dma(msk[:, :], k.cbf[:, CB_MASK:CB_MASK + 256], writes=[msk], eng='gpsimd')
    ph.dma(sk32[:, :], k.sinkrow[l, :, :], writes=[sk32])
    ph.act(skb[:, :], sk32[:, :], AF.Exp, reads=[sk32], writes=[skb])
    ph.memset(esel[:, :], 0.0, writes=[esel])
    ph.memset(esel[0:1, 64:128], 1.0, writes=[esel])
    ph.memset(esel[0:1, 128:192], 1.0, writes=[esel])
    skv = v3(skb[:, :], 256)
    mv = v3(msk[:, :], 128)

    tiles = ([('ctx', 0)] if l == 0 else []) + [('x', t) for t in range(NT)]

    def tile_info(ti):
        kind, t = tiles[ti]
        N = 512 if kind == 'x' else CTX
        off = CTX + t * 512 if kind == 'x' else 0
        return kind, t, N, off

    def load_q(ti):
        kind, t, N, off = tile_info(ti)
        qt = qts[ti % 2]
        ph.dma(v3(qt[:, :], 512)[:, :, 0:N], k.qcD[:, :, off:off + N].rearrange("c p t -> p c t"), writes=[qt])

    groups = []
    for ti in range(len(tiles)):
        kind, t, N, off = tile_info(ti)
        qt, ot = qts[ti % 2], ots[ti % 2]
        qv, ov = v3(qt[:, :], 512), v3(ot[:, :], 512)
        if kind == 'ctx':
            for j in range(3):
                g = dict(ti=ti, NN=CTX, n3=1, nq=CTX, chunks=[(0, 0, None), (128, 1, None)],
                         q=[qv[0:64, j, 0:CTX], qv[64:128, j, 0:CTX]],
                         sink=[skv[0:1, j, 0:CTX], skv[0:1, 3 + j, 0:CTX]],
                         dst=[ov[0:64, j, 0:CTX], ov[64:128, j, 0:CTX]], last=(j == 2))
                groups.append(g)
        else:
            for nb in range(4):
                n = t * 4 + nb
                chunks = [(0, 0, None), (128, 1, None)]
                if n - 1 >= 0:
                    chunks.append((CTX + (n - 1) * 128, 2 + n - 1, 0))
                chunks.append((CTX + n * 128, 2 + n, None))
                if n + 1 < NQB:
                    chunks.append((CTX + (n + 1) * 128, 2 + n + 1, 1))
                cs = slice(nb * 128, (nb + 1) * 128)
                g = dict(ti=ti, NN=384, n3=3, nq=128, chunks=chunks,
                         q=[qv[0:64, :, cs], qv[64:128, :, cs]],
                         sink=[skv[0:1, 0:3, 0:128], skv[0:1, 3:6, 0:128]],
                         dst=[ov[0:64, :, cs], ov[64:128, :, cs]], last=(nb == 3))
                groups.append(g)
    steps = []
    for gi, g in enumerate(groups):
        for ci in range(len(g['chunks'])):
            steps.append((gi, ci))

    def emit_qk(si):
        gi, ci = steps[si]
        g = groups[gi]
        NN = g['NN']
        kcol = g['chunks'][ci][0]
        st = ST[si % NST]
        qt = qts[g['ti'] % 2]
        kt_ = kTk[(kcol // 128) // 8]
        ph.mm(st[:, 0:NN], kT[0:64, kcol:kcol + 128], g['q'][0], start=True, stop=True, reads=[kt_, qt], bank=st, signal=False)
        ph.mm(st[:, 512:512 + NN], kT[64:128, kcol:kcol + 128], g['q'][1], start=True, stop=True, reads=[kt_, qt], bank=st,
              signal=True)

    deferred = []
    load_q(0)
    if len(tiles) > 1:
        load_q(1)
    for s0_ in range(min(LA, len(steps))):
        emit_qk(s0_)
    loaded = {0, 1}
    for si, (gi, ci) in enumerate(steps):
        g = groups[gi]
        NN, n3, nq, ti = g['NN'], g['n3'], g['nq'], g['ti']
        kind, t, N, off = tile_info(ti)
        nch = len(g['chunks'])
        kcol, vci, mi = g['chunks'][ci]
        if ci == 0 and ti + 1 < len(tiles) and (ti + 1) not in loaded and (gi == 0 or groups[gi - 1]['ti'] != ti):
            loaded.add(ti + 1)
            load_q(ti + 1)
        if si + LA < len(steps):
            emit_qk(si + LA)
        st = ST[si % NST]
        pt = pts[si % 4]
        ph.act(v3(pt[:, :], 512)[:, :, 0:NN], v3(st[:, :], 512)[:, :, 0:NN], AF.Exp, reads=[], writes=[st, pt], scale=0.125)
        if mi is not None:
            pv4 = pt[:, :].rearrange("p (h a q) -> p h a q", h=2, a=4)[:, :, 0:3, :]
            mb = mv[:, mi, :].unsqueeze(1).unsqueeze(1).to_broadcast([128, 2, 3, 128])
            ph.tt(pv4, pv4, mb, ALU.mult, reads=[msk], writes=[pt], eng='gpsimd')
        ph.mm(bO[0][:, 0:NN], vCv[:, vci, 0:128], pt[:, 0:NN], start=(ci == 0), stop=False,
              reads=[vCk[vci // 8], pt], bank=bO[0], signal=False)
        ph.mm(bO[1][:, 512:512 + NN], vCv[:, vci, 128:256], pt[:, 512:512 + NN], start=(ci == 0), stop=False,
              reads=[vCk[vci // 8], pt], bank=bO[1], signal=(ci < nch - 1))
        if ci == nch - 1:
            ph.mm(bO[0][:, 0:NN], esel[0:1, 0:128], g['sink'][0], start=False, stop=True, reads=[esel, skb], bank=bO[0],
                  signal=False)
            ph.mm(bO[1][:, 512:512 + NN], esel[0:1, 128:256], g['sink'][1], start=False, stop=True, reads=[esel, skb],
                  bank=bO[1], signal=True)
            oe, rr = oes[gi % NR], rrs[gi % NR]
            ot = ots[ti % 2]
            ph.copy(oe[:, 0:NN], bO[0][:, 0:NN], reads=[], writes=[bO[0], oe], eng='scalar')
            ph.copy(oe[:, 512:512 + NN], bO[1][:, 512:512 + NN], reads=[], writes=[bO[1], oe], eng='scalar')
            ph.op('vector', lambda e, rr=rr, oe=oe, N_=NN: e.reciprocal(out=rr[0:64, 0:N_], in_=oe[64:128, 0:N_]),
                  reads=[oe], writes=[rr])
            ph.op('vector', lambda e, rr=rr, oe=oe, N_=NN: e.reciprocal(out=rr[64:128, 512:512 + N_], in_=oe[0:64, 512:512 + N_]),
                  reads=[oe], writes=[rr])

            def vw(ap, g=g, nq=nq):
                return ap if g['n3'] == 1 else v3(ap, nq)
            ph.tt(g['dst'][0], vw(oe[0:64, 0:NN]), vw(rr[0:64, 0:NN]), ALU.mult, reads=[oe, rr], writes=[ot])
            ph.tt(g['dst'][1], vw(oe[64:128, 512:512 + NN]), vw(rr[64:128, 512:512 + NN]), ALU.mult, reads=[oe, rr], writes=[ot])
            if g['last']:
                ov = v3(ot[:, :], 512)
                ph.dma(k.oD[5:8, :, off:off + N].rearrange("c p t -> p c t"), ov[:, :, 0:N], reads=[ot], eng='sync')
    for _, fn in deferred:
        fn()
    ph.finish()


CB_M1, CB_M2, CB_C128, CB_S128, CB_C256, CB_S256, CB_MASK = 0, 128, 256, 384, 512, 1024, 1536
CBW = 1536 + 256


def phase_E(k, l):
    nc = k.nc
    S = k.S
    L1 = S // 128
    P2 = 2 * L1
    ph = Phase(nc, "E%d" % l)
    banks = ph.psum_banks(8)
    cb = ph.sb("cb", [128, CBW], BF16)
    tw = ph.sb("tw", [128, 256], F32)
    ph.dma(cb[:, :], k.cbf[:, :], writes=[cb], eng='gpsimd')
    ph.dma(tw[:, :], k.cmat[:, 256:512], writes=[tw])
    SL = 32
    vs = ph.sb("v", [128, SL * 256], BF16, n=2)
    Zs = ph.sb("Zs", [128, SL * 256], BF16, n=2)
    t1s = ph.sb("t1", [128, 512], F32, n=2)
    t2s = ph.sb("t2", [128, 512], F32, n=2)
    Zt = ph.sb("Zt", [128, 2 * L1 * 256], BF16)
    ofs = ph.sb("of", [128, S], BF16, n=2)
    zdv = k.zD.rearrange("r (a b) j -> r a (b j)", b=128)
    ZDv = k.ZD.rearrange("r a b j -> (r a) (b j)")
    g = 0
    zdb = [Tile(None, "ZDslab%d" % i) for i in range(128 // SL)]
    for sl in range(128 // SL):
        v = vs[sl % 2]
        Zo = Zs[sl % 2]
        for r in range(2):
            ph.dma(v[r * L1:(r + 1) * L1, :], zdv[r, :, sl * SL * 256:(sl + 1) * SL * 256], writes=[v])
        for cg in range(SL // 2):
            cols = slice(cg * 512, (cg + 1) * 512)
            l2a = sl * SL + cg * 2
            bY, bW = banks[(g % 2) * 2], banks[(g % 2) * 2 + 1]
            t1, t2 = t1s[g % 2], t2s[g % 2]
            g += 1
            ph.mm(bY[0:P2, :], cb[0:P2, CB_M1:CB_M1 + P2], v[0:P2, cols], start=True, stop=True, reads=[cb, v], bank=bY)
            ph.mm(bW[0:P2, :], cb[0:P2, CB_M2:CB_M2 + P2], v[0:P2, cols], start=True, stop=True, reads=[cb, v], bank=bW)
            ph.tt(v3(t1[0:P2, :], 256), v3(bY[0:P2, :], 256), tw[0:P2, l2a:l2a + 2].unsqueeze(2).to_broadcast([P2, 2, 256]),
                  ALU.mult, reads=[tw], writes=[bY, t1])
            ph.tt(v3(t2[0:P2, :], 256), v3(bW[0:P2, :], 256),
                  tw[0:P2, 128 + l2a:128 + l2a + 2].unsqueeze(2).to_broadcast([P2, 2, 256]),
                  ALU.mult, reads=[tw], writes=[bW, t2])
            ph.tt(Zo[0:P2, cols], t1[0:P2, :], t2[0:P2, :], ALU.add, reads=[t1, t2], writes=[Zo], eng='gpsimd')
        ph.dma(ZDv[:, sl * SL * 256:(sl + 1) * SL * 256], Zo[0:P2, :], reads=[Zo], writes=[zdb[sl]], eng='gpsimd')
    Zv = Zt[:, :].rearrange("p (r a j) -> p r a j", r=2, a=L1)
    KG = min(16, L1)
    ztk = {}
    for a0 in range(0, L1, KG):
        for r in range(2):
            ztk[(r, a0)] = Tile(Zt.t, "zt%d_%d" % (r, a0))
            ph.dma(Zv[:, r, a0:a0 + KG, :], k.ZD[r, a0:a0 + KG, :, :].rearrange("a b j -> b a j"), reads=zdb,
                   writes=[ztk[(r, a0)]])
    scale = 1.0 / math.sqrt(S * 64.0)
    gi = 0
    for a0 in range(0, L1, 4):
        for jc in range(2):
            bk = banks[4 + gi % 4]
            for q in range(4):
                a = a0 + q
                ph.mm(bk[:, q * 128:(q + 1) * 128], Zv[:, 0, a, jc * 128:(jc + 1) * 128], cb[:, CB_C128:CB_C128 + 128],
                      start=True, stop=False, reads=[ztk[(0, (a // KG) * KG)], cb], bank=bk, signal=False)
                ph.mm(bk[:, q * 128:(q + 1) * 128], Zv[:, 1, a, jc * 128:(jc + 1) * 128], cb[:, CB_S128:CB_S128 + 128],
                      start=False, stop=True, reads=[ztk[(1, (a // KG) * KG)], cb], bank=bk, signal=(q == 3))
            of = ofs[jc]
            outv = of[:, :].rearrange("p (b a) -> p b a", a=L1)[:, :, a0:a0 + 4]
            inv = bk[:, :].rearrange("p (q b) -> p b q", q=4)
            if gi % 2 == 0:
                ph.act(outv, inv, AF.Copy, reads=[], writes=[bk, of], scale=scale)
            else:
                ph.op('vector', lambda e, outv=outv, inv=inv: e.tensor_scalar(out=outv, in0=inv, scalar1=scale, scalar2=None,
                                                                             op0=ALU.mult), reads=[], writes=[bk, of])
            gi += 1
    for jc in range(2):
        ph.dma(k.oD[3 + jc, :, CTX:CTX + S], ofs[jc][:, :], reads=[ofs[jc]], eng='gpsimd')
    if l == 0:
        zc = ph.sb("zc", [128, 2 * 2 * 256], BF16)
        zcv = zc[:, :].rearrange("p (c r j) -> p c r j", c=2, r=2)
        ofc = ph.sb("ofc", [128, 2 * 256], BF16)
        for c in range(2):
            ph.dma(zcv[:, c, :, :], k.zDc[:, c * 128:(c + 1) * 128, :].rearrange("r p j -> p r j"), writes=[zc])
        sc_c = 1.0 / math.sqrt(CTX * 64.0)
        for jc in range(2):
            bk = banks[jc]
            n = 0
            for c in range(2):
                for r in range(2):
                    base = (CB_C256 if r == 0 else CB_S256) + c * 256
                    ph.mm(bk[:, 0:256], zcv[:, c, r, jc * 128:(jc + 1) * 128], cb[:, base:base + 256],
                          start=(n == 0), stop=(n == 3), reads=[zc, cb], bank=bk, signal=(n == 3))
                    n += 1
            ph.act(ofc[:, jc * 256:(jc + 1) * 256], bk[:, 0:256], AF.Copy, reads=[], writes=[bk, ofc], scale=sc_c)
            ph.dma(k.oD[3 + jc, :, 0:CTX], ofc[:, jc * 256:(jc + 1) * 256], reads=[ofc], eng='gpsimd')
    ph.finish()


def phase_G(k, l):
    nc = k.nc
    S = k.S
    N = 256
    last = (l == DEPTH - 1)
    ph = Phase(nc, "G%d" % l)
    banks = ph.psum_banks(8)
    pp = ph.wrap(k.pp, "pp")
    sm = ph.wrap(k.sm, "sm")
    wo = ph.sb("wo", [128, 8 * D], BF16)
    wg = ph.sb("wg", [128, 8 * DFF], BF16)
    wu = ph.sb("wu", [128, 8 * DFF], BF16)
    wd = ph.sb("wd", [128, NFC * D], BF16)
    wov, wgv, wuv, wdv = v3(wo[:, :], D), v3(wg[:, :], DFF), v3(wu[:, :], DFF), v3(wd[:, :], D)
    xts = ph.sb("xt", [128, 8 * N], F32, n=2)
    ots = ph.sb("ot", [128, 8 * N], BF16, n=2)
    xn = ph.sb("xn", [128, 8 * N], F32)
    hh = ph.sb("hh", [128, 8 * N], BF16)
    sq = hh
    aa = ph.sb("aa", [128, NFC * N], BF16)
    sgs = ph.sb("sg", [128, N], F32, n=2)
    ones = ph.sb("ones", [128, 128], BF16)
    lnt = ph.sb("lnt", [128, N], F32)
    rstd = ph.sb("rstd", [128, N], F32)
    ph.memset(ones[:, :], 1.0, writes=[ones])
    ph.dma(wov, k.woB[l].rearrange("(c p) f -> p c f", p=128), writes=[wo])
    early_load = [True]
    NWG = 4
    WGC = DFF // NWG
    wgs = [Tile(wg.t, "wg%d" % i) for i in range(NWG)]
    wus = [Tile(wu.t, "wu%d" % i) for i in range(NWG)]

    def emit_big_weights():
        for gI in range(NWG):
            cs = slice(gI * WGC, (gI + 1) * WGC)
            ph.dma(wgv[:, :, cs], k.wgB[l, :, cs].rearrange("(c p) f -> p c f", p=128), writes=[wgs[gI]])
            ph.dma(wuv[:, :, cs], k.wuB[l, :, cs].rearrange("(c p) f -> p c f", p=128), writes=[wus[gI]])
        for c0 in range(0, NFC, 11):
            ph.dma(wdv[:, c0:c0 + 11, :], k.wdB[l, c0 * 128:(c0 + 11) * 128, :].rearrange("(c p) f -> p c f", p=128),
                   writes=[wd])

    tiles = [('x', t) for t in range(S // N)] + ([('ctx', 0)] if not last else [])
    lnf = ph.sb("lnf", [128, N], F32)
    rstdf = ph.sb("rstdf", [128, N], F32)
    hv = v3(hh[:, :], N)
    nv = v3(xn[:, :], N)
    av = v3(aa[:, :], N)
    sv = v3(sq[:, :], N)
    bi = [0]
    xnk = [Tile(xn.t, "xnk%d" % c) for c in range(8)]
    hhk = [Tile(hh.t, "hhk%d" % c) for c in range(8)]

    def info(ti):
        kind, t = tiles[ti]
        isx = kind == 'x'
        who = 0 if isx else 1
        xt, ot = xts[ti % 2], ots[ti % 2]
        return kind, t, isx, who, xt, ot, v3(xt[:, :], N), v3(ot[:, :], N)

    def load(ti):
        kind, t, isx, who, xt, ot, xv, ov = info(ti)
        off = CTX + t * N if isx else 0
        if isx:
            src = (k.xT if l == 0 else k.xD1)[:, t * N:(t + 1) * N]
        else:
            src = k.ctxT[:, :]
        ph.dma(xv, src.rearrange("(c p) t -> p c t", p=128), writes=[xt])
        ph.dma(ov, k.oD[:, :, off:off + N].rearrange("c p t -> p c t"), writes=[ot])

    def st_A(ti):
        kind, t, isx, who, xt, ot, xv, ov = info(ti)
        gt1 = MOD(l, who, 2)
        for dc in range(8):
            bk = banks[1 + bi[0] % 2]
            bi[0] += 1
            for mc in range(8):
                ph.mm(bk[:, 0:N], wov[:, mc, dc * 128:(dc + 1) * 128], ov[:, mc, :], start=(mc == 0), stop=(mc == 7),
                      reads=[wo, ot], bank=bk, signal=(mc == 7))
            ph.stt(xv[:, dc, :], bk[:, 0:N], pp[:, gt1 + dc:gt1 + dc + 1], xv[:, dc, :], ALU.mult, ALU.add,
                   reads=[pp], writes=[bk, xt])

    def st_Bn(ti):
        kind, t, isx, who, xt, ot, xv, ov = info(ti)
        ph.act(sv, xv, AF.Square, reads=[xt], writes=hhk)

    def st_Bs(ti):
        bS = banks[0]
        for kc in range(8):
            ph.mm(bS[:, 0:N], ones[:, :], sv[:, kc, :], start=(kc == 0), stop=(kc == 7), reads=[ones, hhk[kc]], bank=bS,
                  signal=(kc == 7))

    def st_Bc(ti):
        kind, t, isx, who, xt, ot, xv, ov = info(ti)
        sh2, gs2 = MOD(l, who, 3), GS(l, who, 1)
        bS = banks[0]
        rstd_from_sums(ph, bS[:, 0:N], D, lnt[:, :], rstd[:, :], bS, lnt, rstd)
        for kc in range(8):
            ph.tt(nv[:, kc, :], xv[:, kc, :], rstd[:, :], ALU.mult, reads=[xt, rstd], writes=[xnk[kc]])
            if kc % 3 == 2:
                ph.act(hv[:, kc, :], nv[:, kc, :], AF.Identity, reads=[xnk[kc], pp], writes=[hhk[kc]],
                       scale=pp[:, gs2 + kc:gs2 + kc + 1], bias=pp[:, sh2 + kc:sh2 + kc + 1])
            else:
                ph.ts(hv[:, kc, :], nv[:, kc, :], pp[:, gs2 + kc:gs2 + kc + 1], pp[:, sh2 + kc:sh2 + kc + 1], ALU.mult,
                      ALU.add, reads=[xnk[kc], pp], writes=[hhk[kc]], eng='gpsimd')

    def st_C(ti, fcs=range(NFC)):
        for fc in fcs:
            bG, bU = banks[3 + (fc % 2) * 2], banks[4 + (fc % 2) * 2]
            sg = sgs[fc % 2]
            for kc in range(8):
                ph.mm(bG[:, 0:N], wgv[:, kc, fc * 128:(fc + 1) * 128], hv[:, kc, :], start=(kc == 0), stop=(kc == 7),
                      reads=[wgs[(fc * 128) // WGC], wgs[(fc * 128 + 127) // WGC], hhk[kc]], bank=bG, signal=(kc == 7))
            for kc in range(8):
                ph.mm(bU[:, 0:N], wuv[:, kc, fc * 128:(fc + 1) * 128], hv[:, kc, :], start=(kc == 0), stop=(kc == 7),
                      reads=[wus[(fc * 128) // WGC], wus[(fc * 128 + 127) // WGC], hhk[kc]], bank=bU, signal=(kc == 7))
            ph.act(sg[:, :], bG[:, 0:N], AF.Silu, reads=[], writes=[bG, sg])
            ph.tt(av[:, fc, :], sg[:, :], bU[:, 0:N], ALU.mult, reads=[sg], writes=[bU, aa])

    def st_D(ti, dcs):
        kind, t, isx, who, xt, ot, xv, ov = info(ti)
        gt2 = MOD(l, who, 5)
        for dc in dcs:
            bk = banks[1 + bi[0] % 2]
            bi[0] += 1
            for fc in range(NFC):
                ph.mm(bk[:, 0:N], wdv[:, fc, dc * 128:(dc + 1) * 128], av[:, fc, :], start=(fc == 0), stop=(fc == NFC - 1),
                      reads=[wd, aa], bank=bk, signal=(fc == NFC - 1))
            ph.stt(xv[:, dc, :], bk[:, 0:N], pp[:, gt2 + dc:gt2 + dc + 1], xv[:, dc, :], ALU.mult, ALU.add,
                   reads=[pp], writes=[bk, xt])

    def st_out(ti, part):
        kind, t, isx, who, xt, ot, xv, ov = info(ti)
        if last:
            sv2 = v3(aa[:, (NFC - 8) * N:NFC * N], N)
            if part == 0:
                ph.act(sv2, xv, AF.Square, reads=[xt], writes=[aa])
                return
            bS = banks[7]
            for kc in range(8):
                ph.mm(bS[:, 0:N], ones[:, :], sv2[:, kc, :], start=(kc == 0), stop=(kc == 7), reads=[ones, aa], bank=bS,
                      signal=(kc == 7))
            rstd_from_sums(ph, bS[:, 0:N], D, lnf[:, :], rstdf[:, :], bS, lnf, rstdf)
            for kc in range(8):
                ph.stt(nv[:, kc, :], xv[:, kc, :], sm[:, SM_GFIN + kc:SM_GFIN + kc + 1], rstdf[:, :], ALU.mult, ALU.mult,
                       reads=[xt, sm, rstdf], writes=[xnk[kc]])
            ph.dma(k.yT[:, t * N:(t + 1) * N].rearrange("(c p) t -> p c t", p=128), nv, reads=xnk, eng='sync')
        else:
            if part == 0:
                return
            dst = k.xD1[:, t * N:(t + 1) * N] if isx else k.ctxD1[:, :]
            ph.dma(dst.rearrange("(c p) t -> p c t", p=128), xv, reads=[xt], eng='sync')

    NTL = len(tiles)
    load(0)
    if NTL > 1:
        load(1)
    emit_big_weights()
    st_A(0)
    st_Bn(0)
    st_Bs(0)
    st_Bc(0)
    for ti in range(NTL):
        st_C(ti, range(0, 2))
        if ti >= 1:
            st_out(ti - 1, 1)
            if ti + 1 < NTL:
                load(ti + 1)
        st_C(ti, range(2, NFC))
        if ti + 1 < NTL:
            st_A(ti + 1)
            st_Bn(ti + 1)
            st_D(ti, range(0, 2))
            st_Bs(ti + 1)
            st_Bc(ti + 1)
            st_D(ti, range(2, 8))
        else:
            st_D(ti, range(0, 8))
        st_out(ti, 0)
    st_out(NTL - 1, 1)
    ph.finish()


def build_program(S):
    nc = bass.Bass("TRN2", target_bir_lowering=False)
    k = K()
    k.nc = nc
    k.S = S
    L1 = S // 128

    def din(name, shape, dt=F32):
        return nc.dram_tensor(name, list(shape), dt, kind="ExternalInput").ap()

    def dscr(name, shape, dt):
        return nc.dram_tensor(name, list(shape), dt).ap()

    k.xT = din("xT", [D, S])
    k.ctxT = din("ctxT", [D, CTX])
    k.cvec = din("cvec", [128, 16])
    k.small = din("small", [128, SMW])
    k.w_ada = din("w_ada", [DEPTH, D, 6 * D])
    k.w_in_p = din("w_in_p", [DEPTH, D, 2304])
    k.w_in_fT = din("w_in_fT", [DEPTH, 256, D])
    k.w_out_p = din("w_out_p", [DEPTH, D, D])
    k.w_gate = din("w_gate", [DEPTH, D, DFF])
    k.w_up = din("w_up", [DEPTH, D, DFF])
    k.w_down = din("w_down", [DEPTH, DFF, D])
    k.sinkrow = din("sinkrow", [DEPTH, 1, 6 * 256])
    k.ropeC = din("ropeC", [128, S])
    k.ropeS = din("ropeS", [128, S])
    k.cmat = din("cmat", [128, 512])
    k.cbf = din("cbf", [128, CBW])
    k.yT = nc.dram_tensor("yT", [D, S], F32, kind="ExternalOutput").ap()

    k.wfD = dscr("wfD", [DEPTH, D, 512], BF16)
    k.qaD = dscr("qaD", [3, 128, CTX + S], BF16)
    k.qcD = dscr("qcD", [3, 128, CTX + S], BF16)
    k.kaD = dscr("kaD", [128, CTX + S], BF16)
    k.kcD = dscr("kcD", [128, CTX + S], BF16)
    k.vD = dscr("vD", [CTX + S, VW], BF16)
    k.zD = dscr("zD", [2, S, 256], BF16)
    k.zDc = dscr("zDc", [2, CTX, 256], BF16)
    k.ZD = dscr("ZD", [2, L1, 128, 256], BF16)
    k.oD = dscr("oD", [8, 128, CTX + S], BF16)
    k.woB = dscr("woB", [DEPTH, D, D], BF16)
    k.wgB = dscr("wgB", [DEPTH, D, DFF], BF16)
    k.wuB = dscr("wuB", [DEPTH, D, DFF], BF16)
    k.wdB = dscr("wdB", [DEPTH, DFF, D], BF16)
    k.winB = dscr("winB", [DEPTH, D, 2304], BF16)
    k.xD1 = dscr("xD1", [D, S], F32)
    k.ctxD1 = dscr("ctxD1", [D, CTX], F32)

    with contextlib.ExitStack() as st:
        Phase.state = SemState(nc, st)
        k.pp = st.enter_context(nc.sbuf_tensor("pp_persist", [128, PPW], F32))
        k.sm = st.enter_context(nc.sbuf_tensor("sm_persist", [128, SMW], F32))
        import os
        sel = os.environ.get("KPHASES", "")
        for l in range(DEPTH):
            for nm, f in (("A", phase_A), ("B", phase_B), ("C", phase_C), ("D", phase_D), ("E", phase_E), ("G", phase_G)):
                if sel and ("%s%d" % (nm, l)) not in sel.split(","):
                    continue
                f(k, l)
    return nc


def host_constants(S):
    L1 = S // 128
    f32 = np.float32
    tok = np.arange(S)
    row = (tok // 64).astype(f32)
    col = (tok % 64).astype(f32)
    inv_freq = (np.float32(10000.0) ** (-np.arange(0, 32, 2, dtype=f32) / np.float32(32.0))).astype(f32)
    ang = np.concatenate([row[:, None] * inv_freq, col[:, None] * inv_freq], axis=-1).astype(f32)
    cos = np.cos(ang).astype(f32).T
    sin = np.sin(ang).astype(f32).T
    ropeC = np.concatenate([cos, cos, cos, cos], axis=0)
    ropeS = np.concatenate([-sin, sin, -sin, sin], axis=0)
    c = np.arange(64)
    a64 = 2 * np.pi * np.outer(c, c) / 64.0
    C64, S64 = np.cos(a64), np.sin(a64)
    Z = np.zeros((64, 64))
    cmat = np.zeros((128, 512), f32)
    cmat[:, 0:128] = np.block([[C64, Z], [Z, C64]])
    cmat[:, 128:256] = np.block([[-S64, Z], [Z, -S64]])
    k1 = np.arange(L1)
    l2 = np.arange(128)
    at = 2 * np.pi * np.outer(k1, l2) / S
    cmat[0:2 * L1, 256:384] = np.concatenate([np.cos(at), np.cos(at)], axis=0)
    cmat[0:2 * L1, 384:512] = np.concatenate([np.sin(at), np.sin(at)], axis=0)
    cbf = np.zeros((128, CBW), f32)
    a1 = 2 * np.pi * np.outer(k1, k1) / L1
    Cc, Sc = np.cos(a1), np.sin(a1)
    M1 = np.block([[Cc, Sc], [-Sc, Cc]])
    M2 = np.block([[-Sc, Cc], [-Cc, -Sc]])
    cbf[0:2 * L1, CB_M1:CB_M1 + 2 * L1] = M1.T
    cbf[0:2 * L1, CB_M2:CB_M2 + 2 * L1] = M2.T
    a128 = 2 * np.pi * np.outer(l2, l2) / 128.0
    cbf[:, CB_C128:CB_C128 + 128] = np.cos(a128)
    cbf[:, CB_S128:CB_S128 + 128] = np.sin(a128)
    n256 = np.arange(256)
    a256 = 2 * np.pi * np.outer(n256, n256) / 256.0
    C256, S256 = np.cos(a256), np.sin(a256)
    for cch in range(2):
        cbf[:, CB_C256 + cch * 256:CB_C256 + (cch + 1) * 256] = C256[cch * 128:(cch + 1) * 128, :]
        cbf[:, CB_S256 + cch * 256:CB_S256 + (cch + 1) * 256] = S256[cch * 128:(cch + 1) * 128, :]
    a = np.arange(128)[:, None]
    i = np.arange(128)[None, :]
    cbf[:, CB_MASK:CB_MASK + 128] = (a >= i)
    cbf[:, CB_MASK + 128:CB_MASK + 256] = (a <= i)
    return ropeC.astype(f32), ropeS.astype(f32), cmat, cbf.astype(f32)


def host_layout(inp, S):
    f32 = np.float32
    g = lambda n: np.asarray(inp[n], dtype=f32)
    w_in = g('w_in')

    def pair(base, j):
        return np.concatenate([np.arange(base + 64 * j, base + 64 * j + 64), np.arange(base + 64 * (3 + j), base + 64 * (3 + j) + 64)])

    def swap(idx):
        idx = idx.reshape(-1, 64)
        return np.concatenate([idx[:, 32:], idx[:, :32]], axis=1).reshape(-1)

    cols = []
    qa = [pair(0, j) for j in range(3)]
    ka = np.arange(384, 512)
    qc = [pair(896, j) for j in range(3)]
    kc = np.arange(1280, 1408)
    cols += qa + [swap(q) for q in qa] + [ka, swap(ka)] + qc + [swap(q) for q in qc] + [kc, swap(kc)]
    cols += [np.arange(512, 640), np.arange(1408, 1536)]
    cols = np.concatenate(cols)
    assert cols.shape[0] == 2304
    w_in_p = np.ascontiguousarray(w_in[:, :, cols])
    w_in_fT = np.ascontiguousarray(np.transpose(w_in[:, :, 640:896], (0, 2, 1)))
    rows = np.concatenate([pair(0, j) for j in range(3)] + [np.arange(384, 640)] + [pair(640, j) for j in range(3)])
    w_out_p = np.ascontiguousarray(g('w_out')[:, rows, :])

    def pl(v):
        return np.ascontiguousarray(v.reshape(8, 128).T)

    small = np.zeros((128, SMW), f32)
    b_ada, g_mix, g_ffn = g('b_ada'), g('g_mix'), g('g_ffn')
    qn, kn = g('q_norm'), g('k_norm')
    p = np.arange(128) % 64
    ps = (p + 32) % 64
    for l in range(DEPTH):
        small[:, SM_BADA(l):SM_BADA(l) + 48] = b_ada[l].reshape(48, 128).T
        small[:, SM_GMIX(l):SM_GMIX(l) + 8] = pl(g_mix[l])
        small[:, SM_GFFN(l):SM_GFFN(l) + 8] = pl(g_ffn[l])
        small[:, SM_QKN(l) + 0] = qn[l][p]
        small[:, SM_QKN(l) + 1] = qn[l][ps]
        small[:, SM_QKN(l) + 2] = kn[l][p]
        small[:, SM_QKN(l) + 3] = kn[l][ps]
    small[:, SM_GFIN:SM_GFIN + 8] = pl(g('g_final'))
    sinkrow = np.ascontiguousarray(np.repeat(g('sink')[:, None, :, None], 256, axis=3).reshape(DEPTH, 1, 6 * 256))
    ropeC, ropeS, cmat, cbf = host_constants(S)
    shared = dict(small=small, w_ada=g('w_ada'), w_in_p=w_in_p, w_in_fT=w_in_fT, w_out_p=w_out_p,
                  w_gate=g('w_gate'), w_up=g('w_up'), w_down=g('w_down'), sinkrow=sinkrow,
                  ropeC=ropeC, ropeS=ropeS, cmat=cmat, cbf=cbf)
    x, c, ctx, c_ctx = g('x'), g('c'), g('ctx'), g('c_ctx')
    B = x.shape[0]
    maps = []
    for b in range(B):
        cvec = np.zeros((128, 16), f32)
        cvec[:, 0::2] = pl(c[b])
        cvec[:, 1::2] = pl(c_ctx)
        m = dict(shared)
        m['xT'] = np.ascontiguousarray(x[b].T)
        m['ctxT'] = np.ascontiguousarray(ctx[b].T)
        m['cvec'] = cvec
        maps.append(m)
    return maps


_CACHE = {}


def run(inp, S):
    maps = host_layout(inp, S)
    if S not in _CACHE:
        _CACHE[S] = build_program(S)
    nc = _CACHE[S]
    res = run_bass_kernel_spmd(nc, maps, core_ids=list(range(len(maps))))
    out = np.stack([np.ascontiguousarray(r["yT"].T) for r in res.results], axis=0)
    return out.astype(np.float32)


def kernel(**inputs):
    return run(inputs, 8192)
```

```python
import contextlib
import math
import numpy as np
import concourse.bass as bass
import concourse.mybir as mybir
from concourse.bass_utils import run_bass_kernel_spmd

F32 = mybir.dt.float32
BF16 = mybir.dt.bfloat16
AF = mybir.ActivationFunctionType
ALU = mybir.AluOpType

D = 1024
DFF = 2816
NFC = DFF // 128
CTX = 256
DEPTH = 2
EPS = 1e-6
VW = 512
ENGS = ['tensor', 'vector', 'scalar', 'gpsimd', 'sync']


class Buf:
    __slots__ = ('name', 'last_w', 'readers')

    def __init__(self, name):
        self.name = name
        self.last_w = None
        self.readers = {}


class Tile:
    def __init__(self, t, name):
        self.t = t
        self.b = Buf(name)

    def __getitem__(self, k):
        return self.t[k]


class SemState:
    def __init__(self, nc, st, n_dma_sems=26):
        self.cnt = {e: 0 for e in ENGS}
        self.known = {e: {} for e in ENGS}
        self.n_dma = n_dma_sems
        self.dma_val = [0] * n_dma_sems
        self.n_hw = n_dma_sems - 10
        self.rr_hw = 0
        self.rr_sw = 0
        self.sems = {}
        for e in ENGS:
            self.sems[e] = st.enter_context(nc.semaphore("sem_" + e))
        for k in range(n_dma_sems):
            self.sems[('dma', k)] = st.enter_context(nc.semaphore("sem_d%d" % k))


class Sched:
    def __init__(self, nc, state):
        self.nc = nc
        self.ops = {e: [] for e in ENGS}
        self.state = state

    @property
    def cnt(self):
        return self.state.cnt

    @property
    def known(self):
        return self.state.known

    @property
    def dma_val(self):
        return self.state.dma_val

    @property
    def n_dma(self):
        return self.state.n_dma

    def emit(self, engine, fn, reads=(), writes=(), dma=False, signal=True):
        need = {}

        def add(ev):
            if ev is None:
                return
            k, v = ev
            if need.get(k, 0) < v:
                need[k] = v

        for b in reads:
            add(b.last_w)
        for b in writes:
            add(b.last_w)
            for k, v in b.readers.items():
                add((k, v))
        if engine == 'tensor':
            need.pop('tensor', None)
        kd = None
        if dma:
            stt_ = self.state
            if engine == 'gpsimd':
                kd = stt_.n_hw + stt_.rr_sw
                stt_.rr_sw = (stt_.rr_sw + 1) % (stt_.n_dma - stt_.n_hw)
            else:
                kd = stt_.rr_hw
                stt_.rr_hw = (stt_.rr_hw + 1) % stt_.n_hw
            if self.dma_val[kd] > 0:
                add((('dma', kd), self.dma_val[kd]))
        kn = self.known[engine]
        waits = []
        for k, v in need.items():
            if kn.get(k, 0) >= v:
                continue
            kn[k] = v
            waits.append((k, v))
        if dma:
            self.dma_val[kd] += 16
            ev = (('dma', kd), self.dma_val[kd])
            inc = ev
        elif signal:
            self.cnt[engine] += 1
            ev = (engine, self.cnt[engine])
            inc = ev
        else:
            ev = (engine, self.cnt[engine] + 1)
            inc = None
        self.ops[engine].append((waits, fn, inc))
        for b in writes:
            b.last_w = ev
            b.readers = {}
        for b in reads:
            if b.last_w is not ev:
                if b.readers.get(ev[0], 0) < ev[1]:
                    b.readers[ev[0]] = ev[1]
        return ev

    def wait_all(self, engine):
        need = {}
        for e in ENGS:
            if e != engine and self.cnt[e] > 0:
                need[e] = self.cnt[e]
        for k in range(self.n_dma):
            if self.dma_val[k] > 0:
                need[('dma', k)] = self.dma_val[k]
        kn = self.known[engine]
        waits = [(k, v) for k, v in need.items() if kn.get(k, 0) < v]
        for k, v in waits:
            kn[k] = v
        self.ops[engine].append((waits, None, None))

    def build(self):
        nc = self.nc
        sems = self.state.sems
        with nc.Block() as block:
            def run(engname):
                def body(eng):
                    for waits, fn, inc in self.ops[engname]:
                        if fn is None:
                            for k, v in waits:
                                eng.wait_ge(sems[k], v)
                            continue
                        for k, v in waits[1:]:
                            eng.wait_ge(sems[k], v)
                        ins = fn(eng)
                        if waits:
                            k, v = waits[0]
                            ins._wait_ge(sems[k], v)
                        if inc is not None:
                            k, v = inc
                            ins.then_inc(sems[k], 16 if isinstance(k, tuple) else 1)
                return body

            block.tensor(run('tensor'))
            block.vector(run('vector'))
            block.scalar(run('scalar'))
            block.gpsimd(run('gpsimd'))
            block.sync(run('sync'))


class Phase:
    _uid = [0]
    state = None

    def __init__(self, nc, name):
        self.nc = nc
        self.name = name
        self.st = contextlib.ExitStack()
        self.S = Sched(nc, Phase.state)
        self.banks = []

    def _nm(self, name):
        Phase._uid[0] += 1
        return "%s_%s_%d" % (self.name, name, Phase._uid[0])

    def sb(self, name, shape, dt, n=1):
        out = []
        for i in range(n):
            nm = self._nm(name)
            t = self.st.enter_context(self.nc.sbuf_tensor(nm, list(shape), dt))
            out.append(Tile(t, nm))
        return out if n > 1 else out[0]

    def psum_banks(self, n=8):
        for i in range(n):
            nm = self._nm("bank")
            t = self.st.enter_context(self.nc.psum_tensor(nm, [128, 512], F32))
            self.banks.append(Tile(t, nm))
        return self.banks

    def wrap(self, t, name):
        return Tile(t, self._nm(name))

    def dma(self, out, in_, reads=(), writes=(), eng='sync'):
        self.S.emit(eng, lambda e: e.dma_start(out=out, in_=in_), reads=[r.b for r in reads],
                    writes=[w.b for w in writes], dma=True)

    def mm(self, out, lhsT, rhs, start, stop, reads, bank, signal=True, skip=False):
        if skip:
            fn = lambda e: e.matmul(out, lhsT=lhsT, rhs=rhs, start=start, stop=stop, skip_group_check=True)
        else:
            fn = lambda e: e.matmul(out, lhsT=lhsT, rhs=rhs, start=start, stop=stop)
        self.S.emit('tensor', fn, reads=[r.b for r in reads], writes=[bank.b], signal=signal)

    def op(self, eng, fn, reads=(), writes=()):
        self.S.emit(eng, fn, reads=[r.b for r in reads], writes=[w.b for w in writes])

    def act(self, out, in_, func, reads, writes, scale=None, bias=None, eng='scalar'):
        kw = {}
        if scale is not None:
            kw['scale'] = scale
        if bias is not None:
            kw['bias'] = bias
        self.op(eng, lambda e: e.activation(out=out, in_=in_, func=func, **kw), reads, writes)

    def tt(self, out, in0, in1, op, reads, writes, eng='vector'):
        self.op(eng, lambda e: e.tensor_tensor(out=out, in0=in0, in1=in1, op=op), reads, writes)

    def stt(self, out, in0, scalar, in1, op0, op1, reads, writes):
        self.op('vector', lambda e: e.scalar_tensor_tensor(out=out, in0=in0, scalar=scalar, in1=in1, op0=op0, op1=op1),
                reads, writes)

    def ts(self, out, in0, s1, s2, op0, op1, reads, writes, eng='gpsimd'):
        if op1 is None:
            self.op(eng, lambda e: e.tensor_scalar(out=out, in0=in0, scalar1=s1, scalar2=None, op0=op0), reads, writes)
        else:
            self.op(eng, lambda e: e.tensor_scalar(out=out, in0=in0, scalar1=s1, scalar2=s2, op0=op0, op1=op1), reads, writes)

    def copy(self, out, in_, reads, writes, eng='vector'):
        if eng == 'scalar':
            self.op(eng, lambda e: e.activation(out=out, in_=in_, func=AF.Copy), reads, writes)
        else:
            self.op(eng, lambda e: e.tensor_copy(out=out, in_=in_), reads, writes)

    def memset(self, ap, val, writes, eng='vector'):
        self.op(eng, lambda e: e.memset(ap, val), (), writes)

    def finish(self):
        for e in ENGS:
            self.S.wait_all(e)
        self.S.build()
        self.st.close()


def v3(ap, inner):
    return ap.rearrange("p (a b) -> p a b", b=inner)


def MOD(l, who, i):
    return ((l * 2 + who) * 6 + i) * 8
GS_BASE = DEPTH * 2 * 6 * 8
def GS(l, who, k):
    return GS_BASE + ((l * 2 + who) * 2 + k) * 8
PPW = GS_BASE + DEPTH * 2 * 2 * 8
def SM_BADA(l): return l * 48
def SM_GMIX(l): return DEPTH * 48 + l * 8
def SM_GFFN(l): return DEPTH * 48 + DEPTH * 8 + l * 8
SM_GFIN = DEPTH * 48 + 2 * DEPTH * 8
def SM_QKN(l): return SM_GFIN + 8 + l * 4
SMW = SM_GFIN + 8 + DEPTH * 4


class K:
    pass


def rstd_from_sums(ph, bank_ap, n_div, tmp, out, reads_bank, tmp_t, out_t):
    ph.act(tmp, bank_ap, AF.Ln, reads=[], writes=[reads_bank, tmp_t], scale=1.0 / n_div, bias=EPS)
    ph.act(out, tmp, AF.Exp, reads=[tmp_t], writes=[out_t], scale=-0.5)


def precast(ph, k, l, ffn=True, win=True, dep=()):
    if win and l < DEPTH:
        for r in range(0, D, 256):
            ph.dma(k.winB[l, r:r + 256, :], k.w_in_p[l, r:r + 256, :], reads=dep, eng='gpsimd')
    if ffn and l < DEPTH:
        for r in range(0, D, 256):
            ph.dma(k.woB[l, r:r + 256, :], k.w_out_p[l, r:r + 256, :], reads=dep, eng='gpsimd')
            ph.dma(k.wgB[l, r:r + 256, :], k.w_gate[l, r:r + 256, :], reads=dep, eng='gpsimd')
            ph.dma(k.wuB[l, r:r + 256, :], k.w_up[l, r:r + 256, :], reads=dep, eng='gpsimd')
        for r in range(0, DFF, 256):
            ph.dma(k.wdB[l, r:r + 256, :], k.w_down[l, r:r + 256, :], reads=dep, eng='gpsimd')


def phase_A(k, l):
    nc = k.nc
    ph = Phase(nc, "A%d" % l)
    banks = ph.psum_banks(4)
    pp = ph.wrap(k.pp, "pp")
    sm = ph.wrap(k.sm, "sm")
    cv = ph.sb("cv", [128, 16], F32)
    sc = ph.sb("sc", [128, 16], F32)
    wst = ph.sb("wst", [128, 6 * D], F32, n=2)
    wfT = ph.sb("wfT", [128, 2 * D], F32)
    bd = ph.sb("bd", [128, 256], F32)
    wfo = ph.sb("wfo", [128, 512], BF16, n=2)
    if l == 0:
        ph.dma(sm[:, :], k.small[:, :], writes=[sm])
        precast(ph, k, 0, ffn=False, win=True)
    ph.dma(cv[:, :], k.cvec[:, :], writes=[cv])
    ph.act(sc[:, :], cv[:, :], AF.Silu, reads=[cv], writes=[sc])
    b0 = banks[0]
    ph.memset(b0[:, 0:96], 0.0, writes=[b0])
    import os
    dbg = os.environ.get("KDBG", "")
    for kc in range(0 if 'nomm' in dbg else 8):
        w = wst[kc % 2]
        ph.dma(w[:, :], k.w_ada[l, kc * 128:(kc + 1) * 128, :], writes=[w])
        for cc in range(48):
            ph.mm(b0[:, cc * 2:cc * 2 + 2], w[:, cc * 128:(cc + 1) * 128], sc[:, kc * 2:kc * 2 + 2],
                  start=False, stop=(kc == 7), reads=[w, sc], bank=b0, signal=(cc == 47), skip=True)
    for who in range(2):
        src = v3(b0[:, 0:96], 2)[:, :, who]
        c0 = MOD(l, who, 0)
        ph.tt(pp[:, c0:c0 + 48], src, sm[:, SM_BADA(l):SM_BADA(l) + 48], ALU.add, reads=[sm], writes=[b0, pp])
        for kk, (i_sc, gcol) in enumerate(((1, SM_GMIX(l)), (4, SM_GFFN(l)))):
            cs = MOD(l, who, i_sc)
            cg = GS(l, who, kk)
            ph.stt(pp[:, cg:cg + 8], pp[:, cs:cs + 8], 1.0, sm[:, gcol:gcol + 8], ALU.add, ALU.mult,
                   reads=[sm], writes=[pp])
    ph.dma(v3(wfT[:, :], D), k.w_in_fT[l].rearrange("(c p) d -> p c d", p=128), writes=[wfT])
    ph.dma(bd[:, :], k.cmat[:, 0:256], writes=[bd])
    for dc in range(0 if 'nowf' in dbg else 8):
        bk = banks[1 + dc % 2]
        for t in range(2):
            for fc in range(2):
                q = t * 2 + fc
                ph.mm(bk[:, q * 128:(q + 1) * 128], wfT[:, fc * D + dc * 128: fc * D + (dc + 1) * 128],
                      bd[:, t * 128:(t + 1) * 128], start=True, stop=True, reads=[wfT, bd], bank=bk,
                      signal=(q == 3))
        o = wfo[dc % 2]
        ph.copy(o[:, :], bk[:, :], reads=[], writes=[bk, o])
        ph.dma(k.wfD[l, dc * 128:(dc + 1) * 128, :], o[:, :], reads=[o], eng='gpsimd')
    ph.finish()


CH_QA, CH_QAS, CH_KA, CH_KAS, CH_QC, CH_QCS, CH_KC, CH_KCS = 0, 3, 6, 7, 8, 11, 14, 15
TM0 = 2048
WCOLS = 2048 + 768


def phase_B(k, l):
    nc = k.nc
    S = k.S
    NT = S // 512
    ph = Phase(nc, "B%d" % l)
    banks = ph.psum_banks(8)
    pp = ph.wrap(k.pp, "pp")
    sm = ph.wrap(k.sm, "sm")
    wT = ph.sb("wT", [128, 8 * WCOLS], BF16)
    wv = v3(wT[:, :], WCOLS)
    xts = ph.sb("xt", [128, 8 * 512], F32, n=2)
    sq = ph.sb("sq", [128, 8 * 512], BF16)
    hTs = ph.sb("hT", [128, 8 * 512], BF16, n=2)
    ones = ph.sb("ones", [128, 128], BF16)
    bones = ph.sb("bones", [128, 128], BF16)
    lnt = ph.sb("lnt", [128, 512], F32)
    rstd = ph.sb("rstd", [128, 512], F32)
    rC = ph.sb("rC", [128, 512], F32, n=3)
    rS = ph.sb("rS", [128, 512], F32, n=3)
    sqh = ph.sb("sqh", [128, 512], BF16, n=2)
    lnh = ph.sb("lnh", [128, 512], F32)
    rinv = ph.sb("rinv", [128, 512], F32, n=2)
    t1s = ph.sb("t1", [128, 512], F32, n=2)
    t2s = ph.sb("t2", [128, 512], F32, n=2)
    qo = ph.sb("qo", [128, 512], BF16, n=4)
    vts = ph.sb("vt", [128, VW], BF16, n=4)
    zts = ph.sb("zt", [128, 512], BF16, n=4)

    ph.memset(ones[:, :], 1.0, writes=[ones])
    ph.memset(bones[:, :], 0.0, writes=[bones])
    ph.memset(bones[0:64, 0:64], 1.0, writes=[bones])
    ph.memset(bones[64:128, 64:128], 1.0, writes=[bones])
    for vt in vts:
        ph.memset(vt[:, :], 1.0, writes=[vt], eng='gpsimd')
    for c0 in range(0, 2304, 768):
        ph.dma(wv[:, :, c0:c0 + 768], k.winB[l, :, c0:c0 + 768].rearrange("(c p) f -> p c f", p=128), writes=[wT])
    ph.dma(wv[:, :, 2304:2816], k.wfD[l].rearrange("(c p) f -> p c f", p=128), writes=[wT])

    qk = SM_QKN(l)
    gq, gqs, gk, gks = (sm[:, qk + i:qk + i + 1] for i in range(4))
    pair_i = [0]
    qo_i = [0]
    vt_i = [0]

    tiles = [('ctx', 0)] + [('x', t) for t in range(NT)]

    def info(ti):
        kind, t = tiles[ti]
        isx = kind == 'x'
        N = 512 if isx else CTX
        off = CTX + t * 512 if isx else 0
        who = 0 if isx else 1
        need_q = isx or l == 0
        xt, hT = xts[ti % 2], hTs[ti % 2]
        return kind, t, isx, N, off, who, need_q, xtk[ti % 2], hTk[ti % 2], v3(xt[:, :], 512), v3(hT[:, :], 512)

    sv = v3(sq[:, :], 512)
    pending = []
    xtk = [[Tile(x_.t, "xk%d_%d" % (i, c)) for c in range(8)] for i, x_ in enumerate(xts)]
    hTk = [[Tile(h_.t, "hk%d_%d" % (i, c)) for c in range(8)] for i, h_ in enumerate(hTs)]

    def pre(ti):
        kind, t, isx, N, off, who, need_q, xt, hT, xv, hv = info(ti)
        if isx:
            src = (k.xT if l == 0 else k.xD1)[:, t * 512:(t + 1) * 512]
        else:
            src = (k.ctxT if l == 0 else k.ctxD1)[:, :]
        ph.dma(xv[:, :, 0:N], src.rearrange("(c p) t -> p c t", p=128), writes=xt)
        if isx:
            c_t, s_t = rC[t % 3], rS[t % 3]
            ph.dma(c_t[:, :], k.ropeC[:, t * 512:(t + 1) * 512], writes=[c_t])
            ph.dma(s_t[:, :], k.ropeS[:, t * 512:(t + 1) * 512], writes=[s_t])

    def chain1(ti):
        kind, t, isx, N, off, who, need_q, xt, hT, xv, hv = info(ti)
        ph.act(sv[:, :, 0:N], xv[:, :, 0:N], AF.Square, reads=xt, writes=[sq])

    def chain2(ti):
        kind, t, isx, N, off, who, need_q, xt, hT, xv, hv = info(ti)
        bS = banks[0]
        for kc in range(8):
            ph.mm(bS[:, 0:N], ones[:, :], sv[:, kc, 0:N], start=(kc == 0), stop=(kc == 7), reads=[ones, sq],
                  bank=bS, signal=(kc == 7))

    def chain3(ti):
        kind, t, isx, N, off, who, need_q, xt, hT, xv, hv = info(ti)
        bS = banks[0]
        rstd_from_sums(ph, bS[:, 0:N], D, lnt[:, 0:N], rstd[:, 0:N], bS, lnt, rstd)
        g0 = GS(l, who, 0)
        s0 = MOD(l, who, 0)
        for kc in range(8):
            ph.tt(xv[:, kc, 0:N], xv[:, kc, 0:N], rstd[:, 0:N], ALU.mult, reads=[rstd], writes=[xt[kc]])
            if kc % 3 == 2:
                ph.act(hv[:, kc, 0:N], xv[:, kc, 0:N], AF.Identity, reads=[xt[kc], pp], writes=[hT[kc]],
                       scale=pp[:, g0 + kc:g0 + kc + 1], bias=pp[:, s0 + kc:s0 + kc + 1])
            else:
                ph.ts(hv[:, kc, 0:N], xv[:, kc, 0:N], pp[:, g0 + kc:g0 + kc + 1], pp[:, s0 + kc:s0 + kc + 1],
                      ALU.mult, ALU.add, reads=[xt[kc], pp], writes=[hT[kc]], eng='gpsimd')

    def work(ti, part):
        kind, t, isx, N, off, who, need_q, xt, hT, xv, hv = info(ti)
        if isx:
            c_t, s_t = rC[t % 3], rS[t % 3]

        def proj(ch, bank):
            for kc in range(8):
                ph.mm(bank[:, 0:N], wv[:, kc, ch * 128:(ch + 1) * 128], hv[:, kc, 0:N], start=(kc == 0),
                      stop=(kc == 7), reads=[wT, hT[kc]], bank=bank, signal=(kc == 7))

        def out_tile():
            o = qo[qo_i[0] % 4]
            qo_i[0] += 1
            return o

        jobs = []
        for j in range(3):
            if need_q:
                jobs.append((CH_QA + j, CH_QAS + j, True, gq, gqs, k.qaD[j, :, off:off + N]))
        jobs.append((CH_KA, CH_KAS, True, gk, gks, k.kaD[:, off:off + N]))
        for j in range(3):
            if need_q:
                jobs.append((CH_QC + j, CH_QCS + j, False, None, None, k.qcD[j, :, off:off + N]))
        jobs.append((CH_KC, CH_KCS, False, None, None, k.kcD[:, off:off + N]))
        nsplit = 3 if len(jobs) > 3 else 1
        jobs = jobs[:nsplit] if part == 0 else jobs[nsplit:]
        for (ch, chs, normed, g, gs_, dst) in jobs:
            pi = pair_i[0]
            pair_i[0] += 1
            b1 = banks[1 + 2 * (pi % 2)]
            b2 = banks[2 + 2 * (pi % 2)]
            bN = banks[5]
            proj(ch, b1)
            if isx:
                proj(chs, b2)
            o = out_tile()
            t1 = t1s[pi % 2]
            t2 = t2s[pi % 2]
            sh_ = sqh[pi % 2]
            ri = rinv[pi % 2]
            if normed:
                ph.act(sh_[:, 0:N], b1[:, 0:N], AF.Square, reads=[], writes=[b1, sh_])

            def back(normed=normed, g=g, gs_=gs_, dst=dst, b1=b1, b2=b2, bN=bN, o=o, t1=t1, t2=t2, sh_=sh_, ri=ri,
                     isx=isx, N=N):
                if normed:
                    ph.mm(bN[:, 0:N], bones[:, :], sh_[:, 0:N], start=True, stop=True, reads=[bones, sh_], bank=bN)
                    rstd_from_sums(ph, bN[:, 0:N], 64, lnh[:, 0:N], ri[:, 0:N], bN, lnh, ri)
                    if isx:
                        ph.stt(t1[:, 0:N], b1[:, 0:N], g, c_t[:, 0:N], ALU.mult, ALU.mult, reads=[sm, c_t], writes=[b1, t1])
                        ph.stt(t2[:, 0:N], b2[:, 0:N], gs_, s_t[:, 0:N], ALU.mult, ALU.mult, reads=[sm, s_t], writes=[b2, t2])
                        ph.tt(t1[:, 0:N], t1[:, 0:N], t2[:, 0:N], ALU.add, reads=[t2], writes=[t1], eng='gpsimd')
                        ph.tt(o[:, 0:N], t1[:, 0:N], ri[:, 0:N], ALU.mult, reads=[t1, ri], writes=[o])
                    else:
                        ph.stt(o[:, 0:N], b1[:, 0:N], g, ri[:, 0:N], ALU.mult, ALU.mult, reads=[sm, ri], writes=[b1, o])
                else:
                    if isx:
                        ph.tt(t1[:, 0:N], b1[:, 0:N], c_t[:, 0:N], ALU.mult, reads=[c_t], writes=[b1, t1])
                        ph.tt(t2[:, 0:N], b2[:, 0:N], s_t[:, 0:N], ALU.mult, reads=[s_t], writes=[b2, t2])
                        ph.tt(o[:, 0:N], t1[:, 0:N], t2[:, 0:N], ALU.add, reads=[t1, t2], writes=[o], eng='gpsimd')
                    else:
                        ph.act(o[:, 0:N], b1[:, 0:N], AF.Copy, reads=[], writes=[b1, o])
                ph.dma(dst, o[:, 0:N], reads=[o], eng='sync')

            if pending:
                pending.pop(0)()
            pending.append(back)
        if part == 0:
            return
        while pending:
            pending.pop(0)()
        for s_ in range(N // 128):
            b6, b7 = banks[6], banks[7]
            ncol = 768 if need_q else 256
            for kc in range(8):
                lw = hv[:, kc, s_ * 128:(s_ + 1) * 128]
                ph.mm(b6[:, 0:min(ncol, 512)], lw, wv[:, kc, TM0:TM0 + min(ncol, 512)], start=(kc == 0), stop=(kc == 7),
                      reads=[wT, hT[kc]], bank=b6, signal=(kc == 7 and not need_q))
                if need_q:
                    ph.mm(b7[:, 0:256], lw, wv[:, kc, TM0 + 512:TM0 + 768], start=(kc == 0), stop=(kc == 7),
                          reads=[wT, hT[kc]], bank=b7, signal=(kc == 7))
            vt = vts[vt_i[0] % 4]
            zt = zts[vt_i[0] % 4]
            vt_i[0] += 1
            vin = b6[:, 0:256].rearrange("p (a g d) -> p a g d", a=2, g=2)
            vout = vt[:, :].rearrange("p (a w) -> p a w", a=2)
            ph.copy(vout[:, :, 0:64], vin[:, :, 0, :], reads=[], writes=[b6, vt], eng='scalar')
            ph.copy(vout[:, :, 192:256], vin[:, :, 1, :], reads=[], writes=[b6, vt], eng='scalar')
            ph.dma(k.vD[off + s_ * 128: off + (s_ + 1) * 128, :], vt[:, :], reads=[vt], eng='sync')
            if need_q:
                ph.copy(zt[:, 0:256], b6[:, 256:512], reads=[], writes=[b6, zt], eng='scalar')
                ph.copy(zt[:, 256:512], b7[:, 0:256], reads=[], writes=[b7, zt], eng='scalar')
                zd = k.zD if isx else k.zDc
                r0 = (t * 512 if isx else 0) + s_ * 128
                ph.dma(zd[:, r0:r0 + 128, :].rearrange("r p j -> p r j"), v3(zt[:, :], 256), reads=[zt], eng='sync')

    NTL = len(tiles)
    pre(0)
    pre(1)
    chain1(0)
    chain2(0)
    chain3(0)
    for ti in range(NTL):
        if ti + 1 < NTL:
            chain1(ti + 1)
        work(ti, 0)
        if ti + 1 < NTL:
            chain2(ti + 1)
            chain3(ti + 1)
        if ti + 2 < NTL:
            pre(ti + 2)
        work(ti, 1)
    ph.finish()


def attn_epilogue(ph, bO, bR, half, N, oe, rr, onesf, selhi, dst_ap):
    if half == 0:
        ph.copy(oe[0:65, 0:N], bO[0:65, 0:N], reads=[], writes=[bO, oe])
        ph.op('vector', lambda e: e.reciprocal(out=rr[64:65, 0:N], in_=oe[64:65, 0:N]), reads=[oe], writes=[rr])
        ph.mm(bR[0:64, 0:N], onesf[64:65, 0:64], rr[64:65, 0:N], start=True, stop=True, reads=[onesf, rr], bank=bR)
        ph.tt(dst_ap, oe[0:64, 0:N] if dst_ap.ndim == 2 else v3(oe[0:64, 0:N], 128),
              bR[0:64, 0:N] if dst_ap.ndim == 2 else v3(bR[0:64, 0:N], 128), ALU.mult, reads=[oe], writes=[bR, ph._dst])
    else:
        ph.copy(oe[:, 0:N], bO[:, 0:N], reads=[], writes=[bO, oe])
        ph.op('vector', lambda e: e.reciprocal(out=rr[0:1, 0:N], in_=oe[0:1, 0:N]), reads=[oe], writes=[rr])
        ph.mm(bR[:, 0:N], selhi[0:1, 0:128], rr[0:1, 0:N], start=True, stop=True, reads=[selhi, rr], bank=bR)
        ph.tt(dst_ap, oe[64:128, 0:N] if dst_ap.ndim == 2 else v3(oe[64:128, 0:N], 128),
              bR[64:128, 0:N] if dst_ap.ndim == 2 else v3(bR[64:128, 0:N], 128), ALU.mult, reads=[oe], writes=[bR, ph._dst])


def attn_consts(ph):
    onesf = ph.sb("onesf", [128, 128], F32)
    selhi = ph.sb("selhi", [128, 128], F32)
    ph.memset(onesf[:, :], 1.0, writes=[onesf])
    ph.memset(selhi[:, :], 0.0, writes=[selhi])
    ph.memset(selhi[0:1, 64:128], 1.0, writes=[selhi])
    return onesf, selhi


def phase_C(k, l):
    nc = k.nc
    S = k.S
    NT = S // 512
    NKC = 2 + S // 128
    ph = Phase(nc, "C%d" % l)
    pairs = []
    for i in range(4):
        nm = ph._nm("pp")
        pairs.append(ph.st.enter_context(nc.psum_tensor(nm, [128, 1024], F32)))
    NST, LA = 3, 2
    ST = [Tile(pairs[0], "st0"), Tile(pairs[1], "st1"), Tile(pairs[2], "st2")]
    bO = [Tile(pairs[3], "bO0"), Tile(pairs[3], "bO1")]
    kT = ph.sb("kT", [128, CTX + S], BF16)
    vA = ph.sb("vA", [128, NKC * 256], BF16)
    vAv = v3(vA[:, :], 256)
    qts = ph.sb("qt", [128, 3 * 512], BF16, n=2)
    pts = ph.sb("pt", [128, 1024], BF16, n=4)
    oes = ph.sb("oe", [128, 1024], F32, n=2)
    rrs = ph.sb("rr", [128, 1024], F32, n=2)
    ots = ph.sb("ot", [128, 3 * 512], BF16, n=2)
    kTk, vAk = [], []
    for c0 in range(0, NKC, 8):
        c1 = min(NKC, c0 + 8)
        kTk.append(Tile(kT.t, "kTk%d" % c0))
        vAk.append(Tile(vA.t, "vAk%d" % c0))
        ph.dma(kT[:, c0 * 128:c1 * 128], k.kaD[:, c0 * 128:c1 * 128], writes=[kTk[-1]])
        ph.dma(vAv[:, c0:c1, :], k.vD[c0 * 128:c1 * 128, 0:256].rearrange("(c p) f -> p c f", p=128), writes=[vAk[-1]])

    tiles = ([('ctx', 0)] if l == 0 else []) + [('x', t) for t in range(NT)]

    def tile_info(ti):
        kind, t = tiles[ti]
        N = 512 if kind == 'x' else CTX
        off = CTX + t * 512 if kind == 'x' else 0
        return N, off

    def load_q(ti):
        N, off = tile_info(ti)
        qt = qts[ti % 2]
        ph.dma(v3(qt[:, :], 512)[:, :, 0:N], k.qaD[:, :, off:off + N].rearrange("c p t -> p c t"), writes=[qt])

    steps = []
    for ti in range(len(tiles)):
        nkc = 2 if tiles[ti][0] == 'ctx' else NKC
        for j in range(3):
            for c in range(nkc):
                steps.append((ti, j, c, nkc))

    def emit_qk(si):
        ti, j, c, nkc = steps[si]
        N, off = tile_info(ti)
        qt = qts[ti % 2]
        st = ST[si % NST]
        qv = v3(qt[:, :], 512)
        ph.mm(st[:, 0:N], kT[0:64, c * 128:(c + 1) * 128], qv[0:64, j, 0:N], start=True, stop=True,
              reads=[kTk[c // 8], qt], bank=st, signal=False)
        ph.mm(st[:, 512:512 + N], kT[64:128, c * 128:(c + 1) * 128], qv[64:128, j, 0:N], start=True, stop=True,
              reads=[kTk[c // 8], qt], bank=st, signal=True)

    deferred = []
    pctok = Tile(None, 'pctok')
    load_q(0)
    if len(tiles) > 1:
        load_q(1)
    for s0_ in range(min(LA, len(steps))):
        emit_qk(s0_)
    grp = 0
    for si, (ti, j, c, nkc) in enumerate(steps):
        N, off = tile_info(ti)
        if c == 0 and j == 0 and ti >= 1 and ti + 1 < len(tiles):
            load_q(ti + 1)
        if si + LA < len(steps):
            emit_qk(si + LA)
        st = ST[si % NST]
        pt = pts[si % 4]
        ph.act(v3(pt[:, :], 512)[:, :, 0:N], v3(st[:, :], 512)[:, :, 0:N], AF.Exp, reads=[], writes=[st, pt], scale=0.125)
        ph.mm(bO[0][:, 0:N], vAv[:, c, 0:128], pt[:, 0:N], start=(c == 0), stop=(c == nkc - 1),
              reads=[vAk[c // 8], pt], bank=bO[0], signal=False)
        ph.mm(bO[1][:, 512:512 + N], vAv[:, c, 128:256], pt[:, 512:512 + N], start=(c == 0), stop=(c == nkc - 1),
              reads=[vAk[c // 8], pt], bank=bO[1], signal=True)
        if c == nkc - 1:
            ot = ots[ti % 2]
            oe = oes[grp % 2]
            rr = rrs[grp % 2]
            grp += 1
            ov = v3(ot[:, :], 512)

            ph.copy(oe[:, 0:N], bO[0][:, 0:N], reads=[], writes=[bO[0], oe], eng='vector')
            ph.copy(oe[:, 512:512 + N], bO[1][:, 512:512 + N], reads=[], writes=[bO[1], oe])
            ph.op('vector', lambda e, rr=rr, oe=oe, N_=N: e.reciprocal(out=rr[0:64, 0:N_], in_=oe[64:128, 0:N_]),
                  reads=[oe], writes=[rr])
            ph.op('vector', lambda e, rr=rr, oe=oe, N_=N: e.reciprocal(out=rr[64:128, 512:512 + N_], in_=oe[0:64, 512:512 + N_]),
                  reads=[oe], writes=[rr])
            ph.tt(ov[0:64, j, 0:N], oe[0:64, 0:N], rr[0:64, 0:N], ALU.mult, reads=[oe, rr], writes=[ot])
            trig = grp == (4 if l == 0 else 1)
            ph.tt(ov[64:128, j, 0:N], oe[64:128, 512:512 + N], rr[64:128, 512:512 + N], ALU.mult, reads=[oe, rr],
                  writes=[ot, pctok] if trig else [ot])
            if j == 2:
                ph.dma(k.oD[0:3, :, off:off + N].rearrange("c p t -> p c t"), ov[:, :, 0:N], reads=[ot], eng='sync')
            if trig:
                precast(ph, k, l, ffn=True, win=False, dep=[pctok])
                precast(ph, k, l + 1, ffn=False, win=True, dep=[pctok])
    for _, fn in deferred:
        fn()
    ph.finish()


def phase_D(k, l):
    nc = k.nc
    S = k.S
    NT = S // 512
    NQB = S // 128
    NKC = 2 + NQB
    ph = Phase(nc, "D%d" % l)
    pairs = []
    for i in range(4):
        nm = ph._nm("pp")
        pairs.append(ph.st.enter_context(nc.psum_tensor(nm, [128, 1024], F32)))
    NST, LA = 3, 2
    ST = [Tile(pairs[0], "st0"), Tile(pairs[1], "st1"), Tile(pairs[2], "st2")]
    bO = [Tile(pairs[3], "bO0"), Tile(pairs[3], "bO1")]
    kT = ph.sb("kT", [128, CTX + S], BF16)
    vC = ph.sb("vC", [128, NKC * 256], BF16)
    vCv = v3(vC[:, :], 256)
    qts = ph.sb("qt", [128, 3 * 512], BF16, n=2)
    pts = ph.sb("pt", [128, 1024], BF16, n=4)
    NR = 3
    oes = ph.sb("oe", [128, 1024], F32, n=NR)
    rrs = ph.sb("rr", [128, 1024], F32, n=NR)
    ots = ph.sb("ot", [128, 3 * 512], BF16, n=2)
    msk = ph.sb("msk", [128, 256], BF16)
    sk32 = ph.sb("sk32", [1, 6 * 256], F32)
    skb = ph.sb("skb", [1, 6 * 256], BF16)
    esel = ph.sb("esel", [1, 256], BF16)
    kTk, vCk = [], []
    for c0 in range(0, NKC, 8):
        c1 = min(NKC, c0 + 8)
        kTk.append(Tile(kT.t, "kTk%d" % c0))
        vCk.append(Tile(vC.t, "vCk%d" % c0))
        ph.dma(kT[:, c0 * 128:c1 * 128], k.kcD[:, c0 * 128:c1 * 128], writes=[kTk[-1]])
        ph.dma(vCv[:, c0:c1, :], k.vD[c0 * 128:c1 * 128, 256:512].rearrange("(c p) f -> p c f", p=128), writes=[vCk[-1]])
    ph.dma(msk[:, :], k.cbf[:, CB_MASK:CB_MASK + 256], writes=[msk], eng='gpsimd')
    ph.dma(sk32[:, :], k.sinkrow[l, :, :], writes=[sk32])
    ph.act(skb[:, :], sk32[:, :], AF.Exp, reads=[sk32], writes=[skb])
    ph.memset(esel[:, :], 0.0, writes=[esel])
    ph.memset(esel[0:1, 64:128], 1.0, writes=[esel])
    ph.memset(esel[0:1, 128:192], 1.0, writes=[esel])
    skv = v3(skb[:, :], 256)
    mv = v3(msk[:, :], 128)
    ident = ph.sb("ident", [128, 128], BF16)
    nmk = ph.sb("nmk", [128, 2 * 384], BF16)
    nmv = v3(nmk[:, :], 384)
    ph.tt(ident[:, :], mv[:, 0, :], mv[:, 1, :], ALU.mult, reads=[msk], writes=[ident])
    for m_ in range(2):
        ph.op('vector', lambda e, m_=m_: e.tensor_scalar(
            out=v3(nmv[:, m_, :], 128), in0=mv[:, m_, :].unsqueeze(1).to_broadcast([128, 3, 128]),
            scalar1=30000.0, scalar2=-30000.0, op0=ALU.mult, op1=ALU.add), reads=[msk], writes=[nmk])

    tiles = ([('ctx', 0)] if l == 0 else []) + [('x', t) for t in range(NT)]

    def tile_info(ti):
        kind, t = tiles[ti]
        N = 512 if kind == 'x' else CTX
        off = CTX + t * 512 if kind == 'x' else 0
        return kind, t, N, off

    def load_q(ti):
        kind, t, N, off = tile_info(ti)
        qt = qts[ti % 2]
        ph.dma(v3(qt[:, :], 512)[:, :, 0:N], k.qcD[:, :, off:off + N].rearrange("c p t -> p c t"), writes=[qt])

    groups = []
    for ti in range(len(tiles)):
        kind, t, N, off = tile_info(ti)
        qt, ot = qts[ti % 2], ots[ti % 2]
        qv, ov = v3(qt[:, :], 512), v3(ot[:, :], 512)
        if kind == 'ctx':
            for j in range(3):
                g = dict(ti=ti, NN=CTX, n3=1, nq=CTX, chunks=[(0, 0, None), (128, 1, None)],
                         q=[qv[0:64, j, 0:CTX], qv[64:128, j, 0:CTX]],
                         sink=[skv[0:1, j, 0:CTX], skv[0:1, 3 + j, 0:CTX]],
                         dst=[ov[0:64, j, 0:CTX], ov[64:128, j, 0:CTX]], last=(j == 2))
                groups.append(g)
        else:
            for nb in range(4):
                n = t * 4 + nb
                chunks = [(0, 0, None), (128, 1, None)]
                if n - 1 >= 0:
                    chunks.append((CTX + (n - 1) * 128, 2 + n - 1, 0))
                chunks.append((CTX + n * 128, 2 + n, None))
                if n + 1 < NQB:
                    chunks.append((CTX + (n + 1) * 128, 2 + n + 1, 1))
                cs = slice(nb * 128, (nb + 1) * 128)
                g = dict(ti=ti, NN=384, n3=3, nq=128, chunks=chunks,
                         q=[qv[0:64, :, cs], qv[64:128, :, cs]],
                         sink=[skv[0:1, 0:3, 0:128], skv[0:1, 3:6, 0:128]],
                         dst=[ov[0:64, :, cs], ov[64:128, :, cs]], last=(nb == 3))
                groups.append(g)
    steps = []
    for gi, g in enumerate(groups):
        for ci in range(len(g['chunks'])):
            steps.append((gi, ci))

    def emit_qk(si):
        gi, ci = steps[si]
        g = groups[gi]
        NN = g['NN']
        kcol = g['chunks'][ci][0]
        st = ST[si % NST]
        qt = qts[g['ti'] % 2]
        kt_ = kTk[(kcol // 128) // 8]
        mi_ = g['chunks'][ci][2]
        nom = mi_ is None
        ph.mm(st[:, 0:NN], kT[0:64, kcol:kcol + 128], g['q'][0], start=True, stop=nom, reads=[kt_, qt], bank=st, signal=False)
        ph.mm(st[:, 512:512 + NN], kT[64:128, kcol:kcol + 128], g['q'][1], start=True, stop=nom, reads=[kt_, qt], bank=st,
              signal=nom)
        if not nom:
            ph.mm(st[:, 0:NN], ident[:, :], nmv[:, mi_, :], start=False, stop=True, reads=[ident, nmk], bank=st, signal=False)
            ph.mm(st[:, 512:512 + NN], ident[:, :], nmv[:, mi_, :], start=False, stop=True, reads=[ident, nmk], bank=st,
                  signal=True)

    deferred = []
    load_q(0)
    if len(tiles) > 1:
        load_q(1)
    for s0_ in range(min(LA, len(steps))):
        emit_qk(s0_)
    loaded = {0, 1}
    for si, (gi, ci) in enumerate(steps):
        g = groups[gi]
        NN, n3, nq, ti = g['NN'], g['n3'], g['nq'], g['ti']
        kind, t, N, off = tile_info(ti)
        nch = len(g['chunks'])
        kcol, vci, mi = g['chunks'][ci]
        if ci == 0 and ti + 1 < len(tiles) and (ti + 1) not in loaded and (gi == 0 or groups[gi - 1]['ti'] != ti):
            loaded.add(ti + 1)
            load_q(ti + 1)
        if si + LA < len(steps):
            emit_qk(si + LA)
        st = ST[si % NST]
        pt = pts[si % 4]
        ph.act(v3(pt[:, :], 512)[:, :, 0:NN], v3(st[:, :], 512)[:, :, 0:NN], AF.Exp, reads=[], writes=[st, pt], scale=0.125)
        ph.mm(bO[0][:, 0:NN], vCv[:, vci, 0:128], pt[:, 0:NN], start=(ci == 0), stop=False,
              reads=[vCk[vci // 8], pt], bank=bO[0], signal=False)
        ph.mm(bO[1][:, 512:512 + NN], vCv[:, vci, 128:256], pt[:, 512:512 + NN], start=(ci == 0), stop=False,
              reads=[vCk[vci // 8], pt], bank=bO[1], signal=(ci < nch - 1))
        if ci == nch - 1:
            ph.mm(bO[0][:, 0:NN], esel[0:1, 0:128], g['sink'][0], start=False, stop=True, reads=[esel, skb], bank=bO[0],
                  signal=False)
            ph.mm(bO[1][:, 512:512 + NN], esel[0:1, 128:256], g['sink'][1], start=False, stop=True, reads=[esel, skb],
                  bank=bO[1], signal=True)
            oe, rr = oes[gi % NR], rrs[gi % NR]
            ot = ots[ti % 2]
            ph.copy(oe[:, 0:NN], bO[0][:, 0:NN], reads=[], writes=[bO[0], oe], eng='scalar')
            ph.copy(oe[:, 512:512 + NN], bO[1][:, 512:512 + NN], reads=[], writes=[bO[1], oe], eng='scalar')
            ph.op('vector', lambda e, rr=rr, oe=oe, N_=NN: e.reciprocal(out=rr[0:64, 0:N_], in_=oe[64:128, 0:N_]),
                  reads=[oe], writes=[rr])
            ph.op('vector', lambda e, rr=rr, oe=oe, N_=NN: e.reciprocal(out=rr[64:128, 512:512 + N_], in_=oe[0:64, 512:512 + N_]),
                  reads=[oe], writes=[rr])

            def vw(ap, g=g, nq=nq):
                return ap if g['n3'] == 1 else v3(ap, nq)
            ph.tt(g['dst'][0], vw(oe[0:64, 0:NN]), vw(rr[0:64, 0:NN]), ALU.mult, reads=[oe, rr], writes=[ot])
            ph.tt(g['dst'][1], vw(oe[64:128, 512:512 + NN]), vw(rr[64:128, 512:512 + NN]), ALU.mult, reads=[oe, rr], writes=[ot])
            if g['last']:
                ov = v3(ot[:, :], 512)
                ph.dma(k.oD[5:8, :, off:off + N].rearrange("c p t -> p c t"), ov[:, :, 0:N], reads=[ot], eng='sync')
    for _, fn in deferred:
        fn()
    ph.finish()


CB_M1, CB_M2, CB_C128, CB_S128, CB_C256, CB_S256, CB_MASK = 0, 128, 256, 384, 512, 1024, 1536
CBW = 1536 + 256


def phase_E(k, l):
    nc = k.nc
    S = k.S
    L1 = S // 128
    P2 = 2 * L1
    ph = Phase(nc, "E%d" % l)
    banks = ph.psum_banks(8)
    cb = ph.sb("cb", [128, CBW], BF16)
    tw = ph.sb("tw", [128, 256], F32)
    ph.dma(cb[:, :], k.cbf[:, :], writes=[cb], eng='gpsimd')
    ph.dma(tw[:, :], k.cmat[:, 256:512], writes=[tw])
    SL = 32
    vs = ph.sb("v", [128, SL * 256], BF16, n=2)
    Zs = ph.sb("Zs", [128, SL * 256], BF16, n=2)
    t1s = ph.sb("t1", [128, 512], F32, n=2)
    t2s = ph.sb("t2", [128, 512], F32, n=2)
    Zt = ph.sb("Zt", [128, 2 * L1 * 256], BF16)
    ofs = ph.sb("of", [128, S], BF16, n=2)
    zdv = k.zD.rearrange("r (a b) j -> r a (b j)", b=128)
    ZDv = k.ZD.rearrange("r a b j -> (r a) (b j)")
    g = 0
    zdb = [Tile(None, "ZDslab%d" % i) for i in range(128 // SL)]
    for sl in range(128 // SL):
        v = vs[sl % 2]
        Zo = Zs[sl % 2]
        for r in range(2):
            ph.dma(v[r * L1:(r + 1) * L1, :], zdv[r, :, sl * SL * 256:(sl + 1) * SL * 256], writes=[v])
        for cg in range(SL // 2):
            cols = slice(cg * 512, (cg + 1) * 512)
            l2a = sl * SL + cg * 2
            bY, bW = banks[(g % 2) * 2], banks[(g % 2) * 2 + 1]
            t1, t2 = t1s[g % 2], t2s[g % 2]
            g += 1
            ph.mm(bY[0:P2, :], cb[0:P2, CB_M1:CB_M1 + P2], v[0:P2, cols], start=True, stop=True, reads=[cb, v], bank=bY)
            ph.mm(bW[0:P2, :], cb[0:P2, CB_M2:CB_M2 + P2], v[0:P2, cols], start=True, stop=True, reads=[cb, v], bank=bW)
            ph.tt(v3(t1[0:P2, :], 256), v3(bY[0:P2, :], 256), tw[0:P2, l2a:l2a + 2].unsqueeze(2).to_broadcast([P2, 2, 256]),
                  ALU.mult, reads=[tw], writes=[bY, t1])
            ph.tt(v3(t2[0:P2, :], 256), v3(bW[0:P2, :], 256),
                  tw[0:P2, 128 + l2a:128 + l2a + 2].unsqueeze(2).to_broadcast([P2, 2, 256]),
                  ALU.mult, reads=[tw], writes=[bW, t2])
            ph.tt(Zo[0:P2, cols], t1[0:P2, :], t2[0:P2, :], ALU.add, reads=[t1, t2], writes=[Zo], eng='gpsimd')
        ph.dma(ZDv[:, sl * SL * 256:(sl + 1) * SL * 256], Zo[0:P2, :], reads=[Zo], writes=[zdb[sl]], eng='gpsimd')
    Zv = Zt[:, :].rearrange("p (r a j) -> p r a j", r=2, a=L1)
    KG = min(16, L1)
    ztk = {}
    for a0 in range(0, L1, KG):
        for r in range(2):
            ztk[(r, a0)] = Tile(Zt.t, "zt%d_%d" % (r, a0))
            ph.dma(Zv[:, r, a0:a0 + KG, :], k.ZD[r, a0:a0 + KG, :, :].rearrange("a b j -> b a j"), reads=zdb,
                   writes=[ztk[(r, a0)]])
    scale = 1.0 / math.sqrt(S * 64.0)
    gi = 0
    for a0 in range(0, L1, 4):
        for jc in range(2):
            bk = banks[4 + gi % 4]
            for q in range(4):
                a = a0 + q
                ph.mm(bk[:, q * 128:(q + 1) * 128], Zv[:, 0, a, jc * 128:(jc + 1) * 128], cb[:, CB_C128:CB_C128 + 128],
                      start=True, stop=False, reads=[ztk[(0, (a // KG) * KG)], cb], bank=bk, signal=False)
                ph.mm(bk[:, q * 128:(q + 1) * 128], Zv[:, 1, a, jc * 128:(jc + 1) * 128], cb[:, CB_S128:CB_S128 + 128],
                      start=False, stop=True, reads=[ztk[(1, (a // KG) * KG)], cb], bank=bk, signal=(q == 3))
            of = ofs[jc]
            outv = of[:, :].rearrange("p (b a) -> p b a", a=L1)[:, :, a0:a0 + 4]
            inv = bk[:, :].rearrange("p (q b) -> p b q", q=4)
            if gi % 2 == 0:
                ph.act(outv, inv, AF.Copy, reads=[], writes=[bk, of], scale=scale)
            else:
                ph.op('vector', lambda e, outv=outv, inv=inv: e.tensor_scalar(out=outv, in0=inv, scalar1=scale, scalar2=None,
                                                                             op0=ALU.mult), reads=[], writes=[bk, of])
            gi += 1
    for jc in range(2):
        ph.dma(k.oD[3 + jc, :, CTX:CTX + S], ofs[jc][:, :], reads=[ofs[jc]], eng='gpsimd')
    if l == 0:
        zc = ph.sb("zc", [128, 2 * 2 * 256], BF16)
        zcv = zc[:, :].rearrange("p (c r j) -> p c r j", c=2, r=2)
        ofc = ph.sb("ofc", [128, 2 * 256], BF16)
        for c in range(2):
            ph.dma(zcv[:, c, :, :], k.zDc[:, c * 128:(c + 1) * 128, :].rearrange("r p j -> p r j"), writes=[zc])
        sc_c = 1.0 / math.sqrt(CTX * 64.0)
        for jc in range(2):
            bk = banks[jc]
            n = 0
            for c in range(2):
                for r in range(2):
                    base = (CB_C256 if r == 0 else CB_S256) + c * 256
                    ph.mm(bk[:, 0:256], zcv[:, c, r, jc * 128:(jc + 1) * 128], cb[:, base:base + 256],
                          start=(n == 0), stop=(n == 3), reads=[zc, cb], bank=bk, signal=(n == 3))
                    n += 1
            ph.act(ofc[:, jc * 256:(jc + 1) * 256], bk[:, 0:256], AF.Copy, reads=[], writes=[bk, ofc], scale=sc_c)
            ph.dma(k.oD[3 + jc, :, 0:CTX], ofc[:, jc * 256:(jc + 1) * 256], reads=[ofc], eng='gpsimd')
    ph.finish()


def phase_G(k, l):
    nc = k.nc
    S = k.S
    N = 256
    last = (l == DEPTH - 1)
    ph = Phase(nc, "G%d" % l)
    banks = ph.psum_banks(8)
    pp = ph.wrap(k.pp, "pp")
    sm = ph.wrap(k.sm, "sm")
    wo = ph.sb("wo", [128, 8 * D], BF16)
    wg = ph.sb("wg", [128, 8 * DFF], BF16)
    wu = ph.sb("wu", [128, 8 * DFF], BF16)
    wd = ph.sb("wd", [128, NFC * D], BF16)
    wov, wgv, wuv, wdv = v3(wo[:, :], D), v3(wg[:, :], DFF), v3(wu[:, :], DFF), v3(wd[:, :], D)
    xts = ph.sb("xt", [128, 8 * N], F32, n=2)
    ots = ph.sb("ot", [128, 8 * N], BF16, n=2)
    xn = ph.sb("xn", [128, 8 * N], F32)
    hh = ph.sb("hh", [128, 8 * N], BF16)
    sq = hh
    aa = ph.sb("aa", [128, NFC * N], BF16)
    sgs = ph.sb("sg", [128, N], F32, n=2)
    ones = ph.sb("ones", [128, 128], BF16)
    lnt = ph.sb("lnt", [128, N], F32)
    rstd = ph.sb("rstd", [128, N], F32)
    ph.memset(ones[:, :], 1.0, writes=[ones])
    ph.dma(wov, k.woB[l].rearrange("(c p) f -> p c f", p=128), writes=[wo])
    early_load = [True]
    NWG = 4
    WGC = DFF // NWG
    wgs = [Tile(wg.t, "wg%d" % i) for i in range(NWG)]
    wus = [Tile(wu.t, "wu%d" % i) for i in range(NWG)]

    def emit_big_weights():
        for gI in range(NWG):
            cs = slice(gI * WGC, (gI + 1) * WGC)
            ph.dma(wgv[:, :, cs], k.wgB[l, :, cs].rearrange("(c p) f -> p c f", p=128), writes=[wgs[gI]])
            ph.dma(wuv[:, :, cs], k.wuB[l, :, cs].rearrange("(c p) f -> p c f", p=128), writes=[wus[gI]])
        for c0 in range(0, NFC, 11):
            ph.dma(wdv[:, c0:c0 + 11, :], k.wdB[l, c0 * 128:(c0 + 11) * 128, :].rearrange("(c p) f -> p c f", p=128),
                   writes=[wd])

    tiles = [('x', t) for t in range(S // N)] + ([('ctx', 0)] if not last else [])
    lnf = ph.sb("lnf", [128, N], F32)
    rstdf = ph.sb("rstdf", [128, N], F32)
    hv = v3(hh[:, :], N)
    nv = v3(xn[:, :], N)
    av = v3(aa[:, :], N)
    sv = v3(sq[:, :], N)
    bi = [0]
    xnk = [Tile(xn.t, "xnk%d" % c) for c in range(8)]
    hhk = [Tile(hh.t, "hhk%d" % c) for c in range(8)]

    def info(ti):
        kind, t = tiles[ti]
        isx = kind == 'x'
        who = 0 if isx else 1
        xt, ot = xts[ti % 2], ots[ti % 2]
        return kind, t, isx, who, xt, ot, v3(xt[:, :], N), v3(ot[:, :], N)

    def load(ti):
        kind, t, isx, who, xt, ot, xv, ov = info(ti)
        off = CTX + t * N if isx else 0
        if isx:
            src = (k.xT if l == 0 else k.xD1)[:, t * N:(t + 1) * N]
        else:
            src = k.ctxT[:, :]
        ph.dma(xv, src.rearrange("(c p) t -> p c t", p=128), writes=[xt])
        ph.dma(ov, k.oD[:, :, off:off + N].rearrange("c p t -> p c t"), writes=[ot])

    def st_A(ti):
        kind, t, isx, who, xt, ot, xv, ov = info(ti)
        gt1 = MOD(l, who, 2)
        for dc in range(8):
            bk = banks[1 + bi[0] % 2]
            bi[0] += 1
            for mc in range(8):
                ph.mm(bk[:, 0:N], wov[:, mc, dc * 128:(dc + 1) * 128], ov[:, mc, :], start=(mc == 0), stop=(mc == 7),
                      reads=[wo, ot], bank=bk, signal=(mc == 7))
            ph.stt(xv[:, dc, :], bk[:, 0:N], pp[:, gt1 + dc:gt1 + dc + 1], xv[:, dc, :], ALU.mult, ALU.add,
                   reads=[pp], writes=[bk, xt])

    def st_Bn(ti):
        kind, t, isx, who, xt, ot, xv, ov = info(ti)
        ph.act(sv, xv, AF.Square, reads=[xt], writes=hhk)

    def st_Bs(ti):
        bS = banks[0]
        for kc in range(8):
            ph.mm(bS[:, 0:N], ones[:, :], sv[:, kc, :], start=(kc == 0), stop=(kc == 7), reads=[ones, hhk[kc]], bank=bS,
                  signal=(kc == 7))

    def st_Bc(ti):
        kind, t, isx, who, xt, ot, xv, ov = info(ti)
        sh2, gs2 = MOD(l, who, 3), GS(l, who, 1)
        bS = banks[0]
        rstd_from_sums(ph, bS[:, 0:N], D, lnt[:, :], rstd[:, :], bS, lnt, rstd)
        for kc in range(8):
            ph.tt(nv[:, kc, :], xv[:, kc, :], rstd[:, :], ALU.mult, reads=[xt, rstd], writes=[xnk[kc]])
            if kc % 3 == 2:
                ph.act(hv[:, kc, :], nv[:, kc, :], AF.Identity, reads=[xnk[kc], pp], writes=[hhk[kc]],
                       scale=pp[:, gs2 + kc:gs2 + kc + 1], bias=pp[:, sh2 + kc:sh2 + kc + 1])
            else:
                ph.ts(hv[:, kc, :], nv[:, kc, :], pp[:, gs2 + kc:gs2 + kc + 1], pp[:, sh2 + kc:sh2 + kc + 1], ALU.mult,
                      ALU.add, reads=[xnk[kc], pp], writes=[hhk[kc]], eng='gpsimd')

    def st_C(ti, fcs=range(NFC)):
        for fc in fcs:
            bG, bU = banks[3 + (fc % 2) * 2], banks[4 + (fc % 2) * 2]
            sg = sgs[fc % 2]
            for kc in range(8):
                ph.mm(bG[:, 0:N], wgv[:, kc, fc * 128:(fc + 1) * 128], hv[:, kc, :], start=(kc == 0), stop=(kc == 7),
                      reads=[wgs[(fc * 128) // WGC], wgs[(fc * 128 + 127) // WGC], hhk[kc]], bank=bG, signal=(kc == 7))
            for kc in range(8):
                ph.mm(bU[:, 0:N], wuv[:, kc, fc * 128:(fc + 1) * 128], hv[:, kc, :], start=(kc == 0), stop=(kc == 7),
                      reads=[wus[(fc * 128) // WGC], wus[(fc * 128 + 127) // WGC], hhk[kc]], bank=bU, signal=(kc == 7))
            ph.act(sg[:, :], bG[:, 0:N], AF.Silu, reads=[], writes=[bG, sg])
            ph.tt(av[:, fc, :], sg[:, :], bU[:, 0:N], ALU.mult, reads=[sg], writes=[bU, aa])

    def st_D(ti, dcs):
        kind, t, isx, who, xt, ot, xv, ov = info(ti)
        gt2 = MOD(l, who, 5)
        for dc in dcs:
            bk = banks[1 + bi[0] % 2]
            bi[0] += 1
            for fc in range(NFC):
                ph.mm(bk[:, 0:N], wdv[:, fc, dc * 128:(dc + 1) * 128], av[:, fc, :], start=(fc == 0), stop=(fc == NFC - 1),
                      reads=[wd, aa], bank=bk, signal=(fc == NFC - 1))
            ph.stt(xv[:, dc, :], bk[:, 0:N], pp[:, gt2 + dc:gt2 + dc + 1], xv[:, dc, :], ALU.mult, ALU.add,
                   reads=[pp], writes=[bk, xt])

    def st_out(ti, part):
        kind, t, isx, who, xt, ot, xv, ov = info(ti)
        if last:
            sv2 = v3(aa[:, (NFC - 8) * N:NFC * N], N)
            if part == 0:
                ph.act(sv2, xv, AF.Square, reads=[xt], writes=[aa])
                return
            bS = banks[7]
            for kc in range(8):
                ph.mm(bS[:, 0:N], ones[:, :], sv2[:, kc, :], start=(kc == 0), stop=(kc == 7), reads=[ones, aa], bank=bS,
                      signal=(kc == 7))
            rstd_from_sums(ph, bS[:, 0:N], D, lnf[:, :], rstdf[:, :], bS, lnf, rstdf)
            for kc in range(8):
                ph.stt(nv[:, kc, :], xv[:, kc, :], sm[:, SM_GFIN + kc:SM_GFIN + kc + 1], rstdf[:, :], ALU.mult, ALU.mult,
                       reads=[xt, sm, rstdf], writes=[xnk[kc]])
            ph.dma(k.yT[:, t * N:(t + 1) * N].rearrange("(c p) t -> p c t", p=128), nv, reads=xnk, eng='sync')
        else:
            if part == 0:
                return
            dst = k.xD1[:, t * N:(t + 1) * N] if isx else k.ctxD1[:, :]
            ph.dma(dst.rearrange("(c p) t -> p c t", p=128), xv, reads=[xt], eng='sync')

    NTL = len(tiles)
    load(0)
    if NTL > 1:
        load(1)
    emit_big_weights()
    st_A(0)
    st_Bn(0)
    st_Bs(0)
    st_Bc(0)
    for ti in range(NTL):
        st_C(ti, range(0, 2))
        if ti >= 1:
            st_out(ti - 1, 1)
            if ti + 1 < NTL:
                load(ti + 1)
        st_C(ti, range(2, NFC))
        if ti + 1 < NTL:
            st_A(ti + 1)
            st_Bn(ti + 1)
            st_D(ti, range(0, 2))
            st_Bs(ti + 1)
            st_Bc(ti + 1)
            st_D(ti, range(2, 8))
        else:
            st_D(ti, range(0, 8))
        st_out(ti, 0)
    st_out(NTL - 1, 1)
    ph.finish()


def build_program(S):
    nc = bass.Bass("TRN2", target_bir_lowering=False)
    k = K()
    k.nc = nc
    k.S = S
    L1 = S // 128

    def din(name, shape, dt=F32):
        return nc.dram_tensor(name, list(shape), dt, kind="ExternalInput").ap()

    def dscr(name, shape, dt):
        return nc.dram_tensor(name, list(shape), dt).ap()

    k.xT = din("xT", [D, S])
    k.ctxT = din("ctxT", [D, CTX])
    k.cvec = din("cvec", [128, 16])
    k.small = din("small", [128, SMW])
    k.w_ada = din("w_ada", [DEPTH, D, 6 * D])
    k.w_in_p = din("w_in_p", [DEPTH, D, 2304])
    k.w_in_fT = din("w_in_fT", [DEPTH, 256, D])
    k.w_out_p = din("w_out_p", [DEPTH, D, D])
    k.w_gate = din("w_gate", [DEPTH, D, DFF])
    k.w_up = din("w_up", [DEPTH, D, DFF])
    k.w_down = din("w_down", [DEPTH, DFF, D])
    k.sinkrow = din("sinkrow", [DEPTH, 1, 6 * 256])
    k.ropeC = din("ropeC", [128, S])
    k.ropeS = din("ropeS", [128, S])
    k.cmat = din("cmat", [128, 512])
    k.cbf = din("cbf", [128, CBW])
    k.yT = nc.dram_tensor("yT", [D, S], F32, kind="ExternalOutput").ap()

    k.wfD = dscr("wfD", [DEPTH, D, 512], BF16)
    k.qaD = dscr("qaD", [3, 128, CTX + S], BF16)
    k.qcD = dscr("qcD", [3, 128, CTX + S], BF16)
    k.kaD = dscr("kaD", [128, CTX + S], BF16)
    k.kcD = dscr("kcD", [128, CTX + S], BF16)
    k.vD = dscr("vD", [CTX + S, VW], BF16)
    k.zD = dscr("zD", [2, S, 256], BF16)
    k.zDc = dscr("zDc", [2, CTX, 256], BF16)
    k.ZD = dscr("ZD", [2, L1, 128, 256], BF16)
    k.oD = dscr("oD", [8, 128, CTX + S], BF16)
    k.woB = dscr("woB", [DEPTH, D, D], BF16)
    k.wgB = dscr("wgB", [DEPTH, D, DFF], BF16)
    k.wuB = dscr("wuB", [DEPTH, D, DFF], BF16)
    k.wdB = dscr("wdB", [DEPTH, DFF, D], BF16)
    k.winB = dscr("winB", [DEPTH, D, 2304], BF16)
    k.xD1 = dscr("xD1", [D, S], F32)
    k.ctxD1 = dscr("ctxD1", [D, CTX], F32)

    with contextlib.ExitStack() as st:
        Phase.state = SemState(nc, st)
        k.pp = st.enter_context(nc.sbuf_tensor("pp_persist", [128, PPW], F32))
        k.sm = st.enter_context(nc.sbuf_tensor("sm_persist", [128, SMW], F32))
        import os
        sel = os.environ.get("KPHASES", "")
        for l in range(DEPTH):
            for nm, f in (("A", phase_A), ("B", phase_B), ("C", phase_C), ("D", phase_D), ("E", phase_E), ("G", phase_G)):
                if sel and ("%s%d" % (nm, l)) not in sel.split(","):
                    continue
                f(k, l)
    return nc


def host_constants(S):
    L1 = S // 128
    f32 = np.float32
    tok = np.arange(S)
    row = (tok // 64).astype(f32)
    col = (tok % 64).astype(f32)
    inv_freq = (np.float32(10000.0) ** (-np.arange(0, 32, 2, dtype=f32) / np.float32(32.0))).astype(f32)
    ang = np.concatenate([row[:, None] * inv_freq, col[:, None] * inv_freq], axis=-1).astype(f32)
    cos = np.cos(ang).astype(f32).T
    sin = np.sin(ang).astype(f32).T
    ropeC = np.concatenate([cos, cos, cos, cos], axis=0)
    ropeS = np.concatenate([-sin, sin, -sin, sin], axis=0)
    c = np.arange(64)
    a64 = 2 * np.pi * np.outer(c, c) / 64.0
    C64, S64 = np.cos(a64), np.sin(a64)
    Z = np.zeros((64, 64))
    cmat = np.zeros((128, 512), f32)
    cmat[:, 0:128] = np.block([[C64, Z], [Z, C64]])
    cmat[:, 128:256] = np.block([[-S64, Z], [Z, -S64]])
    k1 = np.arange(L1)
    l2 = np.arange(128)
    at = 2 * np.pi * np.outer(k1, l2) / S
    cmat[0:2 * L1, 256:384] = np.concatenate([np.cos(at), np.cos(at)], axis=0)
    cmat[0:2 * L1, 384:512] = np.concatenate([np.sin(at), np.sin(at)], axis=0)
    cbf = np.zeros((128, CBW), f32)
    a1 = 2 * np.pi * np.outer(k1, k1) / L1
    Cc, Sc = np.cos(a1), np.sin(a1)
    M1 = np.block([[Cc, Sc], [-Sc, Cc]])
    M2 = np.block([[-Sc, Cc], [-Cc, -Sc]])
    cbf[0:2 * L1, CB_M1:CB_M1 + 2 * L1] = M1.T
    cbf[0:2 * L1, CB_M2:CB_M2 + 2 * L1] = M2.T
    a128 = 2 * np.pi * np.outer(l2, l2) / 128.0
    cbf[:, CB_C128:CB_C128 + 128] = np.cos(a128)
    cbf[:, CB_S128:CB_S128 + 128] = np.sin(a128)
    n256 = np.arange(256)
    a256 = 2 * np.pi * np.outer(n256, n256) / 256.0
    C256, S256 = np.cos(a256), np.sin(a256)
    for cch in range(2):
        cbf[:, CB_C256 + cch * 256:CB_C256 + (cch + 1) * 256] = C256[cch * 128:(cch + 1) * 128, :]
        cbf[:, CB_S256 + cch * 256:CB_S256 + (cch + 1) * 256] = S256[cch * 128:(cch + 1) * 128, :]
    a = np.arange(128)[:, None]
    i = np.arange(128)[None, :]
    cbf[:, CB_MASK:CB_MASK + 128] = (a >= i)
    cbf[:, CB_MASK + 128:CB_MASK + 256] = (a <= i)
    return ropeC.astype(f32), ropeS.astype(f32), cmat, cbf.astype(f32)


def host_layout(inp, S):
    f32 = np.float32
    g = lambda n: np.asarray(inp[n], dtype=f32)
    w_in = g('w_in')

    def pair(base, j):
        return np.concatenate([np.arange(base + 64 * j, base + 64 * j + 64), np.arange(base + 64 * (3 + j), base + 64 * (3 + j) + 64)])

    def swap(idx):
        idx = idx.reshape(-1, 64)
        return np.concatenate([idx[:, 32:], idx[:, :32]], axis=1).reshape(-1)

    cols = []
    qa = [pair(0, j) for j in range(3)]
    ka = np.arange(384, 512)
    qc = [pair(896, j) for j in range(3)]
    kc = np.arange(1280, 1408)
    cols += qa + [swap(q) for q in qa] + [ka, swap(ka)] + qc + [swap(q) for q in qc] + [kc, swap(kc)]
    cols += [np.arange(512, 640), np.arange(1408, 1536)]
    cols = np.concatenate(cols)
    assert cols.shape[0] == 2304
    w_in_p = np.ascontiguousarray(w_in[:, :, cols])
    w_in_fT = np.ascontiguousarray(np.transpose(w_in[:, :, 640:896], (0, 2, 1)))
    rows = np.concatenate([pair(0, j) for j in range(3)] + [np.arange(384, 640)] + [pair(640, j) for j in range(3)])
    w_out_p = np.ascontiguousarray(g('w_out')[:, rows, :])

    def pl(v):
        return np.ascontiguousarray(v.reshape(8, 128).T)

    small = np.zeros((128, SMW), f32)
    b_ada, g_mix, g_ffn = g('b_ada'), g('g_mix'), g('g_ffn')
    qn, kn = g('q_norm'), g('k_norm')
    p = np.arange(128) % 64
    ps = (p + 32) % 64
    for l in range(DEPTH):
        small[:, SM_BADA(l):SM_BADA(l) + 48] = b_ada[l].reshape(48, 128).T
        small[:, SM_GMIX(l):SM_GMIX(l) + 8] = pl(g_mix[l])
        small[:, SM_GFFN(l):SM_GFFN(l) + 8] = pl(g_ffn[l])
        small[:, SM_QKN(l) + 0] = qn[l][p]
        small[:, SM_QKN(l) + 1] = qn[l][ps]
        small[:, SM_QKN(l) + 2] = kn[l][p]
        small[:, SM_QKN(l) + 3] = kn[l][ps]
    small[:, SM_GFIN:SM_GFIN + 8] = pl(g('g_final'))
    sinkrow = np.ascontiguousarray(np.repeat(g('sink')[:, None, :, None], 256, axis=3).reshape(DEPTH, 1, 6 * 256))
    ropeC, ropeS, cmat, cbf = host_constants(S)
    shared = dict(small=small, w_ada=g('w_ada'), w_in_p=w_in_p, w_in_fT=w_in_fT, w_out_p=w_out_p,
                  w_gate=g('w_gate'), w_up=g('w_up'), w_down=g('w_down'), sinkrow=sinkrow,
                  ropeC=ropeC, ropeS=ropeS, cmat=cmat, cbf=cbf)
    x, c, ctx, c_ctx = g('x'), g('c'), g('ctx'), g('c_ctx')
    B = x.shape[0]
    maps = []
    for b in range(B):
        cvec = np.zeros((128, 16), f32)
        cvec[:, 0::2] = pl(c[b])
        cvec[:, 1::2] = pl(c_ctx)
        m = dict(shared)
        m['xT'] = np.ascontiguousarray(x[b].T)
        m['ctxT'] = np.ascontiguousarray(ctx[b].T)
        m['cvec'] = cvec
        maps.append(m)
    return maps


_CACHE = {}


def run(inp, S):
    maps = host_layout(inp, S)
    if S not in _CACHE:
        _CACHE[S] = build_program(S)
    nc = _CACHE[S]
    res = run_bass_kernel_spmd(nc, maps, core_ids=list(range(len(maps))))
    out = np.stack([np.ascontiguousarray(r["yT"].T) for r in res.results], axis=0)
    return out.astype(np.float32)


def kernel(**inputs):
    return run(inputs, 8192)
```

```python
import contextlib
import math
import numpy as np
import concourse.bass as bass
import concourse.mybir as mybir
from concourse.bass_utils import run_bass_kernel_spmd

F32 = mybir.dt.float32
BF16 = mybir.dt.bfloat16
AF = mybir.ActivationFunctionType
ALU = mybir.AluOpType

D = 1024
DFF = 2816
NFC = DFF // 128
CTX = 256
DEPTH = 2
EPS = 1e-6
VW = 512
ENGS = ['tensor', 'vector', 'scalar', 'gpsimd', 'sync']


class Buf:
    __slots__ = ('name', 'last_w', 'readers')

    def __init__(self, name):
        self.name = name
        self.last_w = None
        self.readers = {}


class Tile:
    def __init__(self, t, name):
        self.t = t
        self.b = Buf(name)

    def __getitem__(self, k):
        return self.t[k]


class SemState:
    def __init__(self, nc, st, n_dma_sems=26):
        self.cnt = {e: 0 for e in ENGS}
        self.known = {e: {} for e in ENGS}
        self.n_dma = n_dma_sems
        self.dma_val = [0] * n_dma_sems
        self.n_hw = n_dma_sems - 10
        self.rr_hw = 0
        self.rr_sw = 0
        self.sems = {}
        for e in ENGS:
            self.sems[e] = st.enter_context(nc.semaphore("sem_" + e))
        for k in range(n_dma_sems):
            self.sems[('dma', k)] = st.enter_context(nc.semaphore("sem_d%d" % k))


class Sched:
    def __init__(self, nc, state):
        self.nc = nc
        self.ops = {e: [] for e in ENGS}
        self.state = state

    @property
    def cnt(self):
        return self.state.cnt

    @property
    def known(self):
        return self.state.known

    @property
    def dma_val(self):
        return self.state.dma_val

    @property
    def n_dma(self):
        return self.state.n_dma

    def emit(self, engine, fn, reads=(), writes=(), dma=False, signal=True):
        need = {}

        def add(ev):
            if ev is None:
                return
            k, v = ev
            if need.get(k, 0) < v:
                need[k] = v

        for b in reads:
            add(b.last_w)
        for b in writes:
            add(b.last_w)
            for k, v in b.readers.items():
                add((k, v))
        if engine == 'tensor':
            need.pop('tensor', None)
        kd = None
        if dma:
            stt_ = self.state
            if engine == 'gpsimd':
                kd = stt_.n_hw + stt_.rr_sw
                stt_.rr_sw = (stt_.rr_sw + 1) % (stt_.n_dma - stt_.n_hw)
            else:
                kd = stt_.rr_hw
                stt_.rr_hw = (stt_.rr_hw + 1) % stt_.n_hw
            if self.dma_val[kd] > 0:
                add((('dma', kd), self.dma_val[kd]))
        kn = self.known[engine]
        waits = []
        for k, v in need.items():
            if kn.get(k, 0) >= v:
                continue
            kn[k] = v
            waits.append((k, v))
        if dma:
            self.dma_val[kd] += 16
            ev = (('dma', kd), self.dma_val[kd])
            inc = ev
        elif signal:
            self.cnt[engine] += 1
            ev = (engine, self.cnt[engine])
            inc = ev
        else:
            ev = (engine, self.cnt[engine] + 1)
            inc = None
        self.ops[engine].append((waits, fn, inc))
        for b in writes:
            b.last_w = ev
            b.readers = {}
        for b in reads:
            if b.last_w is not ev:
                if b.readers.get(ev[0], 0) < ev[1]:
                    b.readers[ev[0]] = ev[1]
        return ev

    def wait_all(self, engine):
        need = {}
        for e in ENGS:
            if e != engine and self.cnt[e] > 0:
                need[e] = self.cnt[e]
        for k in range(self.n_dma):
            if self.dma_val[k] > 0:
                need[('dma', k)] = self.dma_val[k]
        kn = self.known[engine]
        waits = [(k, v) for k, v in need.items() if kn.get(k, 0) < v]
        for k, v in waits:
            kn[k] = v
        self.ops[engine].append((waits, None, None))

    def build(self):
        nc = self.nc
        sems = self.state.sems
        with nc.Block() as block:
            def run(engname):
                def body(eng):
                    for waits, fn, inc in self.ops[engname]:
                        if fn is None:
                            for k, v in waits:
                                eng.wait_ge(sems[k], v)
                            continue
                        for k, v in waits[1:]:
                            eng.wait_ge(sems[k], v)
                        ins = fn(eng)
                        if waits:
                            k, v = waits[0]
                            ins._wait_ge(sems[k], v)
                        if inc is not None:
                            k, v = inc
                            ins.then_inc(sems[k], 16 if isinstance(k, tuple) else 1)
                return body

            block.tensor(run('tensor'))
            block.vector(run('vector'))
            block.scalar(run('scalar'))
            block.gpsimd(run('gpsimd'))
            block.sync(run('sync'))


class Phase:
    _uid = [0]
    state = None

    def __init__(self, nc, name):
        self.nc = nc
        self.name = name
        self.st = contextlib.ExitStack()
        self.S = Sched(nc, Phase.state)
        self.banks = []

    def _nm(self, name):
        Phase._uid[0] += 1
        return "%s_%s_%d" % (self.name, name, Phase._uid[0])

    def sb(self, name, shape, dt, n=1):
        out = []
        for i in range(n):
            nm = self._nm(name)
            t = self.st.enter_context(self.nc.sbuf_tensor(nm, list(shape), dt))
            out.append(Tile(t, nm))
        return out if n > 1 else out[0]

    def psum_banks(self, n=8):
        for i in range(n):
            nm = self._nm("bank")
            t = self.st.enter_context(self.nc.psum_tensor(nm, [128, 512], F32))
            self.banks.append(Tile(t, nm))
        return self.banks

    def wrap(self, t, name):
        return Tile(t, self._nm(name))

    def dma(self, out, in_, reads=(), writes=(), eng='sync'):
        self.S.emit(eng, lambda e: e.dma_start(out=out, in_=in_), reads=[r.b for r in reads],
                    writes=[w.b for w in writes], dma=True)

    def mm(self, out, lhsT, rhs, start, stop, reads, bank, signal=True, skip=False):
        if skip:
            fn = lambda e: e.matmul(out, lhsT=lhsT, rhs=rhs, start=start, stop=stop, skip_group_check=True)
        else:
            fn = lambda e: e.matmul(out, lhsT=lhsT, rhs=rhs, start=start, stop=stop)
        self.S.emit('tensor', fn, reads=[r.b for r in reads], writes=[bank.b], signal=signal)

    def op(self, eng, fn, reads=(), writes=()):
        self.S.emit(eng, fn, reads=[r.b for r in reads], writes=[w.b for w in writes])

    def act(self, out, in_, func, reads, writes, scale=None, bias=None, eng='scalar'):
        kw = {}
        if scale is not None:
            kw['scale'] = scale
        if bias is not None:
            kw['bias'] = bias
        self.op(eng, lambda e: e.activation(out=out, in_=in_, func=func, **kw), reads, writes)

    def tt(self, out, in0, in1, op, reads, writes, eng='vector'):
        self.op(eng, lambda e: e.tensor_tensor(out=out, in0=in0, in1=in1, op=op), reads, writes)

    def stt(self, out, in0, scalar, in1, op0, op1, reads, writes):
        self.op('vector', lambda e: e.scalar_tensor_tensor(out=out, in0=in0, scalar=scalar, in1=in1, op0=op0, op1=op1),
                reads, writes)

    def ts(self, out, in0, s1, s2, op0, op1, reads, writes, eng='gpsimd'):
        if op1 is None:
            self.op(eng, lambda e: e.tensor_scalar(out=out, in0=in0, scalar1=s1, scalar2=None, op0=op0), reads, writes)
        else:
            self.op(eng, lambda e: e.tensor_scalar(out=out, in0=in0, scalar1=s1, scalar2=s2, op0=op0, op1=op1), reads, writes)

    def copy(self, out, in_, reads, writes, eng='vector'):
        if eng == 'scalar':
            self.op(eng, lambda e: e.activation(out=out, in_=in_, func=AF.Copy), reads, writes)
        else:
            self.op(eng, lambda e: e.tensor_copy(out=out, in_=in_), reads, writes)

    def memset(self, ap, val, writes, eng='vector'):
        self.op(eng, lambda e: e.memset(ap, val), (), writes)

    def finish(self):
        for e in ENGS:
            self.S.wait_all(e)
        self.S.build()
        self.st.close()


def v3(ap, inner):
    return ap.rearrange("p (a b) -> p a b", b=inner)


def MOD(l, who, i):
    return ((l * 2 + who) * 6 + i) * 8
GS_BASE = DEPTH * 2 * 6 * 8
def GS(l, who, k):
    return GS_BASE + ((l * 2 + who) * 2 + k) * 8
PPW = GS_BASE + DEPTH * 2 * 2 * 8
def SM_BADA(l): return l * 48
def SM_GMIX(l): return DEPTH * 48 + l * 8
def SM_GFFN(l): return DEPTH * 48 + DEPTH * 8 + l * 8
SM_GFIN = DEPTH * 48 + 2 * DEPTH * 8
def SM_QKN(l): return SM_GFIN + 8 + l * 4
SMW = SM_GFIN + 8 + DEPTH * 4


class K:
    pass


def rstd_from_sums(ph, bank_ap, n_div, tmp, out, reads_bank, tmp_t, out_t):
    ph.act(tmp, bank_ap, AF.Ln, reads=[], writes=[reads_bank, tmp_t], scale=1.0 / n_div, bias=EPS)
    ph.act(out, tmp, AF.Exp, reads=[tmp_t], writes=[out_t], scale=-0.5)


def precast(ph, k, l, ffn=True, win=True, dep=()):
    if win and l < DEPTH:
        for r in range(0, D, 256):
            ph.dma(k.winB[l, r:r + 256, :], k.w_in_p[l, r:r + 256, :], reads=dep, eng='gpsimd')
    if ffn and l < DEPTH:
        for r in range(0, D, 256):
            ph.dma(k.woB[l, r:r + 256, :], k.w_out_p[l, r:r + 256, :], reads=dep, eng='gpsimd')
            ph.dma(k.wgB[l, r:r + 256, :], k.w_gate[l, r:r + 256, :], reads=dep, eng='gpsimd')
            ph.dma(k.wuB[l, r:r + 256, :], k.w_up[l, r:r + 256, :], reads=dep, eng='gpsimd')
        for r in range(0, DFF, 256):
            ph.dma(k.wdB[l, r:r + 256, :], k.w_down[l, r:r + 256, :], reads=dep, eng='gpsimd')


def phase_A(k, l):
    nc = k.nc
    ph = Phase(nc, "A%d" % l)
    banks = ph.psum_banks(4)
    pp = ph.wrap(k.pp, "pp")
    sm = ph.wrap(k.sm, "sm")
    cv = ph.sb("cv", [128, 16], F32)
    sc = ph.sb("sc", [128, 16], F32)
    wst = ph.sb("wst", [128, 6 * D], F32, n=2)
    wfT = ph.sb("wfT", [128, 2 * D], F32)
    bd = ph.sb("bd", [128, 256], F32)
    wfo = ph.sb("wfo", [128, 512], BF16, n=2)
    if l == 0:
        ph.dma(sm[:, :], k.small[:, :], writes=[sm])
        precast(ph, k, 0, ffn=False, win=True)
    ph.dma(cv[:, :], k.cvec[:, :], writes=[cv])
    ph.act(sc[:, :], cv[:, :], AF.Silu, reads=[cv], writes=[sc])
    b0 = banks[0]
    ph.memset(b0[:, 0:96], 0.0, writes=[b0])
    import os
    dbg = os.environ.get("KDBG", "")
    for kc in range(0 if 'nomm' in dbg else 8):
        w = wst[kc % 2]
        ph.dma(w[:, :], k.w_ada[l, kc * 128:(kc + 1) * 128, :], writes=[w])
        for cc in range(48):
            ph.mm(b0[:, cc * 2:cc * 2 + 2], w[:, cc * 128:(cc + 1) * 128], sc[:, kc * 2:kc * 2 + 2],
                  start=False, stop=(kc == 7), reads=[w, sc], bank=b0, signal=(cc == 47), skip=True)
    for who in range(2):
        src = v3(b0[:, 0:96], 2)[:, :, who]
        c0 = MOD(l, who, 0)
        ph.tt(pp[:, c0:c0 + 48], src, sm[:, SM_BADA(l):SM_BADA(l) + 48], ALU.add, reads=[sm], writes=[b0, pp])
        for kk, (i_sc, gcol) in enumerate(((1, SM_GMIX(l)), (4, SM_GFFN(l)))):
            cs = MOD(l, who, i_sc)
            cg = GS(l, who, kk)
            ph.stt(pp[:, cg:cg + 8], pp[:, cs:cs + 8], 1.0, sm[:, gcol:gcol + 8], ALU.add, ALU.mult,
                   reads=[sm], writes=[pp])
    ph.dma(v3(wfT[:, :], D), k.w_in_fT[l].rearrange("(c p) d -> p c d", p=128), writes=[wfT])
    ph.dma(bd[:, :], k.cmat[:, 0:256], writes=[bd])
    for dc in range(0 if 'nowf' in dbg else 8):
        bk = banks[1 + dc % 2]
        for t in range(2):
            for fc in range(2):
                q = t * 2 + fc
                ph.mm(bk[:, q * 128:(q + 1) * 128], wfT[:, fc * D + dc * 128: fc * D + (dc + 1) * 128],
                      bd[:, t * 128:(t + 1) * 128], start=True, stop=True, reads=[wfT, bd], bank=bk,
                      signal=(q == 3))
        o = wfo[dc % 2]
        ph.copy(o[:, :], bk[:, :], reads=[], writes=[bk, o])
        ph.dma(k.wfD[l, dc * 128:(dc + 1) * 128, :], o[:, :], reads=[o], eng='gpsimd')
    ph.finish()


CH_QA, CH_QAS, CH_KA, CH_KAS, CH_QC, CH_QCS, CH_KC, CH_KCS = 0, 3, 6, 7, 8, 11, 14, 15
TM0 = 2048
WCOLS = 2048 + 768


def phase_B(k, l):
    nc = k.nc
    S = k.S
    NT = S // 512
    ph = Phase(nc, "B%d" % l)
    banks = ph.psum_banks(8)
    pp = ph.wrap(k.pp, "pp")
    sm = ph.wrap(k.sm, "sm")
    wT = ph.sb("wT", [128, 8 * WCOLS], BF16)
    wv = v3(wT[:, :], WCOLS)
    xts = ph.sb("xt", [128, 8 * 512], F32, n=2)
    sq = ph.sb("sq", [128, 8 * 512], BF16)
    hTs = ph.sb("hT", [128, 8 * 512], BF16, n=2)
    ones = ph.sb("ones", [128, 128], BF16)
    bones = ph.sb("bones", [128, 128], BF16)
    lnt = ph.sb("lnt", [128, 512], F32)
    rstd = ph.sb("rstd", [128, 512], F32)
    rC = ph.sb("rC", [128, 512], F32, n=3)
    rS = ph.sb("rS", [128, 512], F32, n=3)
    sqh = ph.sb("sqh", [128, 512], BF16, n=2)
    lnh = ph.sb("lnh", [128, 512], F32)
    rinv = ph.sb("rinv", [128, 512], F32, n=2)
    t1s = ph.sb("t1", [128, 512], F32, n=2)
    t2s = ph.sb("t2", [128, 512], F32, n=2)
    qo = ph.sb("qo", [128, 512], BF16, n=4)
    vts = ph.sb("vt", [128, VW], BF16, n=4)
    zts = ph.sb("zt", [128, 512], BF16, n=4)

    ph.memset(ones[:, :], 1.0, writes=[ones])
    ph.memset(bones[:, :], 0.0, writes=[bones])
    ph.memset(bones[0:64, 0:64], 1.0, writes=[bones])
    ph.memset(bones[64:128, 64:128], 1.0, writes=[bones])
    for vt in vts:
        ph.memset(vt[:, :], 1.0, writes=[vt], eng='gpsimd')
    for c0 in range(0, 2304, 768):
        ph.dma(wv[:, :, c0:c0 + 768], k.winB[l, :, c0:c0 + 768].rearrange("(c p) f -> p c f", p=128), writes=[wT])
    ph.dma(wv[:, :, 2304:2816], k.wfD[l].rearrange("(c p) f -> p c f", p=128), writes=[wT])

    qk = SM_QKN(l)
    gq, gqs, gk, gks = (sm[:, qk + i:qk + i + 1] for i in range(4))
    pair_i = [0]
    qo_i = [0]
    vt_i = [0]

    tiles = [('ctx', 0)] + [('x', t) for t in range(NT)]

    def info(ti):
        kind, t = tiles[ti]
        isx = kind == 'x'
        N = 512 if isx else CTX
        off = CTX + t * 512 if isx else 0
        who = 0 if isx else 1
        need_q = isx or l == 0
        xt, hT = xts[ti % 2], hTs[ti % 2]
        return kind, t, isx, N, off, who, need_q, xtk[ti % 2], hTk[ti % 2], v3(xt[:, :], 512), v3(hT[:, :], 512)

    sv = v3(sq[:, :], 512)
    pending = []
    xtk = [[Tile(x_.t, "xk%d_%d" % (i, c)) for c in range(8)] for i, x_ in enumerate(xts)]
    hTk = [[Tile(h_.t, "hk%d_%d" % (i, c)) for c in range(8)] for i, h_ in enumerate(hTs)]

    def pre(ti):
        kind, t, isx, N, off, who, need_q, xt, hT, xv, hv = info(ti)
        if isx:
            src = (k.xT if l == 0 else k.xD1)[:, t * 512:(t + 1) * 512]
        else:
            src = (k.ctxT if l == 0 else k.ctxD1)[:, :]
        ph.dma(xv[:, :, 0:N], src.rearrange("(c p) t -> p c t", p=128), writes=xt)
        if isx:
            c_t, s_t = rC[t % 3], rS[t % 3]
            ph.dma(c_t[:, :], k.ropeC[:, t * 512:(t + 1) * 512], writes=[c_t])
            ph.dma(s_t[:, :], k.ropeS[:, t * 512:(t + 1) * 512], writes=[s_t])

    def chain1(ti):
        kind, t, isx, N, off, who, need_q, xt, hT, xv, hv = info(ti)
        ph.act(sv[:, :, 0:N], xv[:, :, 0:N], AF.Square, reads=xt, writes=[sq])

    def chain2(ti):
        kind, t, isx, N, off, who, need_q, xt, hT, xv, hv = info(ti)
        bS = banks[0]
        for kc in range(8):
            ph.mm(bS[:, 0:N], ones[:, :], sv[:, kc, 0:N], start=(kc == 0), stop=(kc == 7), reads=[ones, sq],
                  bank=bS, signal=(kc == 7))

    def chain3(ti):
        kind, t, isx, N, off, who, need_q, xt, hT, xv, hv = info(ti)
        bS = banks[0]
        rstd_from_sums(ph, bS[:, 0:N], D, lnt[:, 0:N], rstd[:, 0:N], bS, lnt, rstd)
        g0 = GS(l, who, 0)
        s0 = MOD(l, who, 0)
        for kc in range(8):
            ph.tt(xv[:, kc, 0:N], xv[:, kc, 0:N], rstd[:, 0:N], ALU.mult, reads=[rstd], writes=[xt[kc]])
            if kc % 3 == 2:
                ph.act(hv[:, kc, 0:N], xv[:, kc, 0:N], AF.Identity, reads=[xt[kc], pp], writes=[hT[kc]],
                       scale=pp[:, g0 + kc:g0 + kc + 1], bias=pp[:, s0 + kc:s0 + kc + 1])
            else:
                ph.ts(hv[:, kc, 0:N], xv[:, kc, 0:N], pp[:, g0 + kc:g0 + kc + 1], pp[:, s0 + kc:s0 + kc + 1],
                      ALU.mult, ALU.add, reads=[xt[kc], pp], writes=[hT[kc]], eng='gpsimd')

    def work(ti, part):
        kind, t, isx, N, off, who, need_q, xt, hT, xv, hv = info(ti)
        if isx:
            c_t, s_t = rC[t % 3], rS[t % 3]

        def proj(ch, bank):
            for kc in range(8):
                ph.mm(bank[:, 0:N], wv[:, kc, ch * 128:(ch + 1) * 128], hv[:, kc, 0:N], start=(kc == 0),
                      stop=(kc == 7), reads=[wT, hT[kc]], bank=bank, signal=(kc == 7))

        def out_tile():
            o = qo[qo_i[0] % 4]
            qo_i[0] += 1
            return o

        jobs = []
        for j in range(3):
            if need_q:
                jobs.append((CH_QA + j, CH_QAS + j, True, gq, gqs, k.qaD[j, :, off:off + N]))
        jobs.append((CH_KA, CH_KAS, True, gk, gks, k.kaD[:, off:off + N]))
        for j in range(3):
            if need_q:
                jobs.append((CH_QC + j, CH_QCS + j, False, None, None, k.qcD[j, :, off:off + N]))
        jobs.append((CH_KC, CH_KCS, False, None, None, k.kcD[:, off:off + N]))
        nsplit = 3 if len(jobs) > 3 else 1
        jobs = jobs[:nsplit] if part == 0 else jobs[nsplit:]
        for (ch, chs, normed, g, gs_, dst) in jobs:
            pi = pair_i[0]
            pair_i[0] += 1
            b1 = banks[1 + 2 * (pi % 2)]
            b2 = banks[2 + 2 * (pi % 2)]
            bN = banks[5]
            proj(ch, b1)
            if isx:
                proj(chs, b2)
            o = out_tile()
            t1 = t1s[pi % 2]
            t2 = t2s[pi % 2]
            sh_ = sqh[pi % 2]
            ri = rinv[pi % 2]
            if normed:
                ph.act(sh_[:, 0:N], b1[:, 0:N], AF.Square, reads=[], writes=[b1, sh_])

            def back(normed=normed, g=g, gs_=gs_, dst=dst, b1=b1, b2=b2, bN=bN, o=o, t1=t1, t2=t2, sh_=sh_, ri=ri,
                     isx=isx, N=N):
                if normed:
                    ph.mm(bN[:, 0:N], bones[:, :], sh_[:, 0:N], start=True, stop=True, reads=[bones, sh_], bank=bN)
                    rstd_from_sums(ph, bN[:, 0:N], 64, lnh[:, 0:N], ri[:, 0:N], bN, lnh, ri)
                    if isx:
                        ph.stt(t1[:, 0:N], b1[:, 0:N], g, c_t[:, 0:N], ALU.mult, ALU.mult, reads=[sm, c_t], writes=[b1, t1])
                        ph.stt(t2[:, 0:N], b2[:, 0:N], gs_, s_t[:, 0:N], ALU.mult, ALU.mult, reads=[sm, s_t], writes=[b2, t2])
                        ph.tt(t1[:, 0:N], t1[:, 0:N], t2[:, 0:N], ALU.add, reads=[t2], writes=[t1], eng='gpsimd')
                        ph.tt(o[:, 0:N], t1[:, 0:N], ri[:, 0:N], ALU.mult, reads=[t1, ri], writes=[o])
                    else:
                        ph.stt(o[:, 0:N], b1[:, 0:N], g, ri[:, 0:N], ALU.mult, ALU.mult, reads=[sm, ri], writes=[b1, o])
                else:
                    if isx:
                        ph.tt(t1[:, 0:N], b1[:, 0:N], c_t[:, 0:N], ALU.mult, reads=[c_t], writes=[b1, t1])
                        ph.tt(t2[:, 0:N], b2[:, 0:N], s_t[:, 0:N], ALU.mult, reads=[s_t], writes=[b2, t2])
                        ph.tt(o[:, 0:N], t1[:, 0:N], t2[:, 0:N], ALU.add, reads=[t1, t2], writes=[o], eng='gpsimd')
                    else:
                        ph.act(o[:, 0:N], b1[:, 0:N], AF.Copy, reads=[], writes=[b1, o])
                ph.dma(dst, o[:, 0:N], reads=[o], eng='sync')

            if pending:
                pending.pop(0)()
            pending.append(back)
        if part == 0:
            return
        while pending:
            pending.pop(0)()
        for s_ in range(N // 128):
            b6, b7 = banks[6], banks[7]
            ncol = 768 if need_q else 256
            for kc in range(8):
                lw = hv[:, kc, s_ * 128:(s_ + 1) * 128]
                ph.mm(b6[:, 0:min(ncol, 512)], lw, wv[:, kc, TM0:TM0 + min(ncol, 512)], start=(kc == 0), stop=(kc == 7),
                      reads=[wT, hT[kc]], bank=b6, signal=(kc == 7 and not need_q))
                if need_q:
                    ph.mm(b7[:, 0:256], lw, wv[:, kc, TM0 + 512:TM0 + 768], start=(kc == 0), stop=(kc == 7),
                          reads=[wT, hT[kc]], bank=b7, signal=(kc == 7))
            vt = vts[vt_i[0] % 4]
            zt = zts[vt_i[0] % 4]
            vt_i[0] += 1
            vin = b6[:, 0:256].rearrange("p (a g d) -> p a g d", a=2, g=2)
            vout = vt[:, :].rearrange("p (a w) -> p a w", a=2)
            ph.copy(vout[:, :, 0:64], vin[:, :, 0, :], reads=[], writes=[b6, vt], eng='scalar')
            ph.copy(vout[:, :, 192:256], vin[:, :, 1, :], reads=[], writes=[b6, vt], eng='scalar')
            ph.dma(k.vD[off + s_ * 128: off + (s_ + 1) * 128, :], vt[:, :], reads=[vt], eng='sync')
            if need_q:
                ph.copy(zt[:, 0:256], b6[:, 256:512], reads=[], writes=[b6, zt], eng='scalar')
                ph.copy(zt[:, 256:512], b7[:, 0:256], reads=[], writes=[b7, zt], eng='scalar')
                zd = k.zD if isx else k.zDc
                r0 = (t * 512 if isx else 0) + s_ * 128
                ph.dma(zd[:, r0:r0 + 128, :].rearrange("r p j -> p r j"), v3(zt[:, :], 256), reads=[zt], eng='sync')

    NTL = len(tiles)
    pre(0)
    pre(1)
    chain1(0)
    chain2(0)
    chain3(0)
    for ti in range(NTL):
        if ti + 1 < NTL:
            chain1(ti + 1)
        work(ti, 0)
        if ti + 1 < NTL:
            chain2(ti + 1)
            chain3(ti + 1)
        if ti + 2 < NTL:
            pre(ti + 2)
        work(ti, 1)
    ph.finish()


def attn_epilogue(ph, bO, bR, half, N, oe, rr, onesf, selhi, dst_ap):
    if half == 0:
        ph.copy(oe[0:65, 0:N], bO[0:65, 0:N], reads=[], writes=[bO, oe])
        ph.op('vector', lambda e: e.reciprocal(out=rr[64:65, 0:N], in_=oe[64:65, 0:N]), reads=[oe], writes=[rr])
        ph.mm(bR[0:64, 0:N], onesf[64:65, 0:64], rr[64:65, 0:N], start=True, stop=True, reads=[onesf, rr], bank=bR)
        ph.tt(dst_ap, oe[0:64, 0:N] if dst_ap.ndim == 2 else v3(oe[0:64, 0:N], 128),
              bR[0:64, 0:N] if dst_ap.ndim == 2 else v3(bR[0:64, 0:N], 128), ALU.mult, reads=[oe], writes=[bR, ph._dst])
    else:
        ph.copy(oe[:, 0:N], bO[:, 0:N], reads=[], writes=[bO, oe])
        ph.op('vector', lambda e: e.reciprocal(out=rr[0:1, 0:N], in_=oe[0:1, 0:N]), reads=[oe], writes=[rr])
        ph.mm(bR[:, 0:N], selhi[0:1, 0:128], rr[0:1, 0:N], start=True, stop=True, reads=[selhi, rr], bank=bR)
        ph.tt(dst_ap, oe[64:128, 0:N] if dst_ap.ndim == 2 else v3(oe[64:128, 0:N], 128),
              bR[64:128, 0:N] if dst_ap.ndim == 2 else v3(bR[64:128, 0:N], 128), ALU.mult, reads=[oe], writes=[bR, ph._dst])


def attn_consts(ph):
    onesf = ph.sb("onesf", [128, 128], F32)
    selhi = ph.sb("selhi", [128, 128], F32)
    ph.memset(onesf[:, :], 1.0, writes=[onesf])
    ph.memset(selhi[:, :], 0.0, writes=[selhi])
    ph.memset(selhi[0:1, 64:128], 1.0, writes=[selhi])
    return onesf, selhi


def phase_C(k, l):
    nc = k.nc
    S = k.S
    NT = S // 512
    NKC = 2 + S // 128
    ph = Phase(nc, "C%d" % l)
    pairs = []
    for i in range(4):
        nm = ph._nm("pp")
        pairs.append(ph.st.enter_context(nc.psum_tensor(nm, [128, 1024], F32)))
    NST, LA = 3, 2
    ST = [Tile(pairs[0], "st0"), Tile(pairs[1], "st1"), Tile(pairs[2], "st2")]
    bO = [Tile(pairs[3], "bO0"), Tile(pairs[3], "bO1")]
    kT = ph.sb("kT", [128, CTX + S], BF16)
    vA = ph.sb("vA", [128, NKC * 256], BF16)
    vAv = v3(vA[:, :], 256)
    qts = ph.sb("qt", [128, 3 * 512], BF16, n=2)
    pts = ph.sb("pt", [128, 1024], BF16, n=4)
    oes = ph.sb("oe", [128, 1024], F32, n=2)
    rrs = ph.sb("rr", [128, 1024], F32, n=2)
    ots = ph.sb("ot", [128, 3 * 512], BF16, n=2)
    kTk, vAk = [], []
    for c0 in range(0, NKC, 8):
        c1 = min(NKC, c0 + 8)
        kTk.append(Tile(kT.t, "kTk%d" % c0))
        vAk.append(Tile(vA.t, "vAk%d" % c0))
        ph.dma(kT[:, c0 * 128:c1 * 128], k.kaD[:, c0 * 128:c1 * 128], writes=[kTk[-1]])
        ph.dma(vAv[:, c0:c1, :], k.vD[c0 * 128:c1 * 128, 0:256].rearrange("(c p) f -> p c f", p=128), writes=[vAk[-1]])

    tiles = ([('ctx', 0)] if l == 0 else []) + [('x', t) for t in range(NT)]

    def tile_info(ti):
        kind, t = tiles[ti]
        N = 512 if kind == 'x' else CTX
        off = CTX + t * 512 if kind == 'x' else 0
        return N, off

    def load_q(ti):
        N, off = tile_info(ti)
        qt = qts[ti % 2]
        ph.dma(v3(qt[:, :], 512)[:, :, 0:N], k.qaD[:, :, off:off + N].rearrange("c p t -> p c t"), writes=[qt])

    steps = []
    for ti in range(len(tiles)):
        nkc = 2 if tiles[ti][0] == 'ctx' else NKC
        for j in range(3):
            for c in range(nkc):
                steps.append((ti, j, c, nkc))

    def emit_qk(si):
        ti, j, c, nkc = steps[si]
        N, off = tile_info(ti)
        qt = qts[ti % 2]
        st = ST[si % NST]
        qv = v3(qt[:, :], 512)
        ph.mm(st[:, 0:N], kT[0:64, c * 128:(c + 1) * 128], qv[0:64, j, 0:N], start=True, stop=True,
              reads=[kTk[c // 8], qt], bank=st, signal=False)
        ph.mm(st[:, 512:512 + N], kT[64:128, c * 128:(c + 1) * 128], qv[64:128, j, 0:N], start=True, stop=True,
              reads=[kTk[c // 8], qt], bank=st, signal=True)

    deferred = []
    pctok = Tile(None, 'pctok')
    load_q(0)
    if len(tiles) > 1:
        load_q(1)
    for s0_ in range(min(LA, len(steps))):
        emit_qk(s0_)
    grp = 0
    for si, (ti, j, c, nkc) in enumerate(steps):
        N, off = tile_info(ti)
        if c == 0 and j == 0 and ti >= 1 and ti + 1 < len(tiles):
            load_q(ti + 1)
        if si + LA < len(steps):
            emit_qk(si + LA)
        st = ST[si % NST]
        pt = pts[si % 4]
        ph.act(v3(pt[:, :], 512)[:, :, 0:N], v3(st[:, :], 512)[:, :, 0:N], AF.Exp, reads=[], writes=[st, pt], scale=0.125)
        ph.mm(bO[0][:, 0:N], vAv[:, c, 0:128], pt[:, 0:N], start=(c == 0), stop=(c == nkc - 1),
              reads=[vAk[c // 8], pt], bank=bO[0], signal=False)
        ph.mm(bO[1][:, 512:512 + N], vAv[:, c, 128:256], pt[:, 512:512 + N], start=(c == 0), stop=(c == nkc - 1),
              reads=[vAk[c // 8], pt], bank=bO[1], signal=True)
        if c == nkc - 1:
            ot = ots[ti % 2]
            oe = oes[grp % 2]
            rr = rrs[grp % 2]
            grp += 1
            ov = v3(ot[:, :], 512)

            ph.copy(oe[:, 0:N], bO[0][:, 0:N], reads=[], writes=[bO[0], oe], eng='vector')
            ph.copy(oe[:, 512:512 + N], bO[1][:, 512:512 + N], reads=[], writes=[bO[1], oe])
            ph.op('vector', lambda e, rr=rr, oe=oe, N_=N: e.reciprocal(out=rr[0:64, 0:N_], in_=oe[64:128, 0:N_]),
                  reads=[oe], writes=[rr])
            ph.op('vector', lambda e, rr=rr, oe=oe, N_=N: e.reciprocal(out=rr[64:128, 512:512 + N_], in_=oe[0:64, 512:512 + N_]),
                  reads=[oe], writes=[rr])
            ph.tt(ov[0:64, j, 0:N], oe[0:64, 0:N], rr[0:64, 0:N], ALU.mult, reads=[oe, rr], writes=[ot])
            trig = grp == (4 if l == 0 else 1)
            ph.tt(ov[64:128, j, 0:N], oe[64:128, 512:512 + N], rr[64:128, 512:512 + N], ALU.mult, reads=[oe, rr],
                  writes=[ot, pctok] if trig else [ot])
            if j == 2:
                ph.dma(k.oD[0:3, :, off:off + N].rearrange("c p t -> p c t"), ov[:, :, 0:N], reads=[ot], eng='sync')
            if trig:
                precast(ph, k, l, ffn=True, win=False, dep=[pctok])
                precast(ph, k, l + 1, ffn=False, win=True, dep=[pctok])
    for _, fn in deferred:
        fn()
    ph.finish()


def phase_D(k, l):
    nc = k.nc
    S = k.S
    NT = S // 512
    NQB = S // 128
    NKC = 2 + NQB
    ph = Phase(nc, "D%d" % l)
    pairs = []
    for i in range(4):
        nm = ph._nm("pp")
        pairs.append(ph.st.enter_context(nc.psum_tensor(nm, [128, 1024], F32)))
    NST, LA = 3, 2
    ST = [Tile(pairs[0], "st0"), Tile(pairs[1], "st1"), Tile(pairs[2], "st2")]
    bO = [Tile(pairs[3], "bO0"), Tile(pairs[3], "bO1")]
    kT = ph.sb("kT", [128, CTX + S], BF16)
    vC = ph.sb("vC", [128, NKC * 256], BF16)
    vCv = v3(vC[:, :], 256)
    qts = ph.sb("qt", [128, 3 * 512], BF16, n=2)
    pts = ph.sb("pt", [128, 1024], BF16, n=4)
    NR = 3
    oes = ph.sb("oe", [128, 1024], F32, n=NR)
    rrs = ph.sb("rr", [128, 1024], F32, n=NR)
    ots = ph.sb("ot", [128, 3 * 512], BF16, n=2)
    msk = ph.sb("msk", [128, 256], BF16)
    sk32 = ph.sb("sk32", [1, 6 * 256], F32)
    skb = ph.sb("skb", [1, 6 * 256], BF16)
    esel = ph.sb("esel", [1, 256], BF16)
    kTk, vCk = [], []
    for c0 in range(0, NKC, 8):
        c1 = min(NKC, c0 + 8)
        kTk.append(Tile(kT.t, "kTk%d" % c0))
        vCk.append(Tile(vC.t, "vCk%d" % c0))
        ph.dma(kT[:, c0 * 128:c1 * 128], k.kcD[:, c0 * 128:c1 * 128], writes=[kTk[-1]])
        ph.dma(vCv[:, c0:c1, :], k.vD[c0 * 128:c1 * 128, 256:512].rearrange("(c p) f -> p c f", p=128), writes=[vCk[-1]])
    ph.dma(msk[:, :], k.cbf[:, CB_MASK:CB_MASK + 256], writes=[msk], eng='gpsimd')
    ph.dma(sk32[:, :], k.sinkrow[l, :, :], writes=[sk32])
    ph.act(skb[:, :], sk32[:, :], AF.Exp, reads=[sk32], writes=[skb])
    ph.memset(esel[:, :], 0.0, writes=[esel])
    ph.memset(esel[0:1, 64:128], 1.0, writes=[esel])
    ph.memset(esel[0:1, 128:192], 1.0, writes=[esel])
    skv = v3(skb[:, :], 256)
    mv = v3(msk[:, :], 128)
    ident = ph.sb("ident", [128, 128], BF16)
    nmk = ph.sb("nmk", [128, 2 * 384], BF16)
    nmv = v3(nmk[:, :], 384)
    ph.tt(ident[:, :], mv[:, 0, :], mv[:, 1, :], ALU.mult, reads=[msk], writes=[ident])
    for m_ in range(2):
        ph.op('vector', lambda e, m_=m_: e.tensor_scalar(
            out=v3(nmv[:, m_, :], 128), in0=mv[:, m_, :].unsqueeze(1).to_broadcast([128, 3, 128]),
            scalar1=30000.0, scalar2=-30000.0, op0=ALU.mult, op1=ALU.add), reads=[msk], writes=[nmk])

    tiles = ([('ctx', 0)] if l == 0 else []) + [('x', t) for t in range(NT)]

    def tile_info(ti):
        kind, t = tiles[ti]
        N = 512 if kind == 'x' else CTX
        off = CTX + t * 512 if kind == 'x' else 0
        return kind, t, N, off

    def load_q(ti):
        kind, t, N, off = tile_info(ti)
        qt = qts[ti % 2]
        ph.dma(v3(qt[:, :], 512)[:, :, 0:N], k.qcD[:, :, off:off + N].rearrange("c p t -> p c t"), writes=[qt])

    groups = []
    for ti in range(len(tiles)):
        kind, t, N, off = tile_info(ti)
        qt, ot = qts[ti % 2], ots[ti % 2]
        qv, ov = v3(qt[:, :], 512), v3(ot[:, :], 512)
        if kind == 'ctx':
            for j in range(3):
                g = dict(ti=ti, NN=CTX, n3=1, nq=CTX, chunks=[(0, 0, None), (128, 1, None)],
                         q=[qv[0:64, j, 0:CTX], qv[64:128, j, 0:CTX]],
                         sink=[skv[0:1, j, 0:CTX], skv[0:1, 3 + j, 0:CTX]],
                         dst=[ov[0:64, j, 0:CTX], ov[64:128, j, 0:CTX]], last=(j == 2))
                groups.append(g)
        else:
            for nb in range(4):
                n = t * 4 + nb
                chunks = [(0, 0, None), (128, 1, None)]
                if n - 1 >= 0:
                    chunks.append((CTX + (n - 1) * 128, 2 + n - 1, 0))
                chunks.append((CTX + n * 128, 2 + n, None))
                if n + 1 < NQB:
                    chunks.append((CTX + (n + 1) * 128, 2 + n + 1, 1))
                cs = slice(nb * 128, (nb + 1) * 128)
                g = dict(ti=ti, NN=384, n3=3, nq=128, chunks=chunks,
                         q=[qv[0:64, :, cs], qv[64:128, :, cs]],
                         sink=[skv[0:1, 0:3, 0:128], skv[0:1, 3:6, 0:128]],
                         dst=[ov[0:64, :, cs], ov[64:128, :, cs]], last=(nb == 3))
                groups.append(g)
    steps = []
    for gi, g in enumerate(groups):
        for ci in range(len(g['chunks'])):
            steps.append((gi, ci))

    def emit_qk(si):
        gi, ci = steps[si]
        g = groups[gi]
        NN = g['NN']
        kcol = g['chunks'][ci][0]
        st = ST[si % NST]
        qt = qts[g['ti'] % 2]
        kt_ = kTk[(kcol // 128) // 8]
        mi_ = g['chunks'][ci][2]
        nom = mi_ is None
        ph.mm(st[:, 0:NN], kT[0:64, kcol:kcol + 128], g['q'][0], start=True, stop=nom, reads=[kt_, qt], bank=st, signal=False)
        ph.mm(st[:, 512:512 + NN], kT[64:128, kcol:kcol + 128], g['q'][1], start=True, stop=nom, reads=[kt_, qt], bank=st,
              signal=nom)
        if not nom:
            ph.mm(st[:, 0:NN], ident[:, :], nmv[:, mi_, :], start=False, stop=True, reads=[ident, nmk], bank=st, signal=False)
            ph.mm(st[:, 512:512 + NN], ident[:, :], nmv[:, mi_, :], start=False, stop=True, reads=[ident, nmk], bank=st,
                  signal=True)

    deferred = []
    load_q(0)
    if len(tiles) > 1:
        load_q(1)
    for s0_ in range(min(LA, len(steps))):
        emit_qk(s0_)
    loaded = {0, 1}
    for si, (gi, ci) in enumerate(steps):
        g = groups[gi]
        NN, n3, nq, ti = g['NN'], g['n3'], g['nq'], g['ti']
        kind, t, N, off = tile_info(ti)
        nch = len(g['chunks'])
        kcol, vci, mi = g['chunks'][ci]
        if ci == 0 and ti + 1 < len(tiles) and (ti + 1) not in loaded and (gi == 0 or groups[gi - 1]['ti'] != ti):
            loaded.add(ti + 1)
            load_q(ti + 1)
        if si + LA < len(steps):
            emit_qk(si + LA)
        st = ST[si % NST]
        pt = pts[si % 4]
        ph.act(v3(pt[:, :], 512)[:, :, 0:NN], v3(st[:, :], 512)[:, :, 0:NN], AF.Exp, reads=[], writes=[st, pt], scale=0.125)
        ph.mm(bO[0][:, 0:NN], vCv[:, vci, 0:128], pt[:, 0:NN], start=(ci == 0), stop=False,
              reads=[vCk[vci // 8], pt], bank=bO[0], signal=False)
        ph.mm(bO[1][:, 512:512 + NN], vCv[:, vci, 128:256], pt[:, 512:512 + NN], start=(ci == 0), stop=False,
              reads=[vCk[vci // 8], pt], bank=bO[1], signal=(ci < nch - 1))
        if ci == nch - 1:
            ph.mm(bO[0][:, 0:NN], esel[0:1, 0:128], g['sink'][0], start=False, stop=True, reads=[esel, skb], bank=bO[0],
                  signal=False)
            ph.mm(bO[1][:, 512:512 + NN], esel[0:1, 128:256], g['sink'][1], start=False, stop=True, reads=[esel, skb],
                  bank=bO[1], signal=True)
            oe, rr = oes[gi % NR], rrs[gi % NR]
            ot = ots[ti % 2]
            ph.copy(oe[:, 0:NN], bO[0][:, 0:NN], reads=[], writes=[bO[0], oe], eng='scalar')
            ph.copy(oe[:, 512:512 + NN], bO[1][:, 512:512 + NN], reads=[], writes=[bO[1], oe], eng='scalar')
            ph.op('vector', lambda e, rr=rr, oe=oe, N_=NN: e.reciprocal(out=rr[0:64, 0:N_], in_=oe[64:128, 0:N_]),
                  reads=[oe], writes=[rr])
            ph.op('vector', lambda e, rr=rr, oe=oe, N_=NN: e.reciprocal(out=rr[64:128, 512:512 + N_], in_=oe[0:64, 512:512 + N_]),
                  reads=[oe], writes=[rr])

            def vw(ap, g=g, nq=nq):
                return ap if g['n3'] == 1 else v3(ap, nq)
            ph.tt(g['dst'][0], vw(oe[0:64, 0:NN]), vw(rr[0:64, 0:NN]), ALU.mult, reads=[oe, rr], writes=[ot], eng='gpsimd')
            ph.tt(g['dst'][1], vw(oe[64:128, 512:512 + NN]), vw(rr[64:128, 512:512 + NN]), ALU.mult, reads=[oe, rr], writes=[ot], eng='gpsimd')
            if g['last']:
                ov = v3(ot[:, :], 512)
                ph.dma(k.oD[5:8, :, off:off + N].rearrange("c p t -> p c t"), ov[:, :, 0:N], reads=[ot], eng='sync')
    for _, fn in deferred:
        fn()
    ph.finish()


CB_M1, CB_M2, CB_C128, CB_S128, CB_C256, CB_S256, CB_MASK = 0, 128, 256, 384, 512, 1024, 1536
CBW = 1536 + 256


def phase_E(k, l):
    nc = k.nc
    S = k.S
    L1 = S // 128
    P2 = 2 * L1
    ph = Phase(nc, "E%d" % l)
    banks = ph.psum_banks(8)
    cb = ph.sb("cb", [128, CBW], BF16)
    tw = ph.sb("tw", [128, 256], F32)
    ph.dma(cb[:, :], k.cbf[:, :], writes=[cb], eng='gpsimd')
    ph.dma(tw[:, :], k.cmat[:, 256:512], writes=[tw])
    SL = 32
    vs = ph.sb("v", [128, SL * 256], BF16, n=2)
    Zs = ph.sb("Zs", [128, SL * 256], BF16, n=2)
    t1s = ph.sb("t1", [128, 512], F32, n=2)
    t2s = ph.sb("t2", [128, 512], F32, n=2)
    Zt = ph.sb("Zt", [128, 2 * L1 * 256], BF16)
    ofs = ph.sb("of", [128, S], BF16, n=2)
    zdv = k.zD.rearrange("r (a b) j -> r a (b j)", b=128)
    ZDv = k.ZD.rearrange("r a b j -> (r a) (b j)")
    g = 0
    zdb = [Tile(None, "ZDslab%d" % i) for i in range(128 // SL)]
    for sl in range(128 // SL):
        v = vs[sl % 2]
        Zo = Zs[sl % 2]
        for r in range(2):
            ph.dma(v[r * L1:(r + 1) * L1, :], zdv[r, :, sl * SL * 256:(sl + 1) * SL * 256], writes=[v])
        for cg in range(SL // 2):
            cols = slice(cg * 512, (cg + 1) * 512)
            l2a = sl * SL + cg * 2
            bY, bW = banks[(g % 2) * 2], banks[(g % 2) * 2 + 1]
            t1, t2 = t1s[g % 2], t2s[g % 2]
            g += 1
            ph.mm(bY[0:P2, :], cb[0:P2, CB_M1:CB_M1 + P2], v[0:P2, cols], start=True, stop=True, reads=[cb, v], bank=bY)
            ph.mm(bW[0:P2, :], cb[0:P2, CB_M2:CB_M2 + P2], v[0:P2, cols], start=True, stop=True, reads=[cb, v], bank=bW)
            ph.tt(v3(t1[0:P2, :], 256), v3(bY[0:P2, :], 256), tw[0:P2, l2a:l2a + 2].unsqueeze(2).to_broadcast([P2, 2, 256]),
                  ALU.mult, reads=[tw], writes=[bY, t1])
            ph.tt(v3(t2[0:P2, :], 256), v3(bW[0:P2, :], 256),
                  tw[0:P2, 128 + l2a:128 + l2a + 2].unsqueeze(2).to_broadcast([P2, 2, 256]),
                  ALU.mult, reads=[tw], writes=[bW, t2])
            ph.tt(Zo[0:P2, cols], t1[0:P2, :], t2[0:P2, :], ALU.add, reads=[t1, t2], writes=[Zo], eng='gpsimd')
        ph.dma(ZDv[:, sl * SL * 256:(sl + 1) * SL * 256], Zo[0:P2, :], reads=[Zo], writes=[zdb[sl]], eng='gpsimd')
    Zv = Zt[:, :].rearrange("p (r a j) -> p r a j", r=2, a=L1)
    KG = min(16, L1)
    ztk = {}
    for a0 in range(0, L1, KG):
        for r in range(2):
            ztk[(r, a0)] = Tile(Zt.t, "zt%d_%d" % (r, a0))
            ph.dma(Zv[:, r, a0:a0 + KG, :], k.ZD[r, a0:a0 + KG, :, :].rearrange("a b j -> b a j"), reads=zdb,
                   writes=[ztk[(r, a0)]])
    scale = 1.0 / math.sqrt(S * 64.0)
    gi = 0
    for a0 in range(0, L1, 4):
        for jc in range(2):
            bk = banks[4 + gi % 4]
            for q in range(4):
                a = a0 + q
                ph.mm(bk[:, q * 128:(q + 1) * 128], Zv[:, 0, a, jc * 128:(jc + 1) * 128], cb[:, CB_C128:CB_C128 + 128],
                      start=True, stop=False, reads=[ztk[(0, (a // KG) * KG)], cb], bank=bk, signal=False)
                ph.mm(bk[:, q * 128:(q + 1) * 128], Zv[:, 1, a, jc * 128:(jc + 1) * 128], cb[:, CB_S128:CB_S128 + 128],
                      start=False, stop=True, reads=[ztk[(1, (a // KG) * KG)], cb], bank=bk, signal=(q == 3))
            of = ofs[jc]
            outv = of[:, :].rearrange("p (b a) -> p b a", a=L1)[:, :, a0:a0 + 4]
            inv = bk[:, :].rearrange("p (q b) -> p b q", q=4)
            if gi % 2 == 0:
                ph.act(outv, inv, AF.Copy, reads=[], writes=[bk, of], scale=scale)
            else:
                ph.op('vector', lambda e, outv=outv, inv=inv: e.tensor_scalar(out=outv, in0=inv, scalar1=scale, scalar2=None,
                                                                             op0=ALU.mult), reads=[], writes=[bk, of])
            gi += 1
    for jc in range(2):
        ph.dma(k.oD[3 + jc, :, CTX:CTX + S], ofs[jc][:, :], reads=[ofs[jc]], eng='gpsimd')
    if l == 0:
        zc = ph.sb("zc", [128, 2 * 2 * 256], BF16)
        zcv = zc[:, :].rearrange("p (c r j) -> p c r j", c=2, r=2)
        ofc = ph.sb("ofc", [128, 2 * 256], BF16)
        for c in range(2):
            ph.dma(zcv[:, c, :, :], k.zDc[:, c * 128:(c + 1) * 128, :].rearrange("r p j -> p r j"), writes=[zc])
        sc_c = 1.0 / math.sqrt(CTX * 64.0)
        for jc in range(2):
            bk = banks[jc]
            n = 0
            for c in range(2):
                for r in range(2):
                    base = (CB_C256 if r == 0 else CB_S256) + c * 256
                    ph.mm(bk[:, 0:256], zcv[:, c, r, jc * 128:(jc + 1) * 128], cb[:, base:base + 256],
                          start=(n == 0), stop=(n == 3), reads=[zc, cb], bank=bk, signal=(n == 3))
                    n += 1
            ph.act(ofc[:, jc * 256:(jc + 1) * 256], bk[:, 0:256], AF.Copy, reads=[], writes=[bk, ofc], scale=sc_c)
            ph.dma(k.oD[3 + jc, :, 0:CTX], ofc[:, jc * 256:(jc + 1) * 256], reads=[ofc], eng='gpsimd')
    ph.finish()


def phase_G(k, l):
    nc = k.nc
    S = k.S
    N = 256
    last = (l == DEPTH - 1)
    ph = Phase(nc, "G%d" % l)
    banks = ph.psum_banks(8)
    pp = ph.wrap(k.pp, "pp")
    sm = ph.wrap(k.sm, "sm")
    wo = ph.sb("wo", [128, 8 * D], BF16)
    wg = ph.sb("wg", [128, 8 * DFF], BF16)
    wu = ph.sb("wu", [128, 8 * DFF], BF16)
    wd = ph.sb("wd", [128, NFC * D], BF16)
    wov, wgv, wuv, wdv = v3(wo[:, :], D), v3(wg[:, :], DFF), v3(wu[:, :], DFF), v3(wd[:, :], D)
    xts = ph.sb("xt", [128, 8 * N], F32, n=2)
    ots = ph.sb("ot", [128, 8 * N], BF16, n=2)
    xn = ph.sb("xn", [128, 8 * N], F32)
    hh = ph.sb("hh", [128, 8 * N], BF16)
    sq = hh
    aa = ph.sb("aa", [128, NFC * N], BF16)
    sgs = ph.sb("sg", [128, N], F32, n=2)
    ones = ph.sb("ones", [128, 128], BF16)
    lnt = ph.sb("lnt", [128, N], F32)
    rstd = ph.sb("rstd", [128, N], F32)
    ph.memset(ones[:, :], 1.0, writes=[ones])
    ph.dma(wov, k.woB[l].rearrange("(c p) f -> p c f", p=128), writes=[wo])
    early_load = [True]
    NWG = 4
    WGC = DFF // NWG
    wgs = [Tile(wg.t, "wg%d" % i) for i in range(NWG)]
    wus = [Tile(wu.t, "wu%d" % i) for i in range(NWG)]

    def emit_big_weights():
        for gI in range(NWG):
            cs = slice(gI * WGC, (gI + 1) * WGC)
            ph.dma(wgv[:, :, cs], k.wgB[l, :, cs].rearrange("(c p) f -> p c f", p=128), writes=[wgs[gI]])
            ph.dma(wuv[:, :, cs], k.wuB[l, :, cs].rearrange("(c p) f -> p c f", p=128), writes=[wus[gI]])
        for c0 in range(0, NFC, 11):
            ph.dma(wdv[:, c0:c0 + 11, :], k.wdB[l, c0 * 128:(c0 + 11) * 128, :].rearrange("(c p) f -> p c f", p=128),
                   writes=[wd])

    tiles = [('x', t) for t in range(S // N)] + ([('ctx', 0)] if not last else [])
    lnf = ph.sb("lnf", [128, N], F32)
    rstdf = ph.sb("rstdf", [128, N], F32)
    hv = v3(hh[:, :], N)
    nv = v3(xn[:, :], N)
    av = v3(aa[:, :], N)
    sv = v3(sq[:, :], N)
    bi = [0]
    xnk = [Tile(xn.t, "xnk%d" % c) for c in range(8)]
    hhk = [Tile(hh.t, "hhk%d" % c) for c in range(8)]

    def info(ti):
        kind, t = tiles[ti]
        isx = kind == 'x'
        who = 0 if isx else 1
        xt, ot = xts[ti % 2], ots[ti % 2]
        return kind, t, isx, who, xt, ot, v3(xt[:, :], N), v3(ot[:, :], N)

    def load(ti):
        kind, t, isx, who, xt, ot, xv, ov = info(ti)
        off = CTX + t * N if isx else 0
        if isx:
            src = (k.xT if l == 0 else k.xD1)[:, t * N:(t + 1) * N]
        else:
            src = k.ctxT[:, :]
        ph.dma(xv, src.rearrange("(c p) t -> p c t", p=128), writes=[xt])
        ph.dma(ov, k.oD[:, :, off:off + N].rearrange("c p t -> p c t"), writes=[ot])

    def st_A(ti):
        kind, t, isx, who, xt, ot, xv, ov = info(ti)
        gt1 = MOD(l, who, 2)
        for dc in range(8):
            bk = banks[1 + bi[0] % 2]
            bi[0] += 1
            for mc in range(8):
                ph.mm(bk[:, 0:N], wov[:, mc, dc * 128:(dc + 1) * 128], ov[:, mc, :], start=(mc == 0), stop=(mc == 7),
                      reads=[wo, ot], bank=bk, signal=(mc == 7))
            ph.stt(xv[:, dc, :], bk[:, 0:N], pp[:, gt1 + dc:gt1 + dc + 1], xv[:, dc, :], ALU.mult, ALU.add,
                   reads=[pp], writes=[bk, xt])

    def st_Bn(ti):
        kind, t, isx, who, xt, ot, xv, ov = info(ti)
        ph.act(sv, xv, AF.Square, reads=[xt], writes=hhk)

    def st_Bs(ti):
        bS = banks[0]
        for kc in range(8):
            ph.mm(bS[:, 0:N], ones[:, :], sv[:, kc, :], start=(kc == 0), stop=(kc == 7), reads=[ones, hhk[kc]], bank=bS,
                  signal=(kc == 7))

    def st_Bc(ti):
        kind, t, isx, who, xt, ot, xv, ov = info(ti)
        sh2, gs2 = MOD(l, who, 3), GS(l, who, 1)
        bS = banks[0]
        rstd_from_sums(ph, bS[:, 0:N], D, lnt[:, :], rstd[:, :], bS, lnt, rstd)
        for kc in range(8):
            ph.tt(nv[:, kc, :], xv[:, kc, :], rstd[:, :], ALU.mult, reads=[xt, rstd], writes=[xnk[kc]])
            if kc % 3 == 2:
                ph.act(hv[:, kc, :], nv[:, kc, :], AF.Identity, reads=[xnk[kc], pp], writes=[hhk[kc]],
                       scale=pp[:, gs2 + kc:gs2 + kc + 1], bias=pp[:, sh2 + kc:sh2 + kc + 1])
            else:
                ph.ts(hv[:, kc, :], nv[:, kc, :], pp[:, gs2 + kc:gs2 + kc + 1], pp[:, sh2 + kc:sh2 + kc + 1], ALU.mult,
                      ALU.add, reads=[xnk[kc], pp], writes=[hhk[kc]], eng='gpsimd')

    def st_C(ti, fcs=range(NFC)):
        for fc in fcs:
            bG, bU = banks[3 + (fc % 2) * 2], banks[4 + (fc % 2) * 2]
            sg = sgs[fc % 2]
            for kc in range(8):
                ph.mm(bG[:, 0:N], wgv[:, kc, fc * 128:(fc + 1) * 128], hv[:, kc, :], start=(kc == 0), stop=(kc == 7),
                      reads=[wgs[(fc * 128) // WGC], wgs[(fc * 128 + 127) // WGC], hhk[kc]], bank=bG, signal=(kc == 7))
            for kc in range(8):
                ph.mm(bU[:, 0:N], wuv[:, kc, fc * 128:(fc + 1) * 128], hv[:, kc, :], start=(kc == 0), stop=(kc == 7),
                      reads=[wus[(fc * 128) // WGC], wus[(fc * 128 + 127) // WGC], hhk[kc]], bank=bU, signal=(kc == 7))
            ph.act(sg[:, :], bG[:, 0:N], AF.Silu, reads=[], writes=[bG, sg])
            ph.tt(av[:, fc, :], sg[:, :], bU[:, 0:N], ALU.mult, reads=[sg], writes=[bU, aa])

    def st_D(ti, dcs):
        kind, t, isx, who, xt, ot, xv, ov = info(ti)
        gt2 = MOD(l, who, 5)
        for dc in dcs:
            bk = banks[1 + bi[0] % 2]
            bi[0] += 1
            for fc in range(NFC):
                ph.mm(bk[:, 0:N], wdv[:, fc, dc * 128:(dc + 1) * 128], av[:, fc, :], start=(fc == 0), stop=(fc == NFC - 1),
                      reads=[wd, aa], bank=bk, signal=(fc == NFC - 1))
            ph.stt(xv[:, dc, :], bk[:, 0:N], pp[:, gt2 + dc:gt2 + dc + 1], xv[:, dc, :], ALU.mult, ALU.add,
                   reads=[pp], writes=[bk, xt])

    def st_out(ti, part):
        kind, t, isx, who, xt, ot, xv, ov = info(ti)
        if last:
            sv2 = v3(aa[:, (NFC - 8) * N:NFC * N], N)
            if part == 0:
                ph.act(sv2, xv, AF.Square, reads=[xt], writes=[aa])
                return
            bS = banks[7]
            for kc in range(8):
                ph.mm(bS[:, 0:N], ones[:, :], sv2[:, kc, :], start=(kc == 0), stop=(kc == 7), reads=[ones, aa], bank=bS,
                      signal=(kc == 7))
            rstd_from_sums(ph, bS[:, 0:N], D, lnf[:, :], rstdf[:, :], bS, lnf, rstdf)
            for kc in range(8):
                ph.stt(nv[:, kc, :], xv[:, kc, :], sm[:, SM_GFIN + kc:SM_GFIN + kc + 1], rstdf[:, :], ALU.mult, ALU.mult,
                       reads=[xt, sm, rstdf], writes=[xnk[kc]])
            ph.dma(k.yT[:, t * N:(t + 1) * N].rearrange("(c p) t -> p c t", p=128), nv, reads=xnk, eng='sync')
        else:
            if part == 0:
                return
            dst = k.xD1[:, t * N:(t + 1) * N] if isx else k.ctxD1[:, :]
            ph.dma(dst.rearrange("(c p) t -> p c t", p=128), xv, reads=[xt], eng='sync')

    NTL = len(tiles)
    load(0)
    if NTL > 1:
        load(1)
    emit_big_weights()
    st_A(0)
    st_Bn(0)
    st_Bs(0)
    st_Bc(0)
    for ti in range(NTL):
        st_C(ti, range(0, 2))
        if ti >= 1:
            st_out(ti - 1, 1)
            if ti + 1 < NTL:
                load(ti + 1)
        st_C(ti, range(2, NFC))
        if ti + 1 < NTL:
            st_A(ti + 1)
            st_Bn(ti + 1)
            st_D(ti, range(0, 2))
            st_Bs(ti + 1)
            st_Bc(ti + 1)
            st_D(ti, range(2, 8))
        else:
            st_D(ti, range(0, 8))
        st_out(ti, 0)
    st_out(NTL - 1, 1)
    ph.finish()


def build_program(S):
    nc = bass.Bass("TRN2", target_bir_lowering=False)
    k = K()
    k.nc = nc
    k.S = S
    L1 = S // 128

    def din(name, shape, dt=F32):
        return nc.dram_tensor(name, list(shape), dt, kind="ExternalInput").ap()

    def dscr(name, shape, dt):
        return nc.dram_tensor(name, list(shape), dt).ap()

    k.xT = din("xT", [D, S])
    k.ctxT = din("ctxT", [D, CTX])
    k.cvec = din("cvec", [128, 16])
    k.small = din("small", [128, SMW])
    k.w_ada = din("w_ada", [DEPTH, D, 6 * D])
    k.w_in_p = din("w_in_p", [DEPTH, D, 2304])
    k.w_in_fT = din("w_in_fT", [DEPTH, 256, D])
    k.w_out_p = din("w_out_p", [DEPTH, D, D])
    k.w_gate = din("w_gate", [DEPTH, D, DFF])
    k.w_up = din("w_up", [DEPTH, D, DFF])
    k.w_down = din("w_down", [DEPTH, DFF, D])
    k.sinkrow = din("sinkrow", [DEPTH, 1, 6 * 256])
    k.ropeC = din("ropeC", [128, S])
    k.ropeS = din("ropeS", [128, S])
    k.cmat = din("cmat", [128, 512])
    k.cbf = din("cbf", [128, CBW])
    k.yT = nc.dram_tensor("yT", [D, S], F32, kind="ExternalOutput").ap()

    k.wfD = dscr("wfD", [DEPTH, D, 512], BF16)
    k.qaD = dscr("qaD", [3, 128, CTX + S], BF16)
    k.qcD = dscr("qcD", [3, 128, CTX + S], BF16)
    k.kaD = dscr("kaD", [128, CTX + S], BF16)
    k.kcD = dscr("kcD", [128, CTX + S], BF16)
    k.vD = dscr("vD", [CTX + S, VW], BF16)
    k.zD = dscr("zD", [2, S, 256], BF16)
    k.zDc = dscr("zDc", [2, CTX, 256], BF16)
    k.ZD = dscr("ZD", [2, L1, 128, 256], BF16)
    k.oD = dscr("oD", [8, 128, CTX + S], BF16)
    k.woB = dscr("woB", [DEPTH, D, D], BF16)
    k.wgB = dscr("wgB", [DEPTH, D, DFF], BF16)
    k.wuB = dscr("wuB", [DEPTH, D, DFF], BF16)
    k.wdB = dscr("wdB", [DEPTH, DFF, D], BF16)
    k.winB = dscr("winB", [DEPTH, D, 2304], BF16)
    k.xD1 = dscr("xD1", [D, S], F32)
    k.ctxD1 = dscr("ctxD1", [D, CTX], F32)

    with contextlib.ExitStack() as st:
        Phase.state = SemState(nc, st)
        k.pp = st.enter_context(nc.sbuf_tensor("pp_persist", [128, PPW], F32))
        k.sm = st.enter_context(nc.sbuf_tensor("sm_persist", [128, SMW], F32))
        import os
        sel = os.environ.get("KPHASES", "")
        for l in range(DEPTH):
            for nm, f in (("A", phase_A), ("B", phase_B), ("C", phase_C), ("D", phase_D), ("E", phase_E), ("G", phase_G)):
                if sel and ("%s%d" % (nm, l)) not in sel.split(","):
                    continue
                f(k, l)
    return nc


def host_constants(S):
    L1 = S // 128
    f32 = np.float32
    tok = np.arange(S)
    row = (tok // 64).astype(f32)
    col = (tok % 64).astype(f32)
    inv_freq = (np.float32(10000.0) ** (-np.arange(0, 32, 2, dtype=f32) / np.float32(32.0))).astype(f32)
    ang = np.concatenate([row[:, None] * inv_freq, col[:, None] * inv_freq], axis=-1).astype(f32)
    cos = np.cos(ang).astype(f32).T
    sin = np.sin(ang).astype(f32).T
    ropeC = np.concatenate([cos, cos, cos, cos], axis=0)
    ropeS = np.concatenate([-sin, sin, -sin, sin], axis=0)
    c = np.arange(64)
    a64 = 2 * np.pi * np.outer(c, c) / 64.0
    C64, S64 = np.cos(a64), np.sin(a64)
    Z = np.zeros((64, 64))
    cmat = np.zeros((128, 512), f32)
    cmat[:, 0:128] = np.block([[C64, Z], [Z, C64]])
    cmat[:, 128:256] = np.block([[-S64, Z], [Z, -S64]])
    k1 = np.arange(L1)
    l2 = np.arange(128)
    at = 2 * np.pi * np.outer(k1, l2) / S
    cmat[0:2 * L1, 256:384] = np.concatenate([np.cos(at), np.cos(at)], axis=0)
    cmat[0:2 * L1, 384:512] = np.concatenate([np.sin(at), np.sin(at)], axis=0)
    cbf = np.zeros((128, CBW), f32)
    a1 = 2 * np.pi * np.outer(k1, k1) / L1
    Cc, Sc = np.cos(a1), np.sin(a1)
    M1 = np.block([[Cc, Sc], [-Sc, Cc]])
    M2 = np.block([[-Sc, Cc], [-Cc, -Sc]])
    cbf[0:2 * L1, CB_M1:CB_M1 + 2 * L1] = M1.T
    cbf[0:2 * L1, CB_M2:CB_M2 + 2 * L1] = M2.T
    a128 = 2 * np.pi * np.outer(l2, l2) / 128.0
    cbf[:, CB_C128:CB_C128 + 128] = np.cos(a128)
    cbf[:, CB_S128:CB_S128 + 128] = np.sin(a128)
    n256 = np.arange(256)
    a256 = 2 * np.pi * np.outer(n256, n256) / 256.0
    C256, S256 = np.cos(a256), np.sin(a256)
    for cch in range(2):
        cbf[:, CB_C256 + cch * 256:CB_C256 + (cch + 1) * 256] = C256[cch * 128:(cch + 1) * 128, :]
        cbf[:, CB_S256 + cch * 256:CB_S256 + (cch + 1) * 256] = S256[cch * 128:(cch + 1) * 128, :]
    a = np.arange(128)[:, None]
    i = np.arange(128)[None, :]
    cbf[:, CB_MASK:CB_MASK + 128] = (a >= i)
    cbf[:, CB_MASK + 128:CB_MASK + 256] = (a <= i)
    return ropeC.astype(f32), ropeS.astype(f32), cmat, cbf.astype(f32)


def host_layout(inp, S):
    f32 = np.float32
    g = lambda n: np.asarray(inp[n], dtype=f32)
    w_in = g('w_in')

    def pair(base, j):
        return np.concatenate([np.arange(base + 64 * j, base + 64 * j + 64), np.arange(base + 64 * (3 + j), base + 64 * (3 + j) + 64)])

    def swap(idx):
        idx = idx.reshape(-1, 64)
        return np.concatenate([idx[:, 32:], idx[:, :32]], axis=1).reshape(-1)

    cols = []
    qa = [pair(0, j) for j in range(3)]
    ka = np.arange(384, 512)
    qc = [pair(896, j) for j in range(3)]
    kc = np.arange(1280, 1408)
    cols += qa + [swap(q) for q in qa] + [ka, swap(ka)] + qc + [swap(q) for q in qc] + [kc, swap(kc)]
    cols += [np.arange(512, 640), np.arange(1408, 1536)]
    cols = np.concatenate(cols)
    assert cols.shape[0] == 2304
    w_in_p = np.ascontiguousarray(w_in[:, :, cols])
    w_in_fT = np.ascontiguousarray(np.transpose(w_in[:, :, 640:896], (0, 2, 1)))
    rows = np.concatenate([pair(0, j) for j in range(3)] + [np.arange(384, 640)] + [pair(640, j) for j in range(3)])
    w_out_p = np.ascontiguousarray(g('w_out')[:, rows, :])

    def pl(v):
        return np.ascontiguousarray(v.reshape(8, 128).T)

    small = np.zeros((128, SMW), f32)
    b_ada, g_mix, g_ffn = g('b_ada'), g('g_mix'), g('g_ffn')
    qn, kn = g('q_norm'), g('k_norm')
    p = np.arange(128) % 64
    ps = (p + 32) % 64
    for l in range(DEPTH):
        small[:, SM_BADA(l):SM_BADA(l) + 48] = b_ada[l].reshape(48, 128).T
        small[:, SM_GMIX(l):SM_GMIX(l) + 8] = pl(g_mix[l])
        small[:, SM_GFFN(l):SM_GFFN(l) + 8] = pl(g_ffn[l])
        small[:, SM_QKN(l) + 0] = qn[l][p]
        small[:, SM_QKN(l) + 1] = qn[l][ps]
        small[:, SM_QKN(l) + 2] = kn[l][p]
        small[:, SM_QKN(l) + 3] = kn[l][ps]
    small[:, SM_GFIN:SM_GFIN + 8] = pl(g('g_final'))
    sinkrow = np.ascontiguousarray(np.repeat(g('sink')[:, None, :, None], 256, axis=3).reshape(DEPTH, 1, 6 * 256))
    ropeC, ropeS, cmat, cbf = host_constants(S)
    shared = dict(small=small, w_ada=g('w_ada'), w_in_p=w_in_p, w_in_fT=w_in_fT, w_out_p=w_out_p,
                  w_gate=g('w_gate'), w_up=g('w_up'), w_down=g('w_down'), sinkrow=sinkrow,
                  ropeC=ropeC, ropeS=ropeS, cmat=cmat, cbf=cbf)
    x, c, ctx, c_ctx = g('x'), g('c'), g('ctx'), g('c_ctx')
    B = x.shape[0]
    maps = []
    for b in range(B):
        cvec = np.zeros((128, 16), f32)
        cvec[:, 0::2] = pl(c[b])
        cvec[:, 1::2] = pl(c_ctx)
        m = dict(shared)
        m['xT'] = np.ascontiguousarray(x[b].T)
        m['ctxT'] = np.ascontiguousarray(ctx[b].T)
        m['cvec'] = cvec
        maps.append(m)
    return maps


_CACHE = {}


def run(inp, S):
    maps = host_layout(inp, S)
    if S not in _CACHE:
        _CACHE[S] = build_program(S)
    nc = _CACHE[S]
    res = run_bass_kernel_spmd(nc, maps, core_ids=list(range(len(maps))))
    out = np.stack([np.ascontiguousarray(r["yT"].T) for r in res.results], axis=0)
    return out.astype(np.float32)


def kernel(**inputs):
    return run(inputs, 8192)
```
